# Optimizing a Trainium2 kernel written in Bass

```python
import math
import jax, jax.numpy as jnp
from jax import lax
import numpy as np

D_MODEL = 1024
BATCH = 4
SEQ = 4096
DEPTH = 2
DEC_BATCH = 32
DEC_SEQ = 64
PAST_LEN = 4096

CHUNK = 64
Q_BLOCK = 128
N_AB_LAYERS = (DEPTH + 1) // 2
N_SB_LAYERS = DEPTH // 2
N_SUB = 3
DN_HEADS = 8
DN_DK = 64
DN_DV = 64
DN_WIDTH = DN_HEADS * DN_DK
DN_CONV = 4
SC_WIDTH = D_MODEL // 2
SC_CONV = 3
AB_PROJ = 3 * DN_WIDTH + DN_HEADS * DN_DV + 2 * DN_HEADS + 3 * SC_WIDTH
AB_OUT = DN_HEADS * DN_DV + SC_WIDTH
SB_HEADS = 16
SB_DH = D_MODEL // SB_HEADS
SB_WIDTH = SB_HEADS * SB_DH
D_FF = 2816
NORM_EPS = 1e-6

kernel_name = 'hybrid_gdn_shortconv_stickbreak_stream_step'


def _rmsnorm(x, g):
    xf = x.astype(jnp.float32)
    y = xf * lax.rsqrt(jnp.mean(xf * xf, axis=-1, keepdims=True) + NORM_EPS)
    return y * g.astype(jnp.float32)


def _modulated_norm(x, g, shift, scale):
    y = _rmsnorm(x, g) * (1.0 + scale[:, None].astype(jnp.float32)) + shift[:, None].astype(jnp.float32)
    return y.astype(x.dtype)


def _swiglu(h, w_in, w_out):
    gate, up = jnp.split(h @ w_in, 2, axis=-1)
    return (jax.nn.silu(gate) * up) @ w_out


def _l2norm(x):
    return x * lax.rsqrt(jnp.sum(x * x, axis=-1, keepdims=True) + 1e-6)


def _causal_dwconv(x, prev, w):
    width = w.shape[0]
    t = x.shape[1]
    xp = jnp.concatenate([prev.astype(x.dtype), x], axis=1)
    y = xp[:, 0:t] * w[0]
    for j in range(1, width):
        y = y + xp[:, j:j + t] * w[j]
    return y, xp[:, t:]


def _gdn_chunk(S, xs):
    q, k, v, g, beta = xs
    c = q.shape[2]
    incl = jnp.tril(jnp.ones((c, c), dtype=bool))
    strict = jnp.tril(jnp.ones((c, c), dtype=bool), k=-1)
    G = jnp.cumsum(g, axis=-1)
    decay = jnp.exp(jnp.where(incl, G[..., :, None] - G[..., None, :], -jnp.inf))
    a_mat = jnp.where(strict, beta[..., :, None] * jnp.einsum('bhtd,bhsd->bhts', k, k) * decay, 0.0)
    eye = jnp.eye(c, dtype=a_mat.dtype)
    rhs = jnp.concatenate([beta[..., None] * v, (beta * jnp.exp(G))[..., None] * k], axis=-1)
    sol = lax.linalg.triangular_solve(eye + a_mat, rhs, left_side=True, lower=True)
    dv = v.shape[-1]
    u = sol[..., :dv] - jnp.einsum('bhtk,bhkv->bhtv', sol[..., dv:], S)
    qk = jnp.einsum('bhtd,bhsd->bhts', q, k) * decay
    o = jnp.exp(G)[..., None] * jnp.einsum('bhtk,bhkv->bhtv', q, S) + jnp.einsum('bhts,bhsv->bhtv', qk, u)
    G_end = G[..., -1:]
    S_new = jnp.exp(G_end)[..., None] * S + jnp.einsum('bhtk,bhtv->bhkv', k * jnp.exp(G_end - G)[..., None], u)
    return S_new, o


def _gated_delta(q, k, v, g, beta, S0):
    b, h, t, _ = q.shape
    c = min(t, CHUNK)
    n = t // c

    def to_chunks(a):
        a = a.reshape((b, h, n, c) + a.shape[3:])
        return jnp.moveaxis(a, 2, 0)

    S, o = lax.scan(_gdn_chunk, S0, (to_chunks(q), to_chunks(k), to_chunks(v), to_chunks(g), to_chunks(beta)))
    o = jnp.moveaxis(o, 0, 2).reshape(b, h, t, -1)
    return o, S


def _ab_mixer(h, S0, conv_prev, sc_prev, w_in, w_conv, A_log, dt_bias, dn_g, w_sc, w_out):
    b, t, _ = h.shape
    f32 = jnp.float32
    splits = np.cumsum([3 * DN_WIDTH, DN_HEADS * DN_DV, DN_HEADS, DN_HEADS, SC_WIDTH, SC_WIDTH]).tolist()
    qkv, z, a, bt, sB, sC, sx = jnp.split(h @ w_in, splits, axis=-1)
    qkv, conv_state = _causal_dwconv(qkv, conv_prev, w_conv)
    qkv = jax.nn.silu(qkv).astype(f32)
    q, k, v = jnp.split(qkv, 3, axis=-1)

    def heads(x):
        return x.reshape(b, t, DN_HEADS, -1).transpose(0, 2, 1, 3)

    q = _l2norm(heads(q)) * DN_DK ** -0.5
    k = _l2norm(heads(k))
    v = heads(v)
    g = -jnp.exp(A_log.astype(f32)) * jax.nn.softplus(a.astype(f32) + dt_bias.astype(f32))
    beta = jax.nn.sigmoid(bt.astype(f32))
    o, S = _gated_delta(q, k, v, g.transpose(0, 2, 1), beta.transpose(0, 2, 1), S0.astype(f32))
    o = o.transpose(0, 2, 1, 3)
    o = _rmsnorm(o, dn_g) * jax.nn.silu(z.astype(f32).reshape(b, t, DN_HEADS, DN_DV))
    o = o.reshape(b, t, DN_HEADS * DN_DV).astype(h.dtype)
    yc, sc_state = _causal_dwconv(sC * sx, sc_prev, w_sc)
    y_sc = sB * yc
    y = jnp.concatenate([o, y_sc], axis=-1) @ w_out
    return y, S.astype(S0.dtype), conv_state, sc_state


def _sb_block(q, k, v, q_pos, k_pos):
    z = jnp.einsum('hqd,hkd->hqk', q.astype(jnp.float32), k.astype(jnp.float32)) * SB_DH ** -0.5
    visible = k_pos[None, :] < q_pos[:, None]
    log_rest = jnp.where(visible, -jax.nn.softplus(z), 0.0)
    later = lax.cumsum(log_rest, axis=2, reverse=True) - log_rest
    w = jnp.where(visible, jnp.exp(jax.nn.log_sigmoid(z) + later), 0.0)
    return jnp.einsum('hqk,hkd->hqd', w, v.astype(jnp.float32))


def _sb_mixer(h, ck, cv, w_qkv, w_out):
    b, t, _ = h.shape
    past = ck.shape[2]
    q, k, v = jnp.split(h @ w_qkv, 3, axis=-1)

    def heads(x):
        return x.reshape(b, t, SB_HEADS, SB_DH).transpose(0, 2, 1, 3)

    q, k, v = heads(q), heads(k), heads(v)
    k_all = jnp.concatenate([ck.astype(k.dtype), k], axis=2)
    v_all = jnp.concatenate([cv.astype(v.dtype), v], axis=2)
    outs = []
    for q0 in range(0, t, Q_BLOCK):
        q1 = min(q0 + Q_BLOCK, t)
        q_pos = past + jnp.arange(q0, q1)
        k_pos = jnp.arange(past + q1)
        ob = lax.map(lambda a: _sb_block(a[0], a[1], a[2], q_pos, k_pos),
                     (q[:, :, q0:q1], k_all[:, :, :past + q1], v_all[:, :, :past + q1]))
        outs.append(ob)
    o = jnp.concatenate(outs, axis=2).astype(h.dtype)
    y = o.transpose(0, 2, 1, 3).reshape(b, t, SB_WIDTH) @ w_out
    return y, k, v


def _trunk(x, c, s_delta, s_qkv, s_sc, ck, cv, norm_g, ada_w, ada_b, ff_w_in, ff_w_out,
           ab_w_in, ab_conv_qkv, dn_A_log, dn_dt_bias, dn_norm_g, sc_conv, ab_w_out,
           sb_w_qkv, sb_w_out, final_g):
    b = x.shape[0]
    new_delta, new_qkv, new_sc, new_k, new_v = [], [], [], [], []
    cond = jax.nn.silu(c)
    for l in range(DEPTH):
        mod = (cond @ ada_w[l] + ada_b[l]).reshape(b, N_SUB, 3, D_MODEL)
        shift, scale, gate = mod[:, :, 0], mod[:, :, 1], mod[:, :, 2]
        hh = _modulated_norm(x, norm_g[l, 0], shift[:, 0], scale[:, 0])
        x = x + 0.5 * gate[:, 0, None] * _swiglu(hh, ff_w_in[l, 0], ff_w_out[l, 0])
        hh = _modulated_norm(x, norm_g[l, 1], shift[:, 1], scale[:, 1])
        i = l // 2
        if l % 2 == 0:
            y, S, cq, cs = _ab_mixer(hh, s_delta[i], s_qkv[i], s_sc[i], ab_w_in[i], ab_conv_qkv[i],
                                     dn_A_log[i], dn_dt_bias[i], dn_norm_g[i], sc_conv[i], ab_w_out[i])
            new_delta.append(S)
            new_qkv.append(cq)
            new_sc.append(cs)
        else:
            y, kr, vr = _sb_mixer(hh, ck[i], cv[i], sb_w_qkv[i], sb_w_out[i])
            new_k.append(kr)
            new_v.append(vr)
        x = x + gate[:, 1, None] * y
        hh = _modulated_norm(x, norm_g[l, 2], shift[:, 2], scale[:, 2])
        x = x + 0.5 * gate[:, 2, None] * _swiglu(hh, ff_w_in[l, 1], ff_w_out[l, 1])
    y = _rmsnorm(x, final_g).astype(x.dtype)
    return y, jnp.stack(new_delta), jnp.stack(new_qkv), jnp.stack(new_sc), jnp.stack(new_k), jnp.stack(new_v)


def setup_inputs(seed: int = 0) -> dict:
    key = jax.random.key(seed)
    ks = jax.random.split(key, 24)
    f32 = jnp.float32

    def nrm(k, shape, s=1.0):
        return s * jax.random.normal(k, shape, f32)

    x_prompt = nrm(ks[0], (BATCH, SEQ, D_MODEL))
    x_sample = nrm(ks[1], (DEC_BATCH, DEC_SEQ, D_MODEL))
    c_prompt = nrm(ks[2], (BATCH, D_MODEL))
    c_sample = nrm(ks[3], (DEC_BATCH, D_MODEL))
    state_delta = nrm(ks[4], (N_AB_LAYERS, DEC_BATCH, DN_HEADS, DN_DK, DN_DV), 0.5)
    state_qkv_conv = nrm(ks[5], (N_AB_LAYERS, DEC_BATCH, DN_CONV - 1, 3 * DN_WIDTH))
    state_sconv = nrm(ks[6], (N_AB_LAYERS, DEC_BATCH, SC_CONV - 1, SC_WIDTH))
    cache_k = nrm(ks[7], (N_SB_LAYERS, DEC_BATCH, SB_HEADS, PAST_LEN, SB_DH))
    cache_v = nrm(ks[8], (N_SB_LAYERS, DEC_BATCH, SB_HEADS, PAST_LEN, SB_DH))
    norm_g = 1.0 + nrm(ks[9], (DEPTH, N_SUB, D_MODEL), 0.02)
    ada_w = nrm(ks[10], (DEPTH, D_MODEL, N_SUB * 3 * D_MODEL), 0.25 * D_MODEL ** -0.5)
    ada_b = nrm(ks[11], (DEPTH, N_SUB, 3, D_MODEL), 0.02).at[:, :, 2].add(1.0).reshape(DEPTH, N_SUB * 3 * D_MODEL)
    ff_w_in = nrm(ks[12], (DEPTH, 2, D_MODEL, 2 * D_FF), D_MODEL ** -0.5)
    ff_w_out = nrm(ks[13], (DEPTH, 2, D_FF, D_MODEL), D_FF ** -0.5)
    ab_w_in = nrm(ks[14], (N_AB_LAYERS, D_MODEL, AB_PROJ), D_MODEL ** -0.5)
    ab_conv_qkv = nrm(ks[15], (N_AB_LAYERS, DN_CONV, 3 * DN_WIDTH), DN_CONV ** -0.5)
    dn_A_log = jnp.log(jax.random.uniform(ks[16], (N_AB_LAYERS, DN_HEADS), f32, 1.0, 16.0))
    dt = jnp.exp(jax.random.uniform(ks[17], (N_AB_LAYERS, DN_HEADS), f32, math.log(1e-3), math.log(1e-1)))
    dn_dt_bias = dt + jnp.log(-jnp.expm1(-dt))
    dn_norm_g = 1.0 + nrm(ks[18], (N_AB_LAYERS, DN_DV), 0.02)
    sc_conv = nrm(ks[19], (N_AB_LAYERS, SC_CONV, SC_WIDTH), SC_CONV ** -0.5)
    ab_w_out = nrm(ks[20], (N_AB_LAYERS, AB_OUT, D_MODEL), AB_OUT ** -0.5)
    sb_w_qkv = nrm(ks[21], (N_SB_LAYERS, D_MODEL, 3 * SB_WIDTH), D_MODEL ** -0.5)
    sb_w_out = nrm(ks[22], (N_SB_LAYERS, SB_WIDTH, D_MODEL), SB_WIDTH ** -0.5)
    final_g = 1.0 + nrm(ks[23], (D_MODEL,), 0.02)
    return {'x_prompt': x_prompt, 'x_sample': x_sample, 'c_prompt': c_prompt, 'c_sample': c_sample,
            'state_delta': state_delta, 'state_qkv_conv': state_qkv_conv, 'state_sconv': state_sconv,
            'cache_k': cache_k, 'cache_v': cache_v, 'norm_g': norm_g, 'ada_w': ada_w, 'ada_b': ada_b,
            'ff_w_in': ff_w_in, 'ff_w_out': ff_w_out, 'ab_w_in': ab_w_in, 'ab_conv_qkv': ab_conv_qkv,
            'dn_A_log': dn_A_log, 'dn_dt_bias': dn_dt_bias, 'dn_norm_g': dn_norm_g, 'sc_conv': sc_conv,
            'ab_w_out': ab_w_out, 'sb_w_qkv': sb_w_qkv, 'sb_w_out': sb_w_out, 'final_g': final_g}


def reference(x_prompt, x_sample, c_prompt, c_sample, state_delta, state_qkv_conv, state_sconv,
              cache_k, cache_v, norm_g, ada_w, ada_b, ff_w_in, ff_w_out, ab_w_in, ab_conv_qkv,
              dn_A_log, dn_dt_bias, dn_norm_g, sc_conv, ab_w_out, sb_w_qkv, sb_w_out, final_g):
    bp = x_prompt.shape[0]
    dt_ = x_prompt.dtype
    p_delta0 = jnp.zeros((N_AB_LAYERS, bp, DN_HEADS, DN_DK, DN_DV), dt_)
    p_qkv0 = jnp.zeros((N_AB_LAYERS, bp, DN_CONV - 1, 3 * DN_WIDTH), dt_)
    p_sc0 = jnp.zeros((N_AB_LAYERS, bp, SC_CONV - 1, SC_WIDTH), dt_)
    p_kv0 = jnp.zeros((N_SB_LAYERS, bp, SB_HEADS, 0, SB_DH), dt_)
    y_prompt, p_delta, p_qkv, p_sc, p_k, p_v = _trunk(
        x_prompt, c_prompt, p_delta0, p_qkv0, p_sc0, p_kv0, p_kv0, norm_g, ada_w, ada_b, ff_w_in, ff_w_out,
        ab_w_in, ab_conv_qkv, dn_A_log, dn_dt_bias, dn_norm_g, sc_conv, ab_w_out, sb_w_qkv, sb_w_out, final_g)
    y_sample, s_delta, s_qkv, s_sc, s_k, s_v = _trunk(
        x_sample, c_sample, state_delta, state_qkv_conv, state_sconv, cache_k, cache_v, norm_g, ada_w, ada_b,
        ff_w_in, ff_w_out, ab_w_in, ab_conv_qkv, dn_A_log, dn_dt_bias, dn_norm_g, sc_conv, ab_w_out,
        sb_w_qkv, sb_w_out, final_g)
    return (y_prompt, y_sample, p_delta, p_qkv, p_sc, p_k, p_v, s_delta, s_qkv, s_sc, s_k, s_v)
```

```python
import numpy as np
from contextlib import ExitStack
import concourse.bass as bass
import concourse.mybir as mybir
from concourse.bass_utils import run_bass_kernel_spmd

F32 = mybir.dt.float32
BF16 = mybir.dt.bfloat16
AF = mybir.ActivationFunctionType
ALU = mybir.AluOpType

D = 1024
FC = 8
DFF = 2816
HC = 22
NS = 4
DS = 64
NSEQ = 1 + NS
TT = 512
EPS = 1e-6
SAME_SYNC = True
ND = 24


class V:
    __slots__ = ("buf", "ap")

    def __init__(self, buf, ap):
        self.buf = buf
        self.ap = ap


class Buf:
    def __init__(self, t, name="", psum=False):
        self.t = t
        self.name = name
        self.w = None
        self.r = {}
        self.psum = psum
        self.arena = False

    def __getitem__(self, idx):
        return V(self, self.t[idx])

    def v(self, ap):
        return V(self, ap)


class Prog:
    ENG = ["pe", "act", "dve", "pool", "sp"]

    def __init__(self, nc, stack):
        self.nc = nc
        self.stack = stack
        self.ops = {e: [] for e in self.ENG}
        self.sems = []
        for e in self.ENG:
            self.sems.append(stack.enter_context(nc.semaphore("s_" + e)))
        for k in range(ND):
            self.sems.append(stack.enter_context(nc.semaphore("d%d" % k)))
        self.eidx = {e: i for i, e in enumerate(self.ENG)}
        self.cnt = {e: 0 for e in self.ENG}
        self.waited = {e: {} for e in self.ENG}
        self.dma_cum = [0] * ND
        self.dma_rr = 0
        self.dma_rr2 = 0
        self.bar = None
        self.nins = 0

    def _wait(self, e, s, v):
        if v <= 0 or self.waited[e].get(s, 0) >= v:
            return
        self.waited[e][s] = v
        sem = self.sems[s]
        self.ops[e].append(lambda eng, sem=sem, v=v: eng.wait_ge(sem, v))
        self.nins += 1

    def _deps(self, e, reads, writes, own_always=False):
        deps = {}

        def add(s, v):
            if v > deps.get(s, 0):
                deps[s] = v
        for b in reads:
            if b.w:
                add(*b.w)
            if b.psum:
                for s, v in b.r.items():
                    add(s, v)
        for b in writes:
            if b.w:
                add(*b.w)
            for s, v in b.r.items():
                add(s, v)
        own = self.eidx[e]
        for s, v in deps.items():
            if s == own and not own_always and (e == "pe" or not SAME_SYNC):
                continue
            self._wait(e, s, v)

    def emit(self, e, fn, reads=(), writes=(), flag=True):
        reads = [b for b in reads if b is not None]
        writes = [b for b in writes if b is not None]
        self._deps(e, reads, writes)
        own = self.eidx[e]
        if flag:
            self.cnt[e] += 1
            tok = (own, self.cnt[e])
            sem = self.sems[own]
            self.ops[e].append(lambda eng, fn=fn, sem=sem: fn(eng).then_inc(sem, 1))
        else:
            tok = (own, self.cnt[e] + 1)
            self.ops[e].append(lambda eng, fn=fn: fn(eng))
        self.nins += 1
        for b in writes:
            b.w = tok
            b.r = {}
        for b in reads:
            if tok[1] > b.r.get(own, 0):
                b.r[own] = tok[1]

    def dma(self, q, out, in_, **kw):
        reads = [in_.buf]
        writes = [out.buf]
        self._deps(q, reads, writes, own_always=True)
        if q == "pool" and out.buf.arena and self.bar is not None:
            bc, bd = self.bar
            for f in self.ENG:
                if f != "pool":
                    self._wait(q, self.eidx[f], bc[f])
            for kk in range(ND // 2):
                self._wait(q, len(self.ENG) + kk, bd[kk])
        half = ND // 2
        if q == "pool":
            k = half + self.dma_rr2
            self.dma_rr2 = (self.dma_rr2 + 1) % half
        else:
            k = self.dma_rr
            self.dma_rr = (self.dma_rr + 1) % half
        sidx = len(self.ENG) + k
        self._wait(q, sidx, self.dma_cum[k])
        self.dma_cum[k] += 16
        tok = (sidx, self.dma_cum[k])
        sem = self.sems[sidx]
        oa, ia = out.ap, in_.ap
        self.ops[q].append(lambda eng, oa=oa, ia=ia, sem=sem, kw=kw: eng.dma_start(out=oa, in_=ia, **kw).then_inc(sem, 16))
        self.nins += 1
        out.buf.w = tok
        out.buf.r = {}
        if tok[1] > in_.buf.r.get(sidx, 0):
            in_.buf.r[sidx] = tok[1]

    def barrier(self):
        self.bar = (dict(self.cnt), list(self.dma_cum))
        for e in ("pe", "act", "dve", "sp"):
            for f in self.ENG:
                if f != e:
                    self._wait(e, self.eidx[f], self.cnt[f])
            for k in range(ND // 2):
                self._wait(e, len(self.ENG) + k, self.dma_cum[k])

    def finish(self):
        for k in range(ND):
            self._wait("sp", len(self.ENG) + k, self.dma_cum[k])
        for f in self.ENG:
            if f != "sp":
                self._wait("sp", self.eidx[f], self.cnt[f])

    def run(self):
        nc = self.nc
        ops = self.ops
        with nc.Block() as block:
            @block.tensor
            def _(eng):
                for f in ops["pe"]:
                    f(eng)

            @block.scalar
            def _(eng):
                for f in ops["act"]:
                    f(eng)

            @block.vector
            def _(eng):
                for f in ops["dve"]:
                    f(eng)

            @block.gpsimd
            def _(eng):
                for f in ops["pool"]:
                    f(eng)

            @block.sync
            def _(eng):
                for f in ops["sp"]:
                    f(eng)

    def mm(self, out, lhsT, rhs, start=True, stop=True, flag=True, skip=False):
        self.emit("pe", lambda eng, o=out.ap, l=lhsT.ap, r=rhs.ap, st=start, sp=stop, sk=skip:
                  eng.matmul(o, l, r, start=st, stop=sp, skip_group_check=sk),
                  reads=[lhsT.buf, rhs.buf], writes=[out.buf], flag=flag)

    def act(self, out, in_, func, bias=None, scale=None, e="act"):
        kw = {}
        rd = [in_.buf]
        if bias is not None:
            if isinstance(bias, V):
                kw["bias"] = bias.ap
                rd.append(bias.buf)
            else:
                kw["bias"] = bias
        if scale is not None:
            if isinstance(scale, V):
                kw["scale"] = scale.ap
                rd.append(scale.buf)
            else:
                kw["scale"] = scale
        self.emit("act", lambda eng, o=out.ap, i=in_.ap, f=func, kw=kw: eng.activation(o, i, f, **kw),
                  reads=rd, writes=[out.buf])

    def tt(self, out, in0, in1, op, e="dve"):
        self.emit(e, lambda eng, o=out.ap, a=in0.ap, b=in1.ap, op=op: eng.tensor_tensor(o, a, b, op),
                  reads=[in0.buf, in1.buf], writes=[out.buf])

    def stt(self, out, in0, scalar, in1, op0, op1, e="dve"):
        rd = [in0.buf, in1.buf]
        if isinstance(scalar, V):
            rd.append(scalar.buf)
            sc = scalar.ap
        else:
            sc = scalar
        self.emit(e, lambda eng, o=out.ap, a=in0.ap, s=sc, b=in1.ap, op0=op0, op1=op1:
                  eng.scalar_tensor_tensor(o, a, s, b, op0, op1),
                  reads=rd, writes=[out.buf])

    def ts(self, out, in0, s1, s2, op0, op1=None, e="dve"):
        rd = [in0.buf]
        a1 = s1
        a2 = s2
        if isinstance(s1, V):
            rd.append(s1.buf)
            a1 = s1.ap
        if isinstance(s2, V):
            rd.append(s2.buf)
            a2 = s2.ap
        if op1 is None:
            self.emit(e, lambda eng, o=out.ap, a=in0.ap, a1=a1, op0=op0: eng.tensor_scalar(o, a, a1, None, op0),
                      reads=rd, writes=[out.buf])
        else:
            self.emit(e, lambda eng, o=out.ap, a=in0.ap, a1=a1, a2=a2, op0=op0, op1=op1:
                      eng.tensor_scalar(o, a, a1, a2, op0, op1),
                      reads=rd, writes=[out.buf])

    def copy(self, out, in_, e="dve"):
        self.emit(e, lambda eng, o=out.ap, i=in_.ap: eng.tensor_copy(o, i), reads=[in_.buf], writes=[out.buf])

    def memset(self, out, val, e="dve"):
        self.emit(e, lambda eng, o=out.ap, v=val: eng.memset(o, v), reads=[], writes=[out.buf])

    def reduce(self, out, in_, op=None):
        self.emit("dve", lambda eng, o=out.ap, i=in_.ap: eng.tensor_reduce(o, i, mybir.AxisListType.X, ALU.add),
                  reads=[in_.buf], writes=[out.buf])

    def recip(self, out, in_):
        self.emit("dve", lambda eng, o=out.ap, i=in_.ap: eng.reciprocal(o, i), reads=[in_.buf], writes=[out.buf])

    def aselect(self, out, in_, pattern, cmp, fill, base, cm):
        self.emit("pool", lambda eng, o=out.ap, i=in_.ap: eng.affine_select(o, i, pattern, cmp, fill, base=base, channel_multiplier=cm),
                  reads=[in_.buf], writes=[out.buf])


class Cfg:
    def __init__(self, seq=4096, past=4096, stage=99, sbdbg=255):
        self.sbdbg = sbdbg
        self.seq = seq
        self.past = past
        self.stage = stage
        self.ntile = seq // TT


class Arena:
    def __init__(self, t, n4):
        self.t = t
        self.n4 = n4
        self.off = 0

    def reset(self, off=0):
        self.off = off

    def alloc(self, name, shape, dt=F32):
        n = 1
        for d in shape[1:]:
            n *= d
        n4 = n if dt == F32 else (n + 1) // 2
        n4 = (n4 + 1) // 2 * 2
        o = self.off
        self.off += n4
        assert self.off <= self.n4, ("arena overflow", name, self.off, self.n4)
        ap = self.t[0:shape[0], o:o + n4]
        if dt != F32:
            ap = ap.bitcast(dt)
        ap = ap[:, 0:n]
        if len(shape) == 3:
            ap = ap.rearrange("p (a b) -> p a b", a=shape[1])
        bf = Buf(ap, name)
        bf.arena = True
        return bf


def build(cfg):
    nc = bass.Bass("TRN2", target_bir_lowering=False)
    SEQ = cfg.seq
    st = ExitStack()
    with st:
        P = Prog(nc, st)

        def dram_in(name, shape, dt=F32):
            return Buf(nc.dram_tensor(name, list(shape), dt, kind="ExternalInput").ap(), name)

        def dram_out(name, shape, dt=F32):
            return Buf(nc.dram_tensor(name, list(shape), dt, kind="ExternalOutput").ap(), name)

        def sb(name, shape, dt=F32):
            return Buf(st.enter_context(nc.sbuf_tensor(name, list(shape), dt)), name)

        def ps(name, shape, dt=F32):
            return Buf(st.enter_context(nc.psum_tensor(name, list(shape), dt)), name, psum=True)

        xp = dram_in("xp", [D, SEQ])
        xs = dram_in("xs", [D, NS * DS])
        cT = dram_in("cT", [128, FC * NSEQ])
        normg = dram_in("normg", [128, 6 * FC])
        finalg = dram_in("finalg", [128, FC])
        ada_w = dram_in("ada_w", [2, D, 9 * D])
        ada_b = dram_in("ada_b", [2, 9 * D])
        ff_w_in = dram_in("ff_w_in", [2, 2, D, 2 * DFF])
        ff_w_out = dram_in("ff_w_out", [2, 2, DFF, D])
        ab_w_in = dram_in("ab_w_in", [D, 3600])
        ab_w_out = dram_in("ab_w_out", [D, D])
        convw_d = dram_in("convw", [128, 12 * 4])
        scw_d = dram_in("scw", [128, 4 * 3])
        alog_d = dram_in("alog", [64, 8])
        dtb_d = dram_in("dtb", [64, 8])
        dng_d = dram_in("dng", [64, 64])
        s_delta = dram_in("s_delta", [NS, 8, 64, 64])
        s_qkv = dram_in("s_qkv", [128, 12 * NS * 3])
        s_sconv = dram_in("s_sconv", [128, 4 * NS * 2])
        sb_w_qkv = dram_in("sb_w_qkv", [D, 3 * D])
        sb_w_out = dram_in("sb_w_out", [D, D])
        PAST = cfg.past
        ckT_d = dram_in("ckT", [NS, D, PAST])
        cv_d = dram_in("cv", [NS, PAST, D])
        kT_scr = Buf(nc.dram_tensor("kT_scr", [D, SEQ], BF16, kind="ExternalOutput").ap(), "kT_scr")
        v_scr = Buf(nc.dram_tensor("v_scr", [SEQ, D], BF16, kind="ExternalOutput").ap(), "v_scr")
        o_p_k = dram_out("o_p_k", [D, SEQ])
        o_p_v = dram_out("o_p_v", [SEQ, D])
        o_s_k = dram_out("o_s_k", [D, NS * DS])
        o_s_v = dram_out("o_s_v", [NS * DS, D])
        yp = dram_out("yp", [D, SEQ])
        ys = dram_out("ys", [D, NS * DS])
        o_p_delta = dram_out("o_p_delta", [8, 64, 64])
        o_s_delta = dram_out("o_s_delta", [NS, 8, 64, 64])
        o_p_qkv = dram_out("o_p_qkv", [128, 12 * 3])
        o_s_qkv = dram_out("o_s_qkv", [128, 12 * NS * 3])
        o_p_sconv = dram_out("o_p_sconv", [128, 4 * 2])
        o_s_sconv = dram_out("o_s_sconv", [128, 4 * NS * 2])

        ident = sb("ident", [128, 128])
        ones_bf = sb("ones_bf", [128, 128], BF16)
        onesf = sb("onesf", [128, 128])
        maskU = sb("maskU", [64, 64])
        maskL = sb("maskL", [64, 64])
        maskS = sb("maskS", [64, 64])
        condT = sb("condT", [128, FC * NSEQ])
        epsb = sb("epsb", [128, 1])
        g_sb = sb("g_sb", [128, 6 * FC])
        fg_sb = sb("fg_sb", [128, FC])
        modT = sb("modT", [128, 2 * 72 * NSEQ])
        gsT = sb("gsT", [128, 6 * FC * NSEQ])
        convw = sb("convw_s", [128, 12, 4])
        scw = sb("scw_s", [128, 4, 3])
        negA = sb("negA", [64, 8])
        dtb = sb("dtb_s", [64, 8])
        dng = sb("dng_s", [64, 64])
        ones512 = sb("ones512", [128, 512], BF16)
        negincl = sb("negincl", [128, 128], BF16)
        ones1b = sb("ones1b", [128, 128], BF16)
        mdiag = [sb("mdiag%d" % d, [128, 512], BF16) for d in range(4)]
        mnew = sb("mnew", [64, 512], BF16)
        hal3 = sb("hal3", [128, 12, 3])
        hal2 = sb("hal2", [128, 4, 2])
        S_sb = sb("S_sb", [64, 512])
        x = sb("x", [128, FC, TT])
        h = sb("h", [128, FC, TT], BF16)
        om = sb("om", [128, FC, TT], BF16)
        WPN = 2
        wp = [sb("wp%d" % i, [128, HC * 512], BF16) for i in range(WPN)]
        modrow = [sb("modrow%d" % i, [8, 512]) for i in range(2)]
        biasrow = [sb("biasrow%d" % i, [8, 512]) for i in range(2)]
        AR4 = 24576
        ar = Arena(st.enter_context(nc.sbuf_tensor("arena", [128, AR4], F32)), AR4)
        psb = [ps("ps%d" % i, [128, 512]) for i in range(8)]
        rr = {"ps": 0, "wp": 0, "mr": 0, "alt": 0, "psr": 0, "kTb": 0}

        def nxt(key, lst):
            i = rr[key]
            rr[key] = (i + 1) % len(lst)
            return lst[i]

        def evac(out, in_):
            rr["alt"] ^= 1
            if rr["alt"]:
                P.act(out, in_, AF.Copy)
            else:
                P.copy(out, in_)

        P.memset(onesf[:, :], 1.0, e="pool")
        P.memset(epsb[:, :], EPS, e="pool")
        P.memset(ones_bf[:, :], 1.0 / D, e="pool")
        P.aselect(ident[:, :], onesf[:, :], [[-1, 128]], ALU.is_equal, 0.0, 0, 1)
        P.aselect(maskU[:, :], onesf[0:64, 0:64], [[1, 64]], ALU.is_ge, 0.0, 0, -1)
        P.aselect(maskL[:, :], onesf[0:64, 0:64], [[-1, 64]], ALU.is_gt, 0.0, 0, 1)
        P.aselect(maskS[:, :], onesf[0:64, 0:64], [[1, 64]], ALU.is_gt, 0.0, 0, -1)
        P.memset(ones512[:, :], 1.0, e="pool")
        P.memset(ones1b[:, :], 1.0, e="pool")
        P.memset(negincl[:, :], -1.0, e="pool")
        P.aselect(negincl[:, :], negincl[:, :], [[-1, 128]], ALU.is_ge, 0.0, 0, 1)
        for d in range(4):
            P.aselect(mdiag[d][:, :], ones512[:, :], [[1, 512]], ALU.is_gt, 0.0, -128 * d, -1)
        P.aselect(V(mnew, mnew.t[:, :].rearrange("p (a b) -> p a b", a=8)),
                  V(ones512, ones512.t[0:64, :].rearrange("p (a b) -> p a b", a=8)), [[0, 8], [1, 64]], ALU.is_gt, 0.0, 0, -1)
        P.dma("sp", condT[:, :], cT[:, :])
        P.dma("sp", g_sb[:, :], normg[:, :])
        P.dma("sp", fg_sb[:, :], finalg[:, :])
        P.dma("sp", V(convw, convw.t[:, :, :]), V(convw_d, convw_d.t[:, :].rearrange("p (a b) -> p a b", a=12)))
        P.dma("sp", V(scw, scw.t[:, :, :]), V(scw_d, scw_d.t[:, :].rearrange("p (a b) -> p a b", a=4)))
        P.dma("sp", negA[:, :], alog_d[:, :])
        P.dma("sp", dtb[:, :], dtb_d[:, :])
        P.dma("sp", dng[:, :], dng_d[:, :])
        P.act(condT[:, :], condT[:, :], AF.Silu)
        P.act(negA[:, :], negA[:, :], AF.Exp)
        P.ts(negA[:, :], negA[:, :], -1.0, None, ALU.mult)
        P.memset(V(hal3, hal3.t[:, :, :]), 0.0)
        P.memset(V(hal2, hal2.t[:, :, :]), 0.0)
        P.memset(S_sb[:, :], 0.0)

        def mod_idx(l, chunk):
            return (l * 72 + chunk) * NSEQ

        for l in range(2):
            for cb in range(18):
                wb = nxt("wp", wp)
                wv = wb.t[:, 0:8192].bitcast(F32)
                src = ada_w.t[l, :, cb * 512:(cb + 1) * 512].rearrange("(kc p) n -> p kc n", p=128)
                P.dma("sp", V(wb, wv.rearrange("p (kc n) -> p kc n", kc=FC)), V(ada_w, src))
                br = nxt("mr", biasrow)
                mr = modrow[biasrow.index(br)]
                P.dma("sp", br[0:NSEQ, :], V(ada_b, ada_b.t[l, cb * 512:(cb + 1) * 512].partition_broadcast(NSEQ)))
                pt = nxt("ps", psb)
                for kc in range(FC):
                    P.mm(pt[0:NSEQ, :], V(condT, condT.t[:, kc * NSEQ:(kc + 1) * NSEQ]),
                         V(wb, wv[:, kc * 512:(kc + 1) * 512]), start=(kc == 0), stop=(kc == FC - 1),
                         flag=(kc == FC - 1))
                P.tt(mr[0:NSEQ, :], pt[0:NSEQ, :], br[0:NSEQ, :], ALU.add)
                pt2 = nxt("ps", psb)
                for j in range(4):
                    P.mm(pt2[:, j * NSEQ:(j + 1) * NSEQ], mr[0:NSEQ, j * 128:(j + 1) * 128],
                         ident[0:NSEQ, 0:NSEQ], flag=(j == 3))
                c0 = mod_idx(l, cb * 4)
                P.copy(modT[:, c0:c0 + 4 * NSEQ], pt2[:, 0:4 * NSEQ])

        def mod_ap(l, sub, kind, fc, seq):
            c0 = mod_idx(l, (sub * 3 + kind) * 8 + fc) + seq
            return V(modT, modT.t[:, c0:c0 + 1])

        def gs_ap(l, sub, fc, seq):
            c0 = ((l * 3 + sub) * FC + fc) * NSEQ + seq
            return V(gsT, gsT.t[:, c0:c0 + 1])

        for l in range(2):
            for sub in range(3):
                c0 = mod_idx(l, (sub * 3 + 1) * 8)
                o0 = (l * 3 + sub) * FC * NSEQ
                gv = g_sb.t[:, (l * 3 + sub) * FC:(l * 3 + sub + 1) * FC].unsqueeze(2).broadcast_to([128, FC, NSEQ])
                P.stt(V(gsT, gsT.t[:, o0:o0 + FC * NSEQ].rearrange("p (f s) -> p f s", s=NSEQ)),
                      V(modT, modT.t[:, c0:c0 + FC * NSEQ].rearrange("p (f s) -> p f s", s=NSEQ)),
                      1.0, V(g_sb, gv), ALU.add, ALU.mult)
                if sub != 1:
                    c2 = mod_idx(l, (sub * 3 + 2) * 8)
                    P.ts(modT[:, c2:c2 + FC * NSEQ], modT[:, c2:c2 + FC * NSEQ], 0.5, None, ALU.mult)

        def mod_norm(T, segs, l, sub, out_f32=None):
            ar.reset()
            P.barrier()
            sq = ar.alloc("sq", [128, FC, TT], BF16)
            rstd = ar.alloc("rstd", [128, TT])
            tmpn = [ar.alloc("tmpn%d" % i, [128, TT]) for i in range(2)]
            for fc in range(FC):
                P.act(sq[:, fc, 0:T], x[:, fc, 0:T], AF.Square)
            pt = nxt("ps", psb)
            for fc in range(FC):
                P.mm(pt[:, 0:T], ones_bf[:, :], sq[:, fc, 0:T], start=(fc == 0), stop=(fc == FC - 1),
                     flag=(fc == FC - 1))
            P.act(rstd[:, 0:T], pt[:, 0:T], AF.Sqrt, bias=V(epsb, epsb.t[:, 0:1]), scale=1.0)
            P.recip(rstd[:, 0:T], rstd[:, 0:T])
            for fc in range(FC):
                tn = tmpn[fc % 2]
                P.tt(tn[:, 0:T], x[:, fc, 0:T], rstd[:, 0:T], ALU.mult)
                for (c0, n, s) in segs:
                    if l is None:
                        P.act(out_f32[:, fc, c0:c0 + n], tn[:, c0:c0 + n], AF.Identity, scale=V(fg_sb, fg_sb.t[:, fc:fc + 1]))
                    else:
                        P.act(h[:, fc, c0:c0 + n], tn[:, c0:c0 + n], AF.Identity,
                              bias=mod_ap(l, sub, 0, fc, s), scale=gs_ap(l, sub, fc, s))

        def load_w(dst_buf, dst_ap, src_buf, src_ap):
            P.dma("pool", V(dst_buf, dst_ap), V(src_buf, src_ap))

        def proj_fm(W2d_buf, W2d, col0, ncols, KC, rhs, T, consume):
            c = 0
            while c < ncols:
                w = min(512, ncols - c)
                wb = nxt("wp", wp)
                wv = wb.t[:, 0:KC * 512].rearrange("p (kc n) -> p kc n", kc=KC)
                load_w(wb, wv[:, :, 0:w], W2d_buf, W2d[:, col0 + c:col0 + c + w].rearrange("(kc p) n -> p kc n", p=128))
                for j in range(w // 128):
                    pt = nxt("ps", psb)
                    for kc in range(KC):
                        P.mm(pt[:, 0:T], V(wb, wv[:, kc, j * 128:(j + 1) * 128]), rhs[:, kc, 0:T],
                             start=(kc == 0), stop=(kc == KC - 1), flag=(kc == KC - 1))
                    consume(c // 128 + j, pt)
                c += w

        def ffn(T, segs, l, sub, fi):
            w_in = ff_w_in.t[l, fi]
            w_out = ff_w_out.t[l, fi]
            ar.reset()
            P.barrier()
            hid = ar.alloc("hid", [128, HC, TT], BF16)
            sg = [ar.alloc("sg%d" % i, [128, TT]) for i in range(2)]
            c0 = 0
            k = 0
            while c0 < DFF:
                w = min(512, DFF - c0)
                wb = nxt("wp", wp)
                gview = wb.t[:, 0:FC * 512].rearrange("p (kc n) -> p kc n", kc=FC)
                uview = wb.t[:, FC * 512:2 * FC * 512].rearrange("p (kc n) -> p kc n", kc=FC)
                load_w(wb, gview[:, :, 0:w], ff_w_in, w_in[:, c0:c0 + w].rearrange("(kc p) n -> p kc n", p=128))
                load_w(wb, uview[:, :, 0:w], ff_w_in, w_in[:, DFF + c0:DFF + c0 + w].rearrange("(kc p) n -> p kc n", p=128))
                for j in range(w // 128):
                    pg = nxt("ps", psb)
                    pu = nxt("ps", psb)
                    for kc in range(FC):
                        P.mm(pg[:, 0:T], V(wb, gview[:, kc, j * 128:(j + 1) * 128]), h[:, kc, 0:T],
                             start=(kc == 0), stop=(kc == FC - 1), flag=(kc == FC - 1))
                    for kc in range(FC):
                        P.mm(pu[:, 0:T], V(wb, uview[:, kc, j * 128:(j + 1) * 128]), h[:, kc, 0:T],
                             start=(kc == 0), stop=(kc == FC - 1), flag=(kc == FC - 1))
                    s_ = sg[k % 2]
                    k += 1
                    P.act(s_[:, 0:T], pg[:, 0:T], AF.Silu)
                    P.tt(hid[:, c0 // 128 + j, 0:T], s_[:, 0:T], pu[:, 0:T], ALU.mult)
                c0 += w
            for half in range(2):
                wb = nxt("wp", wp)
                wv = wb.t[:, 0:HC * 512].rearrange("p (kc n) -> p kc n", kc=HC)
                for k0 in range(0, HC, 11):
                    load_w(wb, wv[:, k0:k0 + 11, :], ff_w_out,
                           w_out[k0 * 128:(k0 + 11) * 128, half * 512:(half + 1) * 512].rearrange("(kc p) n -> p kc n", p=128))
                for m in range(4):
                    po = nxt("ps", psb)
                    for kc in range(HC):
                        P.mm(po[:, 0:T], V(wb, wv[:, kc, m * 128:(m + 1) * 128]), hid[:, kc, 0:T],
                             start=(kc == 0), stop=(kc == HC - 1), flag=(kc == HC - 1))
                    fc = half * 4 + m
                    for (c0, n, s) in segs:
                        P.stt(x[:, fc, c0:c0 + n], po[:, c0:c0 + n], mod_ap(l, sub, 2, fc, s), x[:, fc, c0:c0 + n],
                              ALU.mult, ALU.add)

        def resid_add(l, sub, segs):
            def consume(j, pt):
                for (c0, n, s) in segs:
                    P.stt(x[:, j, c0:c0 + n], pt[:, c0:c0 + n], mod_ap(l, sub, 2, j, s), x[:, j, c0:c0 + n],
                          ALU.mult, ALU.add)
            return consume

        def bc3(v_ap, n_mid, n_in, axis):
            if axis == 2:
                return v_ap.unsqueeze(2).broadcast_to([v_ap.shape[0], n_mid, n_in])
            return v_ap.unsqueeze(1).broadcast_to([v_ap.shape[0], n_mid, n_in])

        def v3(buf, lo, nh, w=64):
            return buf.t[:, lo:lo + nh * w].rearrange("p (a b) -> p a b", a=nh)

        def ab_mixer(kind, ti, T, segs, last):
            W = ab_w_in.t
            ar.reset()
            P.barrier()
            offs3, offs2 = [], []
            o3 = o2 = 0
            for (c0, n, s) in segs:
                offs3.append(o3)
                offs2.append(o2)
                o3 += 3 + n
                o2 += 2 + n
            L3, L2 = o3, o2
            qkv_pre = ar.alloc("qkv_pre", [128, 12, L3])
            mark = ar.off
            sBs = ar.alloc("sBs", [128, 4, TT])
            sCs = ar.alloc("sCs", [128, 4, TT])
            scx = ar.alloc("scx", [128, 4, L2])
            acc = [ar.alloc("acc%d" % i, [128, L2]) for i in range(2)]

            if kind == "p":
                P.copy(V(qkv_pre, qkv_pre.t[:, :, 0:3]), V(hal3, hal3.t[:, :, :]))
                P.copy(V(scx, scx.t[:, :, 0:2]), V(hal2, hal2.t[:, :, :]))
            else:
                for si, (c0, n, s) in enumerate(segs):
                    q = s - 1
                    P.dma("sp", V(qkv_pre, qkv_pre.t[:, :, offs3[si]:offs3[si] + 3]),
                          V(s_qkv, s_qkv.t[:, :].rearrange("p (a q k) -> p a q k", a=12, q=NS)[:, :, q, :]))
                    P.dma("sp", V(scx, scx.t[:, :, offs2[si]:offs2[si] + 2]),
                          V(s_sconv, s_sconv.t[:, :].rearrange("p (a q k) -> p a q k", a=4, q=NS)[:, :, q, :]))

            def c_qkv(j, pt):
                for si, (c0, n, s) in enumerate(segs):
                    evac(V(qkv_pre, qkv_pre.t[:, j, offs3[si] + 3:offs3[si] + 3 + n]), pt[:, c0:c0 + n])
            proj_fm(ab_w_in, W, 0, 1536, FC, h, T, c_qkv)

            def c_sB(j, pt):
                evac(sBs[:, j, 0:T], pt[:, 0:T])

            def c_sC(j, pt):
                evac(sCs[:, j, 0:T], pt[:, 0:T])

            def c_sx(j, pt):
                for si, (c0, n, s) in enumerate(segs):
                    P.tt(V(scx, scx.t[:, j, offs2[si] + 2:offs2[si] + 2 + n]), sCs[:, j, c0:c0 + n], pt[:, c0:c0 + n], ALU.mult)
            proj_fm(ab_w_in, W, 2064, 512, FC, h, T, c_sB)
            proj_fm(ab_w_in, W, 2576, 512, FC, h, T, c_sC)
            proj_fm(ab_w_in, W, 3088, 512, FC, h, T, c_sx)
            for j in range(4):
                a_ = acc[j % 2]
                P.ts(a_[:, 0:L2 - 2], V(scx, scx.t[:, j, 0:L2 - 2]), V(scw, scw.t[:, j, 0:1]), None, ALU.mult)
                P.stt(a_[:, 0:L2 - 2], V(scx, scx.t[:, j, 1:L2 - 1]), V(scw, scw.t[:, j, 1:2]), a_[:, 0:L2 - 2], ALU.mult, ALU.add)
                P.stt(a_[:, 0:L2 - 2], V(scx, scx.t[:, j, 2:L2]), V(scw, scw.t[:, j, 2:3]), a_[:, 0:L2 - 2], ALU.mult, ALU.add)
                for si, (c0, n, s) in enumerate(segs):
                    P.tt(om[:, 4 + j, c0:c0 + n], sBs[:, j, c0:c0 + n], a_[:, offs2[si]:offs2[si] + n], ALU.mult)
            if kind == "p":
                P.copy(V(hal2, hal2.t[:, :, :]), V(scx, scx.t[:, :, T:T + 2]))
                if last:
                    P.dma("sp", V(o_p_sconv, o_p_sconv.t[:, :].rearrange("p (a k) -> p a k", a=4)), V(hal2, hal2.t[:, :, :]))
            else:
                for si, (c0, n, s) in enumerate(segs):
                    q = s - 1
                    P.dma("sp", V(o_s_sconv, o_s_sconv.t[:, :].rearrange("p (a q k) -> p a q k", a=4, q=NS)[:, :, q, :]),
                          V(scx, scx.t[:, :, offs2[si] + n:offs2[si] + n + 2]))

            P.barrier()
            ar.reset(mark)
            f = lambda nm, shp: ar.alloc(nm, shp)
            cacc = f("cacc", [128, 12, 64])
            ctmp = f("ctmp", [128, 12, 64])
            qkvc = f("qkvc", [128, 12, 64])
            Qr, Kr, Vr, Kb, Qg, RHSk, Kdec, zs, sqt, O = [f(nm, [64, 512]) for nm in
                                                           ("Qr", "Kr", "Vr", "Kb", "Qg", "RHSk", "Kdec", "zs", "sqt", "O")]
            sm = {nm: f(nm, [64, 16]) for nm in ("ssq", "rn", "ab", "t8", "g8", "b8", "beta", "Gs", "eGG", "dG", "eGe", "ss8")}
            G = {}
            for g in range(2):
                for nm in ("kT", "kbT", "qT", "qgT", "rhsE", "E", "EmS", "EmI", "Xa", "Xb", "XTa", "XTb", "Pa", "Pb",
                           "QKD", "solv", "nsolkT", "U", "Stmp"):
                    G[(nm, g)] = f(nm + str(g), [64, 256])
            zwb = nxt("wp", wp)
            zw = zwb.t[:, 0:FC * 528].rearrange("p (kc n) -> p kc n", kc=FC)
            load_w(zwb, zw[:, :, :], ab_w_in, W[:, 1536:2064].rearrange("(kc p) n -> p kc n", p=128))
            I64 = V(ident, ident.t[0:64, 0:64])
            one64 = V(onesf, onesf.t[0:64, 0:1])
            eps64 = V(epsb, epsb.t[0:64, 0:1])
            MUL, ADD, SUB = ALU.mult, ALU.add, ALU.subtract

            def chunk(cc, pc):
                def pre(k):
                    return V(qkv_pre, qkv_pre.t[:, :, pc + k:pc + k + 64])

                def wv(k):
                    return V(convw, convw.t[:, :, k:k + 1].broadcast_to([128, 12, 64]))
                A3 = lambda b: V(b, b.t[:, :, :])
                P.tt(A3(cacc), pre(0), wv(0), MUL)
                for k in range(1, 4):
                    P.tt(A3(ctmp), pre(k), wv(k), MUL)
                    P.tt(A3(cacc), A3(cacc), A3(ctmp), ADD)
                P.act(A3(qkvc), A3(cacc), AF.Silu)
                for dst, base in ((Qr, 0), (Kr, 4), (Vr, 8)):
                    pt = nxt("ps", psb)
                    for j in range(4):
                        P.mm(pt[0:64, j * 128:(j + 1) * 128], V(qkvc, qkvc.t[:, base + j, :]), ident[:, :], flag=(j == 3))
                    evac(dst[:, :], pt[0:64, :])
                pz = nxt("ps", psb)
                pab = nxt("ps", psb)
                for kc in range(FC):
                    P.mm(pz[0:64, :], h[:, kc, cc:cc + 64], V(zwb, zw[:, kc, 0:512]), start=(kc == 0), stop=(kc == FC - 1),
                         flag=(kc == FC - 1))
                for kc in range(FC):
                    P.mm(pab[0:64, 0:16], h[:, kc, cc:cc + 64], V(zwb, zw[:, kc, 512:528]), start=(kc == 0),
                         stop=(kc == FC - 1), flag=(kc == FC - 1))
                P.act(zs[:, :], pz[0:64, :], AF.Silu)
                P.copy(sm["ab"][:, :], pab[0:64, 0:16])
                P.tt(sqt[:, :], Qr[:, :], Qr[:, :], MUL)
                P.reduce(sm["ssq"][:, 0:8], V(sqt, v3(sqt, 0, 8)))
                P.tt(sqt[:, :], Kr[:, :], Kr[:, :], MUL)
                P.reduce(sm["ssq"][:, 8:16], V(sqt, v3(sqt, 0, 8)))
                P.act(sm["rn"][:, :], sm["ssq"][:, :], AF.Sqrt, bias=eps64, scale=1.0)
                P.recip(sm["rn"][:, :], sm["rn"][:, :])
                P.ts(sm["rn"][:, 0:8], sm["rn"][:, 0:8], 0.125, None, MUL)
                P.tt(V(Qr, v3(Qr, 0, 8)), V(Qr, v3(Qr, 0, 8)), V(sm["rn"], bc3(sm["rn"].t[:, 0:8], 8, 64, 2)), MUL)
                P.tt(V(Kr, v3(Kr, 0, 8)), V(Kr, v3(Kr, 0, 8)), V(sm["rn"], bc3(sm["rn"].t[:, 8:16], 8, 64, 2)), MUL)
                P.tt(sm["t8"][:, 0:8], sm["ab"][:, 0:8], dtb[:, :], ADD)
                P.act(sm["t8"][:, 0:8], sm["t8"][:, 0:8], AF.Exp)
                P.act(sm["t8"][:, 0:8], sm["t8"][:, 0:8], AF.Ln, bias=one64, scale=1.0)
                P.tt(sm["g8"][:, 0:8], sm["t8"][:, 0:8], negA[:, :], MUL)
                P.act(sm["b8"][:, 0:8], sm["ab"][:, 8:16], AF.Exp, scale=-1.0)
                P.ts(sm["b8"][:, 0:8], sm["b8"][:, 0:8], 1.0, None, ADD)
                P.recip(sm["beta"][:, 0:8], sm["b8"][:, 0:8])
                pg = nxt("ps", psb)
                P.mm(pg[0:64, 0:8], maskU[:, :], sm["g8"][:, 0:8], flag=False)
                P.mm(pg[0:64, 8:16], onesf[0:64, 0:64], sm["g8"][:, 0:8])
                P.copy(sm["Gs"][:, :], pg[0:64, 0:16])
                P.act(sm["eGG"][:, :], sm["Gs"][:, :], AF.Exp)
                P.tt(sm["dG"][:, 0:8], sm["Gs"][:, 8:16], sm["Gs"][:, 0:8], SUB)
                P.act(sm["eGe"][:, 0:8], sm["dG"][:, 0:8], AF.Exp)
                beta_b = V(sm["beta"], bc3(sm["beta"].t[:, 0:8], 8, 64, 2))
                eG_b = V(sm["eGG"], bc3(sm["eGG"].t[:, 0:8], 8, 64, 2))
                eGe_b = V(sm["eGe"], bc3(sm["eGe"].t[:, 0:8], 8, 64, 2))
                P.tt(V(Kb, v3(Kb, 0, 8)), V(Kr, v3(Kr, 0, 8)), beta_b, MUL)
                P.tt(V(Qg, v3(Qg, 0, 8)), V(Qr, v3(Qr, 0, 8)), eG_b, MUL)
                P.tt(V(RHSk, v3(RHSk, 0, 8)), V(Kb, v3(Kb, 0, 8)), eG_b, MUL)
                P.tt(V(Vr, v3(Vr, 0, 8)), V(Vr, v3(Vr, 0, 8)), beta_b, MUL)
                P.tt(V(Kdec, v3(Kdec, 0, 8)), V(Kr, v3(Kr, 0, 8)), eGe_b, MUL)
                for g in range(2):
                    T_ = lambda nm: G[(nm, g)]
                    hc = lambda i: slice((g * 4 + i) * 64, (g * 4 + i + 1) * 64)
                    lc = lambda i: slice(i * 64, (i + 1) * 64)
                    for src, dn in ((Kr, "kT"), (Kb, "kbT"), (Qr, "qT"), (Qg, "qgT")):
                        pt = nxt("ps", psb)
                        for i in range(4):
                            P.mm(pt[0:64, lc(i)], src[:, hc(i)], I64, flag=(i == 3))
                        evac(T_(dn)[:, :], pt[0:64, 0:256])
                    P.tt(V(T_("rhsE"), v3(T_("rhsE"), 0, 4)), V(maskU, bc3(maskU.t[:, :], 4, 64, 1)),
                         V(sm["g8"], bc3(sm["g8"].t[:, g * 4:g * 4 + 4], 4, 64, 2)), MUL)
                    pe_ = nxt("ps", psb)
                    P.mm(pe_[0:64, 0:256], maskL[:, :], T_("rhsE")[:, :])
                    P.act(T_("E")[:, :], pe_[0:64, 0:256], AF.Exp)
                    P.tt(V(T_("EmS"), v3(T_("EmS"), 0, 4)), V(T_("E"), v3(T_("E"), 0, 4)), V(maskS, bc3(maskS.t[:, :], 4, 64, 1)), MUL)
                    P.tt(V(T_("EmI"), v3(T_("EmI"), 0, 4)), V(T_("E"), v3(T_("E"), 0, 4)), V(maskU, bc3(maskU.t[:, :], 4, 64, 1)), MUL)
                    pA = nxt("ps", psb)
                    pQ = nxt("ps", psb)
                    for i in range(4):
                        P.mm(pA[0:64, lc(i)], T_("kT")[:, lc(i)], T_("kbT")[:, lc(i)], flag=(i == 3))
                    for i in range(4):
                        P.mm(pQ[0:64, lc(i)], T_("kT")[:, lc(i)], T_("qT")[:, lc(i)], flag=(i == 3))
                    P.tt(T_("Xa")[:, :], pA[0:64, 0:256], T_("EmS")[:, :], MUL)
                    P.tt(T_("QKD")[:, :], pQ[0:64, 0:256], T_("EmI")[:, :], MUL)
                    pX = nxt("ps", psb)
                    for i in range(4):
                        P.mm(pX[0:64, lc(i)], T_("Xa")[:, lc(i)], I64, flag=(i == 3))
                    evac(T_("XTa")[:, :], pX[0:64, 0:256])
                    P.tt(V(T_("Pa"), v3(T_("Pa"), 0, 4)), V(ident, bc3(ident.t[0:64, 0:64], 4, 64, 1)),
                         V(T_("Xa"), v3(T_("Xa"), 0, 4)), SUB)
                    Xc, XTc, Pc = "Xa", "XTa", "Pa"
                    for lvl in range(1, 6):
                        Xn = "Xb" if Xc == "Xa" else "Xa"
                        XTn = "XTb" if XTc == "XTa" else "XTa"
                        Pn = "Pb" if Pc == "Pa" else "Pa"
                        if lvl < 5:
                            p1 = nxt("ps", psb)
                            for i in range(4):
                                P.mm(p1[0:64, lc(i)], T_(XTc)[:, lc(i)], T_(Xc)[:, lc(i)], flag=(i == 3))
                        p2 = nxt("ps", psb)
                        for i in range(4):
                            P.mm(p2[0:64, lc(i)], T_(Xc)[:, lc(i)], T_(XTc)[:, lc(i)], flag=(i == 3))
                        if lvl < 5:
                            evac(T_(Xn)[:, :], p1[0:64, 0:256])
                        evac(T_(XTn)[:, :], p2[0:64, 0:256])
                        p3 = nxt("ps", psb)
                        for i in range(4):
                            P.mm(p3[0:64, lc(i)], T_(XTn)[:, lc(i)], T_(Pc)[:, lc(i)], flag=(i == 3))
                        P.tt(T_(Pn)[:, :], T_(Pc)[:, :], p3[0:64, 0:256], ADD)
                        Xc, XTc, Pc = Xn, XTn, Pn
                    TTn = Pc
                    pSv = nxt("ps", psb)
                    for i in range(4):
                        P.mm(pSv[0:64, lc(i)], T_(TTn)[:, lc(i)], Vr[:, hc(i)], flag=(i == 3))
                    evac(T_("solv")[:, :], pSv[0:64, 0:256])
                    pSk = nxt("ps", psb)
                    for i in range(4):
                        P.mm(pSk[0:64, lc(i)], RHSk[:, hc(i)], T_(TTn)[:, lc(i)], flag=(i == 3))
                    P.ts(T_("nsolkT")[:, :], pSk[0:64, 0:256], -1.0, None, MUL)
                for g in range(2):
                    T_ = lambda nm: G[(nm, g)]
                    hc = lambda i: slice((g * 4 + i) * 64, (g * 4 + i + 1) * 64)
                    lc = lambda i: slice(i * 64, (i + 1) * 64)
                    gs = slice(g * 256, (g + 1) * 256)
                    pU = nxt("ps", psb)
                    for i in range(4):
                        P.mm(pU[0:64, lc(i)], T_("nsolkT")[:, lc(i)], S_sb[:, hc(i)], flag=(i == 3))
                    P.tt(T_("U")[:, :], T_("solv")[:, :], pU[0:64, 0:256], ADD)
                    pO = nxt("ps", psb)
                    for i in range(4):
                        P.mm(pO[0:64, lc(i)], T_("qgT")[:, lc(i)], S_sb[:, hc(i)], start=True, stop=False, flag=False)
                        P.mm(pO[0:64, lc(i)], T_("QKD")[:, lc(i)], T_("U")[:, lc(i)], start=False, stop=True, flag=(i == 3))
                    evac(O[:, gs], pO[0:64, 0:256])
                    P.tt(V(T_("Stmp"), v3(T_("Stmp"), 0, 4)), V(S_sb, v3(S_sb, g * 256, 4)),
                         V(sm["eGG"], bc3(sm["eGG"].t[:, 8 + g * 4:8 + g * 4 + 4], 4, 64, 2)), MUL)
                    pS = nxt("ps", psb)
                    for i in range(4):
                        P.mm(pS[0:64, lc(i)], Kdec[:, hc(i)], T_("U")[:, lc(i)], flag=(i == 3))
                    P.tt(S_sb[:, gs], T_("Stmp")[:, :], pS[0:64, 0:256], ADD)
                P.tt(sqt[:, :], O[:, :], O[:, :], MUL)
                P.reduce(sm["ss8"][:, 0:8], V(sqt, v3(sqt, 0, 8)))
                P.act(sm["ss8"][:, 0:8], sm["ss8"][:, 0:8], AF.Sqrt, bias=eps64, scale=1.0 / 64)
                P.recip(sm["ss8"][:, 0:8], sm["ss8"][:, 0:8])
                P.tt(V(O, v3(O, 0, 8)), V(O, v3(O, 0, 8)), V(sm["ss8"], bc3(sm["ss8"].t[:, 0:8], 8, 64, 2)), MUL)
                P.tt(V(O, v3(O, 0, 8)), V(O, v3(O, 0, 8)), V(dng, bc3(dng.t[:, :], 8, 64, 1)), MUL)
                P.tt(O[:, :], O[:, :], zs[:, :], MUL)
                pT = nxt("ps", psb)
                for c in range(4):
                    P.mm(pT[:, c * 64:(c + 1) * 64], O[:, c * 128:(c + 1) * 128], I64, flag=(c == 3))
                evac(V(om, om.t[:, 0:4, cc:cc + 64]), V(pT, pT.t[:, 0:256].rearrange("p (a b) -> p a b", a=4)))

            for si, (c0, n, s) in enumerate(segs):
                if kind == "s":
                    P.dma("sp", V(S_sb, v3(S_sb, 0, 8)), V(s_delta, s_delta.t[s - 1].rearrange("h k v -> k h v")))
                for k in range(n // 64):
                    chunk(c0 + 64 * k, offs3[si] + 64 * k)
                if kind == "s":
                    P.dma("sp", V(o_s_delta, o_s_delta.t[s - 1].rearrange("h k v -> k h v")), V(S_sb, v3(S_sb, 0, 8)))
                    P.dma("sp", V(o_s_qkv, o_s_qkv.t[:, :].rearrange("p (a q k) -> p a q k", a=12, q=NS)[:, :, s - 1, :]),
                          V(qkv_pre, qkv_pre.t[:, :, offs3[si] + n:offs3[si] + n + 3]))
            if kind == "p":
                P.copy(V(hal3, hal3.t[:, :, :]), V(qkv_pre, qkv_pre.t[:, :, T:T + 3]))
                if last:
                    P.dma("sp", V(o_p_qkv, o_p_qkv.t[:, :].rearrange("p (a k) -> p a k", a=12)), V(hal3, hal3.t[:, :, :]))
                    P.dma("sp", V(o_p_delta, o_p_delta.t[:, :, :].rearrange("h k v -> k h v")), V(S_sb, v3(S_sb, 0, 8)))
            proj_fm(ab_w_out, ab_w_out.t, 0, D, FC, om, T, resid_add(0, 1, segs))

        def sb_mixer(kind, ti, T, segs):
            Wq = sb_w_qkv.t
            ar.reset()
            P.barrier()
            DB = cfg.sbdbg
            qT = ar.alloc("qT", [128, FC, TT], BF16)
            kbf = ar.alloc("kbf", [128, FC, TT], BF16)
            vbf = ar.alloc("vbf", [128, 4, D], BF16)
            mark = ar.off
            kf = ar.alloc("kf", [128, FC, TT])
            vf = ar.alloc("vf", [128, 4, D])
            vblk = 128 if kind == "p" else 64
            nvb = T // vblk

            def c_q(j, pt):
                P.act(qT[:, j, 0:T], pt[:, 0:T], AF.Copy, scale=0.125)

            def c_k(j, pt):
                P.act(kf[:, j, 0:T], pt[:, 0:T], AF.Copy)
                P.copy(kbf[:, j, 0:T], pt[:, 0:T])
            if DB & 1:
                proj_fm(sb_w_qkv, Wq, 0, D, FC, h, T, c_q)
            if DB & 2:
                proj_fm(sb_w_qkv, Wq, D, D, FC, h, T, c_k)
            for half in range(2 if DB & 4 else 0):
                wb = nxt("wp", wp)
                wv = wb.t[:, 0:FC * 512].rearrange("p (kc n) -> p kc n", kc=FC)
                load_w(wb, wv[:, :, :], sb_w_qkv, Wq[:, 2 * D + half * 512:2 * D + (half + 1) * 512].rearrange("(kc p) n -> p kc n", p=128))
                for b in range(nvb):
                    pt = nxt("ps", psb)
                    for kc in range(FC):
                        P.mm(pt[0:vblk, :], h[:, kc, b * vblk:(b + 1) * vblk], V(wb, wv[:, kc, :]),
                             start=(kc == 0), stop=(kc == FC - 1), flag=(kc == FC - 1))
                    P.act(vf[0:vblk, b, half * 512:(half + 1) * 512], pt[0:vblk, :], AF.Copy)
                    P.copy(vbf[0:vblk, b, half * 512:(half + 1) * 512], pt[0:vblk, :])
            t0 = ti * TT
            if not (DB & 8):
                pass
            elif kind == "p":
                P.dma("sp", V(o_p_k, o_p_k.t[:, t0:t0 + T].rearrange("(fc p) t -> p fc t", p=128)), kf[:, :, 0:T])
                P.dma("sp", V(o_p_v, o_p_v.t[t0:t0 + T, :].rearrange("(b p) c -> p b c", p=128)), vf[:, 0:nvb, :])
                if DB & 16:
                    P.dma("sp", V(kT_scr, kT_scr.t[:, t0:t0 + T].rearrange("(fc p) t -> p fc t", p=128)), kbf[:, :, 0:T])
                    P.dma("sp", V(v_scr, v_scr.t[t0:t0 + T, :].rearrange("(b p) c -> p b c", p=128)), vbf[:, 0:nvb, :])
            else:
                P.dma("sp", V(o_s_k, o_s_k.t[:, 0:T].rearrange("(fc p) t -> p fc t", p=128)), kf[:, :, 0:T])
                P.dma("sp", V(o_s_v, o_s_v.t[0:T, :].rearrange("(b p) c -> p b c", p=64)), vf[0:64, 0:nvb, :])

            if not (DB & 96):
                return
            P.barrier()
            ar.reset(mark)
            E_ = [ar.alloc("E%d" % i, [128, 512]) for i in range(2)]
            SP_ = [ar.alloc("SP%d" % i, [128, 512], BF16) for i in range(2)]
            ARG_ = [ar.alloc("ARG%d" % i, [128, 512]) for i in range(2)]
            W_ = [ar.alloc("W%d" % i, [128, 512], BF16) for i in range(2)]
            R_ = [ar.alloc("R%d" % i, [128, 512]) for i in range(2)]
            kTb = [ar.alloc("kTb%d" % i, [128, 4096], BF16) for i in range(2)]
            vb = [ar.alloc("vb%d" % i, [128, 4, 512], BF16) for i in range(2)]
            if kind == "s":
                qS = ar.alloc("qS", [64, 16, NS * DS], BF16)
                kS = ar.alloc("kS", [64, 16, NS * DS], BF16)
                for dstb, srcb in ((qS, qT), (kS, kbf)):
                    dv = dstb.t[:, :, :].rearrange("p (c two) t -> p c two t", two=2)
                    P.dma("sp", V(dstb, dv[:, :, 0, :]), V(srcb, srcb.t[0:64, :, 0:T]))
                    P.dma("sp", V(dstb, dv[:, :, 1, :]), V(srcb, srcb.t[64:128, :, 0:T]))
            psr = psb[0:6]
            one_col = lambda kk: V(onesf, onesf.t[0:kk, 0:1])

            def run_stream(blocks):
                nb = len(blocks)
                st_ = {}

                def stA(b):
                    B = blocks[b]
                    kk, GW = B["kk"], B["GW"]
                    pz = nxt("psr", psr)
                    st_[b] = pz
                    hl = B["heads"]
                    for i, (c0, N, kTv, qv, vv, pov, ost) in enumerate(hl):
                        P.mm(pz[0:kk, c0:c0 + N], kTv, qv, start=(i == 0), stop=False, flag=(i == len(hl) - 1), skip=True)
                    E, SP = E_[b % 2], SP_[b % 2]
                    P.act(E[0:kk, 0:GW], pz[0:kk, 0:GW], AF.Exp)
                    P.act(SP[0:kk, 0:GW], E[0:kk, 0:GW], AF.Ln, bias=one_col(kk), scale=1.0)
                    if B["mask"] is not None:
                        P.tt(SP[0:kk, 0:GW], SP[0:kk, 0:GW], B["mask"], ALU.mult)

                def stB(b):
                    B = blocks[b]
                    kk, GW = B["kk"], B["GW"]
                    pz = st_[b]
                    SP, ARG, W, R = SP_[b % 2], ARG_[b % 2], W_[b % 2], B["R"]
                    P.mm(pz[0:kk, 0:GW], negincl[0:kk, 0:kk], SP[0:kk, 0:GW], start=False, stop=True, skip=True)
                    pt = None
                    if not B["last"]:
                        pt = nxt("psr", psr)
                        P.mm(pt[:, 0:GW], ones1b[0:kk, :], SP[0:kk, 0:GW])
                    if B["first"]:
                        P.act(W[0:kk, 0:GW], pz[0:kk, 0:GW], AF.Exp)
                    else:
                        P.tt(ARG[0:kk, 0:GW], pz[0:kk, 0:GW], R[0:kk, 0:GW], ALU.subtract)
                        P.act(W[0:kk, 0:GW], ARG[0:kk, 0:GW], AF.Exp)
                    if B["mask"] is not None:
                        P.tt(W[0:kk, 0:GW], W[0:kk, 0:GW], B["mask"], ALU.mult)
                    if pt is not None:
                        if B["first"]:
                            P.copy(R[:, 0:GW], pt[:, 0:GW])
                        else:
                            P.tt(R[:, 0:GW], R[:, 0:GW], pt[:, 0:GW], ALU.add)

                def stC(b):
                    B = blocks[b]
                    kk = B["kk"]
                    W = W_[b % 2]
                    hl = B["heads"]
                    for i, (c0, N, kTv, qv, vv, pov, ost) in enumerate(hl):
                        P.mm(pov, vv, W[0:kk, c0:c0 + N], start=(B["first"] and ost), stop=B["last"], flag=(i == len(hl) - 1), skip=True)

                for step in range(nb + 2):
                    if step < nb:
                        stA(step)
                    if 0 <= step - 1 < nb:
                        stB(step - 1)
                    if 0 <= step - 2 < nb:
                        stC(step - 2)
                        if "post" in blocks[step - 2]:
                            blocks[step - 2]["post"]()

            if kind == "p" and (DB & 32):
                i = ti
                for c in range(8):
                    po = psb[6 + c % 2]
                    blocks = []
                    last_of = []
                    for j, kb in enumerate(range(i, -1, -1)):
                        kt = kTb[j % 2]
                        vt = vb[j % 2]

                        def load(kt=kt, vt=vt, kb=kb):
                            P.dma("sp", V(kt, kt.t[:, 0:512]), V(kT_scr, kT_scr.t[c * 128:(c + 1) * 128, kb * 512:(kb + 1) * 512]))
                            P.dma("sp", V(vt, vt.t[:, :, 0:128]),
                                  V(v_scr, v_scr.t[kb * 512:(kb + 1) * 512, c * 128:(c + 1) * 128].rearrange("(j p) c -> p j c", p=128)))
                        if j < 2:
                            load()
                        else:
                            last_of[j - 2]["post"] = load
                        for d in range(3, -1, -1):
                            for hh in range(2):
                                pb = hh * 64
                                blocks.append(dict(
                                    kk=128, GW=512,
                                    heads=[(0, 512, V(kt, kt.t[pb:pb + 64, d * 128:(d + 1) * 128]),
                                            V(qT, qT.t[pb:pb + 64, c, 0:512]),
                                            V(vt, vt.t[:, d, hh * 64:(hh + 1) * 64]),
                                            V(po, po.t[pb:pb + 64, 0:512]), True)],
                                    mask=(mdiag[d][:, :] if kb == i else None),
                                    first=(kb == i and d == 3), last=(kb == 0 and d == 0), R=R_[hh]))
                        last_of.append(blocks[-1])
                    run_stream(blocks)
                    evac(om[:, c, 0:T], po[:, 0:T])
            if kind == "s" and (DB & 64):
                NKB = PAST // 512
                for si, (c0, n, s) in enumerate(segs):
                    q = s - 1
                    for g in range(2):
                        po = psb[6 + (si * 2 + g) % 2]
                        blocks = []

                        def heads_for(kk, kT_of, v_of):
                            hl = []
                            for gi in range(8):
                                hd = g * 8 + gi
                                pb = (hd % 2) * 64
                                hl.append((gi * 64, 64, kT_of(hd, pb), V(qS, qS.t[0:64, hd, c0:c0 + 64]),
                                           v_of(hd), V(po, po.t[pb:pb + 64, (gi // 2) * 64:(gi // 2 + 1) * 64]), gi < 2))
                            return hl
                        blocks.append(dict(
                            kk=64, GW=512,
                            heads=heads_for(64, lambda hd, pb: V(kS, kS.t[0:64, hd, c0:c0 + 64]),
                                            lambda hd: V(vbf, vbf.t[0:64, si, hd * 64:(hd + 1) * 64])),
                            mask=mnew[:, :], first=True, last=(NKB == 0), R=R_[0]))
                        last_of = []
                        for j, kb in enumerate(range(NKB - 1, -1, -1)):
                            kt = kTb[j % 2]
                            vt = vb[j % 2]
                            ktv = kt.t[0:64, :].rearrange("p (hh k) -> p hh k", hh=8)

                            def load(kt=kt, vt=vt, ktv=ktv, kb=kb):
                                P.dma("pool", V(kt, ktv),
                                      V(ckT_d, ckT_d.t[q, g * 512:(g + 1) * 512, kb * 512:(kb + 1) * 512].rearrange("(hh dd) k -> dd hh k", dd=64)))
                                P.dma("pool", V(vt, vt.t[:, :, :]),
                                      V(cv_d, cv_d.t[q, kb * 512:(kb + 1) * 512, g * 512:(g + 1) * 512].rearrange("(j p) c -> p j c", p=128)))
                            if j < 2:
                                load()
                            else:
                                last_of[j - 2]["post"] = load
                            for d in range(3, -1, -1):
                                blocks.append(dict(
                                    kk=128, GW=512,
                                    heads=heads_for(128, lambda hd, pb, kt=kt, ktv=ktv, d=d: V(kt, ktv[:, hd - 8 * g, d * 128:(d + 1) * 128]),
                                                    lambda hd, vt=vt, d=d: V(vt, vt.t[:, d, (hd - 8 * g) * 64:(hd - 8 * g + 1) * 64])),
                                    mask=None, first=False, last=(kb == 0 and d == 0), R=R_[0]))
                            last_of.append(blocks[-1])
                        run_stream(blocks)
                        evac(V(om, om.t[:, 4 * g:4 * g + 4, c0:c0 + 64]), V(po, po.t[:, 0:256].rearrange("p (a b) -> p a b", a=4)))
            proj_fm(sb_w_out, sb_w_out.t, 0, D, FC, om, T, resid_add(1, 1, segs))

        tiles = []
        for i in range(cfg.ntile):
            tiles.append(("p", i, TT, [(0, TT, 0)], i == cfg.ntile - 1))
        tiles.append(("s", 0, NS * DS, [(j * DS, DS, 1 + j) for j in range(NS)], True))

        for (kind, ti, T, segs, last) in tiles:
            src = xp if kind == "p" else xs
            dst = yp if kind == "p" else ys
            t0 = ti * TT
            P.barrier()
            P.dma("sp", x[:, :, 0:T], V(src, src.t[:, t0:t0 + T].rearrange("(fc p) t -> p fc t", p=128)))
            for l in range(2):
                mod_norm(T, segs, l, 0)
                ffn(T, segs, l, 0, 0)
                if cfg.stage <= 1:
                    break
                mod_norm(T, segs, l, 1)
                if l == 0:
                    ab_mixer(kind, ti, T, segs, last)
                    if cfg.stage <= 2:
                        break
                else:
                    sb_mixer(kind, ti, T, segs)
                    if cfg.stage <= 3:
                        break
                mod_norm(T, segs, l, 2)
                ffn(T, segs, l, 2, 1)
            if cfg.stage >= 99:
                ar.reset()
                yo = Buf(ar.t[:, AR4 - FC * TT:AR4].rearrange("p (a b) -> p a b", a=FC), "yo")
                mod_norm(T, segs, None, None, out_f32=yo)
                P.dma("sp", V(dst, dst.t[:, t0:t0 + T].rearrange("(fc p) t -> p fc t", p=128)), yo[:, :, 0:T])
            else:
                P.dma("sp", V(dst, dst.t[:, t0:t0 + T].rearrange("(fc p) t -> p fc t", p=128)), x[:, :, 0:T])

        P.finish()
        P.run()
        print("instructions:", P.nins, {e: len(P.ops[e]) for e in P.ENG})
    return nc


def host_ab(inp, ssl):
    f = np.ascontiguousarray
    cw = inp["ab_conv_qkv"][0]
    scw = inp["sc_conv"][0]
    sq = inp["state_qkv_conv"][0, ssl]
    ss = inp["state_sconv"][0, ssl]
    return {
        "ab_w_in": inp["ab_w_in"][0], "ab_w_out": inp["ab_w_out"][0],
        "convw": f(cw.reshape(4, 12, 128).transpose(2, 1, 0).reshape(128, 48)),
        "scw": f(scw.reshape(3, 4, 128).transpose(2, 1, 0).reshape(128, 12)),
        "alog": f(np.broadcast_to(inp["dn_A_log"][0][None, :], (64, 8))),
        "dtb": f(np.broadcast_to(inp["dn_dt_bias"][0][None, :], (64, 8))),
        "dng": f(np.broadcast_to(inp["dn_norm_g"][0][None, :], (64, 64))),
        "s_delta": f(inp["state_delta"][0, ssl]),
        "s_qkv": f(sq.reshape(NS, 3, 12, 128).transpose(3, 2, 0, 1).reshape(128, 12 * NS * 3)),
        "s_sconv": f(ss.reshape(NS, 2, 4, 128).transpose(3, 2, 0, 1).reshape(128, 4 * NS * 2)),
    }


def host_sb(inp, ssl, past):
    f = np.ascontiguousarray
    ck = inp["cache_k"][0, ssl, :, :past]
    cv = inp["cache_v"][0, ssl, :, :past]
    return {
        "sb_w_qkv": inp["sb_w_qkv"][0], "sb_w_out": inp["sb_w_out"][0],
        "ckT": f(ck.transpose(0, 1, 3, 2).reshape(ck.shape[0], 1024, past)),
        "cv": f(cv.transpose(0, 2, 1, 3).reshape(cv.shape[0], past, 1024)),
    }


def host_common(inp, b, ssl):
    f = np.ascontiguousarray
    c_all = np.concatenate([inp["c_prompt"][b][None], inp["c_sample"][ssl]], 0)
    cT = c_all.T.reshape(8, 128, NSEQ).transpose(1, 0, 2).reshape(128, 8 * NSEQ)
    normg = inp["norm_g"].reshape(6, 8, 128).transpose(2, 0, 1).reshape(128, 48)
    finalg = inp["final_g"].reshape(8, 128).T
    return {
        "xp": f(inp["x_prompt"][b].T), "xs": f(inp["x_sample"][ssl].reshape(NS * DS, D).T),
        "cT": f(cT), "normg": f(normg), "finalg": f(finalg),
        "ada_w": inp["ada_w"], "ada_b": inp["ada_b"], "ff_w_in": inp["ff_w_in"], "ff_w_out": inp["ff_w_out"],
    }


_NC_CACHE = {}


def kernel(**inputs):
    inp = {k: np.asarray(v, dtype=np.float32) for k, v in inputs.items()}
    SEQ = inp["x_prompt"].shape[1]
    PAST = inp["cache_k"].shape[3]
    key = (SEQ, PAST)
    if key not in _NC_CACHE:
        _NC_CACHE[key] = build(Cfg(seq=SEQ, past=PAST, stage=99))
    nc = _NC_CACHE[key]
    n_cores = 8
    in_maps = []
    for c in range(n_cores):
        b = c // 2
        ssl = slice(NS * c, NS * (c + 1))
        m = host_common(inp, b, ssl)
        m.update(host_ab(inp, ssl))
        m.update(host_sb(inp, ssl, PAST))
        in_maps.append(m)
    res = run_bass_kernel_spmd(nc, in_maps, core_ids=list(range(n_cores)))
    R = res.results
    NB = inp["x_prompt"].shape[0]
    f32 = np.float32
    y_prompt = np.stack([R[2 * b]["yp"].T for b in range(NB)]).astype(f32)
    y_sample = np.concatenate([R[c]["ys"].T.reshape(NS, DS, D) for c in range(n_cores)]).astype(f32)
    p_delta = np.stack([R[2 * b]["o_p_delta"] for b in range(NB)])[None].astype(f32)
    p_qkv = np.stack([R[2 * b]["o_p_qkv"].reshape(128, 12, 3).transpose(2, 1, 0).reshape(3, 1536) for b in range(NB)])[None].astype(f32)
    p_sc = np.stack([R[2 * b]["o_p_sconv"].reshape(128, 4, 2).transpose(2, 1, 0).reshape(2, 512) for b in range(NB)])[None].astype(f32)
    p_k = np.stack([R[2 * b]["o_p_k"].reshape(16, 64, SEQ).transpose(0, 2, 1) for b in range(NB)])[None].astype(f32)
    p_v = np.stack([R[2 * b]["o_p_v"].reshape(SEQ, 16, 64).transpose(1, 0, 2) for b in range(NB)])[None].astype(f32)
    s_delta = np.concatenate([R[c]["o_s_delta"] for c in range(n_cores)])[None].astype(f32)
    s_qkv = np.concatenate([R[c]["o_s_qkv"].reshape(128, 12, NS, 3).transpose(2, 3, 1, 0).reshape(NS, 3, 1536) for c in range(n_cores)])[None].astype(f32)
    s_sc = np.concatenate([R[c]["o_s_sconv"].reshape(128, 4, NS, 2).transpose(2, 3, 1, 0).reshape(NS, 2, 512) for c in range(n_cores)])[None].astype(f32)
    s_k = np.concatenate([R[c]["o_s_k"].reshape(16, 64, NS, DS).transpose(2, 0, 3, 1) for c in range(n_cores)])[None].astype(f32)
    s_v = np.concatenate([R[c]["o_s_v"].reshape(NS, DS, 16, 64).transpose(0, 2, 1, 3) for c in range(n_cores)])[None].astype(f32)
    asc = np.ascontiguousarray
    return tuple(asc(a) for a in (y_prompt, y_sample, p_delta, p_qkv, p_sc, p_k, p_v, s_delta, s_qkv, s_sc, s_k, s_v))
```

```python
import numpy as np
from contextlib import ExitStack
import concourse.bass as bass
import concourse.mybir as mybir
from concourse.bass_utils import run_bass_kernel_spmd

F32 = mybir.dt.float32
BF16 = mybir.dt.bfloat16
AF = mybir.ActivationFunctionType
ALU = mybir.AluOpType

D = 1024
FC = 8
DFF = 2816
HC = 22
NS = 4
DS = 64
NSEQ = 1 + NS
TT = 512
EPS = 1e-6
SAME_SYNC = True
ND = 24


class V:
    __slots__ = ("buf", "ap")

    def __init__(self, buf, ap):
        self.buf = buf
        self.ap = ap


class Buf:
    def __init__(self, t, name="", psum=False):
        self.t = t
        self.name = name
        self.w = None
        self.r = {}
        self.psum = psum
        self.arena = False

    def __getitem__(self, idx):
        return V(self, self.t[idx])

    def v(self, ap):
        return V(self, ap)


class Prog:
    ENG = ["pe", "act", "dve", "pool", "sp"]

    def __init__(self, nc, stack):
        self.nc = nc
        self.stack = stack
        self.ops = {e: [] for e in self.ENG}
        self.sems = []
        for e in self.ENG:
            self.sems.append(stack.enter_context(nc.semaphore("s_" + e)))
        for k in range(ND):
            self.sems.append(stack.enter_context(nc.semaphore("d%d" % k)))
        self.eidx = {e: i for i, e in enumerate(self.ENG)}
        self.cnt = {e: 0 for e in self.ENG}
        self.waited = {e: {} for e in self.ENG}
        self.dma_cum = [0] * ND
        self.dma_rr = 0
        self.dma_rr2 = 0
        self.bar = None
        self.nins = 0

    def _wait(self, e, s, v):
        if v <= 0 or self.waited[e].get(s, 0) >= v:
            return
        self.waited[e][s] = v
        sem = self.sems[s]
        self.ops[e].append(lambda eng, sem=sem, v=v: eng.wait_ge(sem, v))
        self.nins += 1

    def _deps(self, e, reads, writes, own_always=False):
        deps = {}

        def add(s, v):
            if v > deps.get(s, 0):
                deps[s] = v
        for b in reads:
            if b.w:
                add(*b.w)
            if b.psum:
                for s, v in b.r.items():
                    add(s, v)
        for b in writes:
            if b.w:
                add(*b.w)
            for s, v in b.r.items():
                add(s, v)
        own = self.eidx[e]
        for s, v in deps.items():
            if s == own and not own_always and (e == "pe" or not SAME_SYNC):
                continue
            self._wait(e, s, v)

    def emit(self, e, fn, reads=(), writes=(), flag=True):
        reads = [b for b in reads if b is not None]
        writes = [b for b in writes if b is not None]
        self._deps(e, reads, writes)
        own = self.eidx[e]
        if flag:
            self.cnt[e] += 1
            tok = (own, self.cnt[e])
            sem = self.sems[own]
            self.ops[e].append(lambda eng, fn=fn, sem=sem: fn(eng).then_inc(sem, 1))
        else:
            tok = (own, self.cnt[e] + 1)
            self.ops[e].append(lambda eng, fn=fn: fn(eng))
        self.nins += 1
        for b in writes:
            b.w = tok
            b.r = {}
        for b in reads:
            if tok[1] > b.r.get(own, 0):
                b.r[own] = tok[1]

    def dma(self, q, out, in_, **kw):
        reads = [in_.buf]
        writes = [out.buf]
        self._deps(q, reads, writes, own_always=True)
        if q == "pool" and out.buf.arena and self.bar is not None:
            bc, bd = self.bar
            for f in self.ENG:
                if f != "pool":
                    self._wait(q, self.eidx[f], bc[f])
            for kk in range(ND // 2):
                self._wait(q, len(self.ENG) + kk, bd[kk])
        half = ND // 2
        if q == "pool":
            k = half + self.dma_rr2
            self.dma_rr2 = (self.dma_rr2 + 1) % half
        else:
            k = self.dma_rr
            self.dma_rr = (self.dma_rr + 1) % half
        sidx = len(self.ENG) + k
        self._wait(q, sidx, self.dma_cum[k])
        self.dma_cum[k] += 16
        tok = (sidx, self.dma_cum[k])
        sem = self.sems[sidx]
        oa, ia = out.ap, in_.ap
        self.ops[q].append(lambda eng, oa=oa, ia=ia, sem=sem, kw=kw: eng.dma_start(out=oa, in_=ia, **kw).then_inc(sem, 16))
        self.nins += 1
        out.buf.w = tok
        out.buf.r = {}
        if tok[1] > in_.buf.r.get(sidx, 0):
            in_.buf.r[sidx] = tok[1]

    def barrier(self):
        self.bar = (dict(self.cnt), list(self.dma_cum))
        for e in ("pe", "act", "dve", "sp"):
            for f in self.ENG:
                if f != e:
                    self._wait(e, self.eidx[f], self.cnt[f])
            for k in range(ND // 2):
                self._wait(e, len(self.ENG) + k, self.dma_cum[k])

    def finish(self):
        for k in range(ND):
            self._wait("sp", len(self.ENG) + k, self.dma_cum[k])
        for f in self.ENG:
            if f != "sp":
                self._wait("sp", self.eidx[f], self.cnt[f])

    def run(self):
        nc = self.nc
        ops = self.ops
        with nc.Block() as block:
            @block.tensor
            def _(eng):
                for f in ops["pe"]:
                    f(eng)

            @block.scalar
            def _(eng):
                for f in ops["act"]:
                    f(eng)

            @block.vector
            def _(eng):
                for f in ops["dve"]:
                    f(eng)

            @block.gpsimd
            def _(eng):
                for f in ops["pool"]:
                    f(eng)

            @block.sync
            def _(eng):
                for f in ops["sp"]:
                    f(eng)

    def mm(self, out, lhsT, rhs, start=True, stop=True, flag=True, skip=False):
        self.emit("pe", lambda eng, o=out.ap, l=lhsT.ap, r=rhs.ap, st=start, sp=stop, sk=skip:
                  eng.matmul(o, l, r, start=st, stop=sp, skip_group_check=sk),
                  reads=[lhsT.buf, rhs.buf], writes=[out.buf], flag=flag)

    def act(self, out, in_, func, bias=None, scale=None, e="act"):
        kw = {}
        rd = [in_.buf]
        if bias is not None:
            if isinstance(bias, V):
                kw["bias"] = bias.ap
                rd.append(bias.buf)
            else:
                kw["bias"] = bias
        if scale is not None:
            if isinstance(scale, V):
                kw["scale"] = scale.ap
                rd.append(scale.buf)
            else:
                kw["scale"] = scale
        self.emit("act", lambda eng, o=out.ap, i=in_.ap, f=func, kw=kw: eng.activation(o, i, f, **kw),
                  reads=rd, writes=[out.buf])

    def tt(self, out, in0, in1, op, e="dve"):
        self.emit(e, lambda eng, o=out.ap, a=in0.ap, b=in1.ap, op=op: eng.tensor_tensor(o, a, b, op),
                  reads=[in0.buf, in1.buf], writes=[out.buf])

    def stt(self, out, in0, scalar, in1, op0, op1, e="dve"):
        rd = [in0.buf, in1.buf]
        if isinstance(scalar, V):
            rd.append(scalar.buf)
            sc = scalar.ap
        else:
            sc = scalar
        self.emit(e, lambda eng, o=out.ap, a=in0.ap, s=sc, b=in1.ap, op0=op0, op1=op1:
                  eng.scalar_tensor_tensor(o, a, s, b, op0, op1),
                  reads=rd, writes=[out.buf])

    def ts(self, out, in0, s1, s2, op0, op1=None, e="dve"):
        rd = [in0.buf]
        a1 = s1
        a2 = s2
        if isinstance(s1, V):
            rd.append(s1.buf)
            a1 = s1.ap
        if isinstance(s2, V):
            rd.append(s2.buf)
            a2 = s2.ap
        if op1 is None:
            self.emit(e, lambda eng, o=out.ap, a=in0.ap, a1=a1, op0=op0: eng.tensor_scalar(o, a, a1, None, op0),
                      reads=rd, writes=[out.buf])
        else:
            self.emit(e, lambda eng, o=out.ap, a=in0.ap, a1=a1, a2=a2, op0=op0, op1=op1:
                      eng.tensor_scalar(o, a, a1, a2, op0, op1),
                      reads=rd, writes=[out.buf])

    def copy(self, out, in_, e="dve"):
        self.emit(e, lambda eng, o=out.ap, i=in_.ap: eng.tensor_copy(o, i), reads=[in_.buf], writes=[out.buf])

    def memset(self, out, val, e="dve"):
        self.emit(e, lambda eng, o=out.ap, v=val: eng.memset(o, v), reads=[], writes=[out.buf])

    def reduce(self, out, in_, op=None):
        self.emit("dve", lambda eng, o=out.ap, i=in_.ap: eng.tensor_reduce(o, i, mybir.AxisListType.X, ALU.add),
                  reads=[in_.buf], writes=[out.buf])

    def recip(self, out, in_):
        self.emit("dve", lambda eng, o=out.ap, i=in_.ap: eng.reciprocal(o, i), reads=[in_.buf], writes=[out.buf])

    def aselect(self, out, in_, pattern, cmp, fill, base, cm):
        self.emit("pool", lambda eng, o=out.ap, i=in_.ap: eng.affine_select(o, i, pattern, cmp, fill, base=base, channel_multiplier=cm),
                  reads=[in_.buf], writes=[out.buf])


class Cfg:
    def __init__(self, seq=4096, past=4096, stage=99, sbdbg=255):
        self.sbdbg = sbdbg
        self.seq = seq
        self.past = past
        self.stage = stage
        self.ntile = seq // TT


class Arena:
    def __init__(self, t, n4):
        self.t = t
        self.n4 = n4
        self.off = 0

    def reset(self, off=0):
        self.off = off

    def alloc(self, name, shape, dt=F32):
        n = 1
        for d in shape[1:]:
            n *= d
        n4 = n if dt == F32 else (n + 1) // 2
        n4 = (n4 + 1) // 2 * 2
        o = self.off
        self.off += n4
        assert self.off <= self.n4, ("arena overflow", name, self.off, self.n4)
        ap = self.t[0:shape[0], o:o + n4]
        if dt != F32:
            ap = ap.bitcast(dt)
        ap = ap[:, 0:n]
        if len(shape) == 3:
            ap = ap.rearrange("p (a b) -> p a b", a=shape[1])
        bf = Buf(ap, name)
        bf.arena = True
        return bf


def build(cfg):
    nc = bass.Bass("TRN2", target_bir_lowering=False)
    SEQ = cfg.seq
    st = ExitStack()
    with st:
        P = Prog(nc, st)

        def dram_in(name, shape, dt=F32):
            return Buf(nc.dram_tensor(name, list(shape), dt, kind="ExternalInput").ap(), name)

        def dram_out(name, shape, dt=F32):
            return Buf(nc.dram_tensor(name, list(shape), dt, kind="ExternalOutput").ap(), name)

        def sb(name, shape, dt=F32):
            return Buf(st.enter_context(nc.sbuf_tensor(name, list(shape), dt)), name)

        def ps(name, shape, dt=F32):
            return Buf(st.enter_context(nc.psum_tensor(name, list(shape), dt)), name, psum=True)

        xp = dram_in("xp", [D, SEQ])
        xs = dram_in("xs", [D, NS * DS])
        cT = dram_in("cT", [128, FC * NSEQ])
        normg = dram_in("normg", [128, 6 * FC])
        finalg = dram_in("finalg", [128, FC])
        ada_w = dram_in("ada_w", [2, D, 9 * D])
        ada_b = dram_in("ada_b", [2, 9 * D])
        ff_w_in = dram_in("ff_w_in", [2, 2, D, 2 * DFF])
        ff_w_out = dram_in("ff_w_out", [2, 2, DFF, D])
        ab_w_in = dram_in("ab_w_in", [D, 3600])
        ab_w_out = dram_in("ab_w_out", [D, D])
        convw_d = dram_in("convw", [128, 12 * 4])
        scw_d = dram_in("scw", [128, 4 * 3])
        alog_d = dram_in("alog", [64, 8])
        dtb_d = dram_in("dtb", [64, 8])
        dng_d = dram_in("dng", [64, 64])
        s_delta = dram_in("s_delta", [NS, 8, 64, 64])
        s_qkv = dram_in("s_qkv", [128, 12 * NS * 3])
        s_sconv = dram_in("s_sconv", [128, 4 * NS * 2])
        sb_w_qkv = dram_in("sb_w_qkv", [D, 3 * D])
        sb_w_out = dram_in("sb_w_out", [D, D])
        PAST = cfg.past
        ckT_d = dram_in("ckT", [NS, D, PAST])
        cv_d = dram_in("cv", [NS, PAST, D])
        kT_scr = Buf(nc.dram_tensor("kT_scr", [D, SEQ], BF16, kind="ExternalOutput").ap(), "kT_scr")
        v_scr = Buf(nc.dram_tensor("v_scr", [SEQ, D], BF16, kind="ExternalOutput").ap(), "v_scr")
        o_p_k = dram_out("o_p_k", [D, SEQ])
        o_p_v = dram_out("o_p_v", [SEQ, D])
        o_s_k = dram_out("o_s_k", [D, NS * DS])
        o_s_v = dram_out("o_s_v", [NS * DS, D])
        yp = dram_out("yp", [D, SEQ])
        ys = dram_out("ys", [D, NS * DS])
        o_p_delta = dram_out("o_p_delta", [8, 64, 64])
        o_s_delta = dram_out("o_s_delta", [NS, 8, 64, 64])
        o_p_qkv = dram_out("o_p_qkv", [128, 12 * 3])
        o_s_qkv = dram_out("o_s_qkv", [128, 12 * NS * 3])
        o_p_sconv = dram_out("o_p_sconv", [128, 4 * 2])
        o_s_sconv = dram_out("o_s_sconv", [128, 4 * NS * 2])

        ident = sb("ident", [128, 128])
        ones_bf = sb("ones_bf", [128, 128], BF16)
        onesf = sb("onesf", [128, 128])
        maskU = sb("maskU", [64, 64])
        maskL = sb("maskL", [64, 64])
        maskS = sb("maskS", [64, 64])
        condT = sb("condT", [128, FC * NSEQ])
        epsb = sb("epsb", [128, 1])
        g_sb = sb("g_sb", [128, 6 * FC])
        fg_sb = sb("fg_sb", [128, FC])
        modT = sb("modT", [128, 2 * 72 * NSEQ])
        gsT = sb("gsT", [128, 6 * FC * NSEQ])
        convw = sb("convw_s", [128, 12, 4])
        scw = sb("scw_s", [128, 4, 3])
        negA = sb("negA", [64, 8])
        dtb = sb("dtb_s", [64, 8])
        dng = sb("dng_s", [64, 64])
        ones512 = sb("ones512", [128, 512], BF16)
        negincl = sb("negincl", [128, 128], BF16)
        ones1b = sb("ones1b", [128, 128], BF16)
        mdiag = [sb("mdiag%d" % d, [128, 512], BF16) for d in range(4)]
        mnew = sb("mnew", [64, 512], BF16)
        hal3 = sb("hal3", [128, 12, 3])
        hal2 = sb("hal2", [128, 4, 2])
        S_sb = sb("S_sb", [64, 512])
        x = sb("x", [128, FC, TT])
        h = sb("h", [128, FC, TT], BF16)
        om = sb("om", [128, FC, TT], BF16)
        WPN = 2
        wp = [sb("wp%d" % i, [128, HC * 512], BF16) for i in range(WPN)]
        modrow = [sb("modrow%d" % i, [8, 512]) for i in range(2)]
        biasrow = [sb("biasrow%d" % i, [8, 512]) for i in range(2)]
        AR4 = 24576
        ar = Arena(st.enter_context(nc.sbuf_tensor("arena", [128, AR4], F32)), AR4)
        psb = [ps("ps%d" % i, [128, 512]) for i in range(8)]
        rr = {"ps": 0, "wp": 0, "mr": 0, "alt": 0, "psr": 0, "kTb": 0}

        def nxt(key, lst):
            i = rr[key]
            rr[key] = (i + 1) % len(lst)
            return lst[i]

        def drive(gens):
            gens = list(gens)
            while gens:
                for gg in list(gens):
                    try:
                        next(gg)
                    except StopIteration:
                        gens.remove(gg)

        def evac(out, in_):
            rr["alt"] ^= 1
            if rr["alt"]:
                P.act(out, in_, AF.Copy)
            else:
                P.copy(out, in_)

        P.memset(onesf[:, :], 1.0, e="pool")
        P.memset(epsb[:, :], EPS, e="pool")
        P.memset(ones_bf[:, :], 1.0 / D, e="pool")
        P.aselect(ident[:, :], onesf[:, :], [[-1, 128]], ALU.is_equal, 0.0, 0, 1)
        P.aselect(maskU[:, :], onesf[0:64, 0:64], [[1, 64]], ALU.is_ge, 0.0, 0, -1)
        P.aselect(maskL[:, :], onesf[0:64, 0:64], [[-1, 64]], ALU.is_gt, 0.0, 0, 1)
        P.aselect(maskS[:, :], onesf[0:64, 0:64], [[1, 64]], ALU.is_gt, 0.0, 0, -1)
        P.memset(ones512[:, :], 1.0, e="pool")
        P.memset(ones1b[:, :], 1.0, e="pool")
        P.memset(negincl[:, :], -1.0, e="pool")
        P.aselect(negincl[:, :], negincl[:, :], [[-1, 128]], ALU.is_ge, 0.0, 0, 1)
        for d in range(4):
            P.aselect(mdiag[d][:, :], ones512[:, :], [[1, 512]], ALU.is_gt, 0.0, -128 * d, -1)
        P.aselect(V(mnew, mnew.t[:, :].rearrange("p (a b) -> p a b", a=8)),
                  V(ones512, ones512.t[0:64, :].rearrange("p (a b) -> p a b", a=8)), [[0, 8], [1, 64]], ALU.is_gt, 0.0, 0, -1)
        P.dma("sp", condT[:, :], cT[:, :])
        P.dma("sp", g_sb[:, :], normg[:, :])
        P.dma("sp", fg_sb[:, :], finalg[:, :])
        P.dma("sp", V(convw, convw.t[:, :, :]), V(convw_d, convw_d.t[:, :].rearrange("p (a b) -> p a b", a=12)))
        P.dma("sp", V(scw, scw.t[:, :, :]), V(scw_d, scw_d.t[:, :].rearrange("p (a b) -> p a b", a=4)))
        P.dma("sp", negA[:, :], alog_d[:, :])
        P.dma("sp", dtb[:, :], dtb_d[:, :])
        P.dma("sp", dng[:, :], dng_d[:, :])
        P.act(condT[:, :], condT[:, :], AF.Silu)
        P.act(negA[:, :], negA[:, :], AF.Exp)
        P.ts(negA[:, :], negA[:, :], -1.0, None, ALU.mult)
        P.memset(V(hal3, hal3.t[:, :, :]), 0.0)
        P.memset(V(hal2, hal2.t[:, :, :]), 0.0)
        P.memset(S_sb[:, :], 0.0)

        def mod_idx(l, chunk):
            return (l * 72 + chunk) * NSEQ

        for l in range(2):
            for cb in range(18):
                wb = nxt("wp", wp)
                wv = wb.t[:, 0:8192].bitcast(F32)
                src = ada_w.t[l, :, cb * 512:(cb + 1) * 512].rearrange("(kc p) n -> p kc n", p=128)
                P.dma("sp", V(wb, wv.rearrange("p (kc n) -> p kc n", kc=FC)), V(ada_w, src))
                br = nxt("mr", biasrow)
                mr = modrow[biasrow.index(br)]
                P.dma("sp", br[0:NSEQ, :], V(ada_b, ada_b.t[l, cb * 512:(cb + 1) * 512].partition_broadcast(NSEQ)))
                pt = nxt("ps", psb)
                for kc in range(FC):
                    P.mm(pt[0:NSEQ, :], V(condT, condT.t[:, kc * NSEQ:(kc + 1) * NSEQ]),
                         V(wb, wv[:, kc * 512:(kc + 1) * 512]), start=(kc == 0), stop=(kc == FC - 1),
                         flag=(kc == FC - 1))
                P.tt(mr[0:NSEQ, :], pt[0:NSEQ, :], br[0:NSEQ, :], ALU.add)
                pt2 = nxt("ps", psb)
                for j in range(4):
                    P.mm(pt2[:, j * NSEQ:(j + 1) * NSEQ], mr[0:NSEQ, j * 128:(j + 1) * 128],
                         ident[0:NSEQ, 0:NSEQ], flag=(j == 3))
                c0 = mod_idx(l, cb * 4)
                P.copy(modT[:, c0:c0 + 4 * NSEQ], pt2[:, 0:4 * NSEQ])

        def mod_ap(l, sub, kind, fc, seq):
            c0 = mod_idx(l, (sub * 3 + kind) * 8 + fc) + seq
            return V(modT, modT.t[:, c0:c0 + 1])

        def gs_ap(l, sub, fc, seq):
            c0 = ((l * 3 + sub) * FC + fc) * NSEQ + seq
            return V(gsT, gsT.t[:, c0:c0 + 1])

        for l in range(2):
            for sub in range(3):
                c0 = mod_idx(l, (sub * 3 + 1) * 8)
                o0 = (l * 3 + sub) * FC * NSEQ
                gv = g_sb.t[:, (l * 3 + sub) * FC:(l * 3 + sub + 1) * FC].unsqueeze(2).broadcast_to([128, FC, NSEQ])
                P.stt(V(gsT, gsT.t[:, o0:o0 + FC * NSEQ].rearrange("p (f s) -> p f s", s=NSEQ)),
                      V(modT, modT.t[:, c0:c0 + FC * NSEQ].rearrange("p (f s) -> p f s", s=NSEQ)),
                      1.0, V(g_sb, gv), ALU.add, ALU.mult)
                if sub != 1:
                    c2 = mod_idx(l, (sub * 3 + 2) * 8)
                    P.ts(modT[:, c2:c2 + FC * NSEQ], modT[:, c2:c2 + FC * NSEQ], 0.5, None, ALU.mult)

        def mod_norm(T, segs, l, sub, out_f32=None):
            ar.reset()
            P.barrier()
            sq = ar.alloc("sq", [128, FC, TT], BF16)
            rstd = ar.alloc("rstd", [128, TT])
            tmpn = [ar.alloc("tmpn%d" % i, [128, TT]) for i in range(2)]
            for fc in range(FC):
                P.act(sq[:, fc, 0:T], x[:, fc, 0:T], AF.Square)
            pt = nxt("ps", psb)
            for fc in range(FC):
                P.mm(pt[:, 0:T], ones_bf[:, :], sq[:, fc, 0:T], start=(fc == 0), stop=(fc == FC - 1),
                     flag=(fc == FC - 1))
            P.act(rstd[:, 0:T], pt[:, 0:T], AF.Sqrt, bias=V(epsb, epsb.t[:, 0:1]), scale=1.0)
            P.recip(rstd[:, 0:T], rstd[:, 0:T])
            for fc in range(FC):
                tn = tmpn[fc % 2]
                P.tt(tn[:, 0:T], x[:, fc, 0:T], rstd[:, 0:T], ALU.mult)
                for (c0, n, s) in segs:
                    if l is None:
                        P.act(out_f32[:, fc, c0:c0 + n], tn[:, c0:c0 + n], AF.Identity, scale=V(fg_sb, fg_sb.t[:, fc:fc + 1]))
                    else:
                        P.act(h[:, fc, c0:c0 + n], tn[:, c0:c0 + n], AF.Identity,
                              bias=mod_ap(l, sub, 0, fc, s), scale=gs_ap(l, sub, fc, s))

        def load_w(dst_buf, dst_ap, src_buf, src_ap):
            P.dma("pool", V(dst_buf, dst_ap), V(src_buf, src_ap))

        def proj_fm(W2d_buf, W2d, col0, ncols, KC, rhs, T, consume):
            c = 0
            while c < ncols:
                w = min(512, ncols - c)
                wb = nxt("wp", wp)
                wv = wb.t[:, 0:KC * 512].rearrange("p (kc n) -> p kc n", kc=KC)
                load_w(wb, wv[:, :, 0:w], W2d_buf, W2d[:, col0 + c:col0 + c + w].rearrange("(kc p) n -> p kc n", p=128))
                for j in range(w // 128):
                    pt = nxt("ps", psb)
                    for kc in range(KC):
                        P.mm(pt[:, 0:T], V(wb, wv[:, kc, j * 128:(j + 1) * 128]), rhs[:, kc, 0:T],
                             start=(kc == 0), stop=(kc == KC - 1), flag=(kc == KC - 1))
                    consume(c // 128 + j, pt)
                c += w

        def ffn(T, segs, l, sub, fi):
            w_in = ff_w_in.t[l, fi]
            w_out = ff_w_out.t[l, fi]
            ar.reset()
            P.barrier()
            hid = ar.alloc("hid", [128, HC, TT], BF16)
            sg = [ar.alloc("sg%d" % i, [128, TT]) for i in range(2)]
            c0 = 0
            k = 0
            while c0 < DFF:
                w = min(512, DFF - c0)
                wb = nxt("wp", wp)
                gview = wb.t[:, 0:FC * 512].rearrange("p (kc n) -> p kc n", kc=FC)
                uview = wb.t[:, FC * 512:2 * FC * 512].rearrange("p (kc n) -> p kc n", kc=FC)
                load_w(wb, gview[:, :, 0:w], ff_w_in, w_in[:, c0:c0 + w].rearrange("(kc p) n -> p kc n", p=128))
                load_w(wb, uview[:, :, 0:w], ff_w_in, w_in[:, DFF + c0:DFF + c0 + w].rearrange("(kc p) n -> p kc n", p=128))
                for j in range(w // 128):
                    pg = nxt("ps", psb)
                    pu = nxt("ps", psb)
                    for kc in range(FC):
                        P.mm(pg[:, 0:T], V(wb, gview[:, kc, j * 128:(j + 1) * 128]), h[:, kc, 0:T],
                             start=(kc == 0), stop=(kc == FC - 1), flag=(kc == FC - 1))
                    for kc in range(FC):
                        P.mm(pu[:, 0:T], V(wb, uview[:, kc, j * 128:(j + 1) * 128]), h[:, kc, 0:T],
                             start=(kc == 0), stop=(kc == FC - 1), flag=(kc == FC - 1))
                    s_ = sg[k % 2]
                    k += 1
                    P.act(s_[:, 0:T], pg[:, 0:T], AF.Silu)
                    P.tt(hid[:, c0 // 128 + j, 0:T], s_[:, 0:T], pu[:, 0:T], ALU.mult)
                c0 += w
            for half in range(2):
                wb = nxt("wp", wp)
                wv = wb.t[:, 0:HC * 512].rearrange("p (kc n) -> p kc n", kc=HC)
                for k0 in range(0, HC, 11):
                    load_w(wb, wv[:, k0:k0 + 11, :], ff_w_out,
                           w_out[k0 * 128:(k0 + 11) * 128, half * 512:(half + 1) * 512].rearrange("(kc p) n -> p kc n", p=128))
                for m in range(4):
                    po = nxt("ps", psb)
                    for kc in range(HC):
                        P.mm(po[:, 0:T], V(wb, wv[:, kc, m * 128:(m + 1) * 128]), hid[:, kc, 0:T],
                             start=(kc == 0), stop=(kc == HC - 1), flag=(kc == HC - 1))
                    fc = half * 4 + m
                    for (c0, n, s) in segs:
                        P.stt(x[:, fc, c0:c0 + n], po[:, c0:c0 + n], mod_ap(l, sub, 2, fc, s), x[:, fc, c0:c0 + n],
                              ALU.mult, ALU.add)

        def resid_add(l, sub, segs):
            def consume(j, pt):
                for (c0, n, s) in segs:
                    P.stt(x[:, j, c0:c0 + n], pt[:, c0:c0 + n], mod_ap(l, sub, 2, j, s), x[:, j, c0:c0 + n],
                          ALU.mult, ALU.add)
            return consume

        def bc3(v_ap, n_mid, n_in, axis):
            if axis == 2:
                return v_ap.unsqueeze(2).broadcast_to([v_ap.shape[0], n_mid, n_in])
            return v_ap.unsqueeze(1).broadcast_to([v_ap.shape[0], n_mid, n_in])

        def v3(buf, lo, nh, w=64):
            return buf.t[:, lo:lo + nh * w].rearrange("p (a b) -> p a b", a=nh)

        def ab_mixer(kind, ti, T, segs, last):
            W = ab_w_in.t
            ar.reset()
            P.barrier()
            offs3, offs2 = [], []
            o3 = o2 = 0
            for (c0, n, s) in segs:
                offs3.append(o3)
                offs2.append(o2)
                o3 += 3 + n
                o2 += 2 + n
            L3, L2 = o3, o2
            qkv_pre = ar.alloc("qkv_pre", [128, 12, L3])
            mark = ar.off
            sBs = ar.alloc("sBs", [128, 4, TT])
            sCs = ar.alloc("sCs", [128, 4, TT])
            scx = ar.alloc("scx", [128, 4, L2])
            acc = [ar.alloc("acc%d" % i, [128, L2]) for i in range(2)]

            if kind == "p":
                P.copy(V(qkv_pre, qkv_pre.t[:, :, 0:3]), V(hal3, hal3.t[:, :, :]))
                P.copy(V(scx, scx.t[:, :, 0:2]), V(hal2, hal2.t[:, :, :]))
            else:
                for si, (c0, n, s) in enumerate(segs):
                    q = s - 1
                    P.dma("sp", V(qkv_pre, qkv_pre.t[:, :, offs3[si]:offs3[si] + 3]),
                          V(s_qkv, s_qkv.t[:, :].rearrange("p (a q k) -> p a q k", a=12, q=NS)[:, :, q, :]))
                    P.dma("sp", V(scx, scx.t[:, :, offs2[si]:offs2[si] + 2]),
                          V(s_sconv, s_sconv.t[:, :].rearrange("p (a q k) -> p a q k", a=4, q=NS)[:, :, q, :]))

            def c_qkv(j, pt):
                for si, (c0, n, s) in enumerate(segs):
                    evac(V(qkv_pre, qkv_pre.t[:, j, offs3[si] + 3:offs3[si] + 3 + n]), pt[:, c0:c0 + n])
            proj_fm(ab_w_in, W, 0, 1536, FC, h, T, c_qkv)

            def c_sB(j, pt):
                evac(sBs[:, j, 0:T], pt[:, 0:T])

            def c_sC(j, pt):
                evac(sCs[:, j, 0:T], pt[:, 0:T])

            def c_sx(j, pt):
                for si, (c0, n, s) in enumerate(segs):
                    P.tt(V(scx, scx.t[:, j, offs2[si] + 2:offs2[si] + 2 + n]), sCs[:, j, c0:c0 + n], pt[:, c0:c0 + n], ALU.mult)
            proj_fm(ab_w_in, W, 2064, 512, FC, h, T, c_sB)
            proj_fm(ab_w_in, W, 2576, 512, FC, h, T, c_sC)
            proj_fm(ab_w_in, W, 3088, 512, FC, h, T, c_sx)
            for j in range(4):
                a_ = acc[j % 2]
                P.ts(a_[:, 0:L2 - 2], V(scx, scx.t[:, j, 0:L2 - 2]), V(scw, scw.t[:, j, 0:1]), None, ALU.mult)
                P.stt(a_[:, 0:L2 - 2], V(scx, scx.t[:, j, 1:L2 - 1]), V(scw, scw.t[:, j, 1:2]), a_[:, 0:L2 - 2], ALU.mult, ALU.add)
                P.stt(a_[:, 0:L2 - 2], V(scx, scx.t[:, j, 2:L2]), V(scw, scw.t[:, j, 2:3]), a_[:, 0:L2 - 2], ALU.mult, ALU.add)
                for si, (c0, n, s) in enumerate(segs):
                    P.tt(om[:, 4 + j, c0:c0 + n], sBs[:, j, c0:c0 + n], a_[:, offs2[si]:offs2[si] + n], ALU.mult)
            if kind == "p":
                P.copy(V(hal2, hal2.t[:, :, :]), V(scx, scx.t[:, :, T:T + 2]))
                if last:
                    P.dma("sp", V(o_p_sconv, o_p_sconv.t[:, :].rearrange("p (a k) -> p a k", a=4)), V(hal2, hal2.t[:, :, :]))
            else:
                for si, (c0, n, s) in enumerate(segs):
                    q = s - 1
                    P.dma("sp", V(o_s_sconv, o_s_sconv.t[:, :].rearrange("p (a q k) -> p a q k", a=4, q=NS)[:, :, q, :]),
                          V(scx, scx.t[:, :, offs2[si] + n:offs2[si] + n + 2]))

            P.barrier()
            ar.reset(mark)
            f = lambda nm, shp: ar.alloc(nm, shp)
            cacc = f("cacc", [128, 12, 64])
            ctmp = f("ctmp", [128, 12, 64])
            qkvc = f("qkvc", [128, 12, 64])
            Qr, Kr, Vr, Kb, Qg, RHSk, Kdec, zs, sqt, O = [f(nm, [64, 512]) for nm in
                                                           ("Qr", "Kr", "Vr", "Kb", "Qg", "RHSk", "Kdec", "zs", "sqt", "O")]
            sm = {nm: f(nm, [64, 16]) for nm in ("ssq", "rn", "ab", "t8", "g8", "b8", "beta", "Gs", "eGG", "dG", "eGe", "ss8")}
            G = {}
            for g in range(2):
                for nm in ("kT", "kbT", "qT", "qgT", "rhsE", "E", "EmS", "EmI", "Xa", "Xb", "XTa", "XTb", "Pa", "Pb",
                           "QKD", "solv", "nsolkT", "U", "Stmp"):
                    G[(nm, g)] = f(nm + str(g), [64, 256])
            zwb = nxt("wp", wp)
            zw = zwb.t[:, 0:FC * 528].rearrange("p (kc n) -> p kc n", kc=FC)
            load_w(zwb, zw[:, :, :], ab_w_in, W[:, 1536:2064].rearrange("(kc p) n -> p kc n", p=128))
            I64 = V(ident, ident.t[0:64, 0:64])
            one64 = V(onesf, onesf.t[0:64, 0:1])
            eps64 = V(epsb, epsb.t[0:64, 0:1])
            MUL, ADD, SUB = ALU.mult, ALU.add, ALU.subtract

            def chunk(cc, pc):
                def pre(k):
                    return V(qkv_pre, qkv_pre.t[:, :, pc + k:pc + k + 64])

                def wv(k):
                    return V(convw, convw.t[:, :, k:k + 1].broadcast_to([128, 12, 64]))
                A3 = lambda b: V(b, b.t[:, :, :])
                P.tt(A3(cacc), pre(0), wv(0), MUL)
                for k in range(1, 4):
                    P.tt(A3(ctmp), pre(k), wv(k), MUL)
                    P.tt(A3(cacc), A3(cacc), A3(ctmp), ADD)
                P.act(A3(qkvc), A3(cacc), AF.Silu)
                for dst, base in ((Qr, 0), (Kr, 4), (Vr, 8)):
                    pt = nxt("ps", psb)
                    for j in range(4):
                        P.mm(pt[0:64, j * 128:(j + 1) * 128], V(qkvc, qkvc.t[:, base + j, :]), ident[:, :], flag=(j == 3))
                    evac(dst[:, :], pt[0:64, :])
                pz = nxt("ps", psb)
                pab = nxt("ps", psb)
                for kc in range(FC):
                    P.mm(pz[0:64, :], h[:, kc, cc:cc + 64], V(zwb, zw[:, kc, 0:512]), start=(kc == 0), stop=(kc == FC - 1),
                         flag=(kc == FC - 1))
                for kc in range(FC):
                    P.mm(pab[0:64, 0:16], h[:, kc, cc:cc + 64], V(zwb, zw[:, kc, 512:528]), start=(kc == 0),
                         stop=(kc == FC - 1), flag=(kc == FC - 1))
                P.act(zs[:, :], pz[0:64, :], AF.Silu)
                P.copy(sm["ab"][:, :], pab[0:64, 0:16])
                P.tt(sqt[:, :], Qr[:, :], Qr[:, :], MUL)
                P.reduce(sm["ssq"][:, 0:8], V(sqt, v3(sqt, 0, 8)))
                P.tt(sqt[:, :], Kr[:, :], Kr[:, :], MUL)
                P.reduce(sm["ssq"][:, 8:16], V(sqt, v3(sqt, 0, 8)))
                P.act(sm["rn"][:, :], sm["ssq"][:, :], AF.Sqrt, bias=eps64, scale=1.0)
                P.recip(sm["rn"][:, :], sm["rn"][:, :])
                P.ts(sm["rn"][:, 0:8], sm["rn"][:, 0:8], 0.125, None, MUL)
                P.tt(V(Qr, v3(Qr, 0, 8)), V(Qr, v3(Qr, 0, 8)), V(sm["rn"], bc3(sm["rn"].t[:, 0:8], 8, 64, 2)), MUL)
                P.tt(V(Kr, v3(Kr, 0, 8)), V(Kr, v3(Kr, 0, 8)), V(sm["rn"], bc3(sm["rn"].t[:, 8:16], 8, 64, 2)), MUL)
                P.tt(sm["t8"][:, 0:8], sm["ab"][:, 0:8], dtb[:, :], ADD)
                P.act(sm["t8"][:, 0:8], sm["t8"][:, 0:8], AF.Exp)
                P.act(sm["t8"][:, 0:8], sm["t8"][:, 0:8], AF.Ln, bias=one64, scale=1.0)
                P.tt(sm["g8"][:, 0:8], sm["t8"][:, 0:8], negA[:, :], MUL)
                P.act(sm["b8"][:, 0:8], sm["ab"][:, 8:16], AF.Exp, scale=-1.0)
                P.ts(sm["b8"][:, 0:8], sm["b8"][:, 0:8], 1.0, None, ADD)
                P.recip(sm["beta"][:, 0:8], sm["b8"][:, 0:8])
                pg = nxt("ps", psb)
                P.mm(pg[0:64, 0:8], maskU[:, :], sm["g8"][:, 0:8], flag=False)
                P.mm(pg[0:64, 8:16], onesf[0:64, 0:64], sm["g8"][:, 0:8])
                P.copy(sm["Gs"][:, :], pg[0:64, 0:16])
                P.act(sm["eGG"][:, :], sm["Gs"][:, :], AF.Exp)
                P.tt(sm["dG"][:, 0:8], sm["Gs"][:, 8:16], sm["Gs"][:, 0:8], SUB)
                P.act(sm["eGe"][:, 0:8], sm["dG"][:, 0:8], AF.Exp)
                beta_b = V(sm["beta"], bc3(sm["beta"].t[:, 0:8], 8, 64, 2))
                eG_b = V(sm["eGG"], bc3(sm["eGG"].t[:, 0:8], 8, 64, 2))
                eGe_b = V(sm["eGe"], bc3(sm["eGe"].t[:, 0:8], 8, 64, 2))
                P.tt(V(Kb, v3(Kb, 0, 8)), V(Kr, v3(Kr, 0, 8)), beta_b, MUL)
                P.tt(V(Qg, v3(Qg, 0, 8)), V(Qr, v3(Qr, 0, 8)), eG_b, MUL)
                P.tt(V(RHSk, v3(RHSk, 0, 8)), V(Kb, v3(Kb, 0, 8)), eG_b, MUL)
                P.tt(V(Vr, v3(Vr, 0, 8)), V(Vr, v3(Vr, 0, 8)), beta_b, MUL)
                P.tt(V(Kdec, v3(Kdec, 0, 8)), V(Kr, v3(Kr, 0, 8)), eGe_b, MUL)
                def pre_g(g):
                    T_ = lambda nm: G[(nm, g)]
                    hc = lambda i: slice((g * 4 + i) * 64, (g * 4 + i + 1) * 64)
                    lc = lambda i: slice(i * 64, (i + 1) * 64)
                    for src, dn in ((Kr, "kT"), (Kb, "kbT"), (Qr, "qT"), (Qg, "qgT")):
                        yield
                        pt = nxt("ps", psb)
                        for i in range(4):
                            P.mm(pt[0:64, lc(i)], src[:, hc(i)], I64, flag=(i == 3))
                        evac(T_(dn)[:, :], pt[0:64, 0:256])
                    P.tt(V(T_("rhsE"), v3(T_("rhsE"), 0, 4)), V(maskU, bc3(maskU.t[:, :], 4, 64, 1)),
                         V(sm["g8"], bc3(sm["g8"].t[:, g * 4:g * 4 + 4], 4, 64, 2)), MUL)
                    yield
                    pe_ = nxt("ps", psb)
                    P.mm(pe_[0:64, 0:256], maskL[:, :], T_("rhsE")[:, :])
                    P.act(T_("E")[:, :], pe_[0:64, 0:256], AF.Exp)
                    P.tt(V(T_("EmS"), v3(T_("EmS"), 0, 4)), V(T_("E"), v3(T_("E"), 0, 4)), V(maskS, bc3(maskS.t[:, :], 4, 64, 1)), MUL)
                    P.tt(V(T_("EmI"), v3(T_("EmI"), 0, 4)), V(T_("E"), v3(T_("E"), 0, 4)), V(maskU, bc3(maskU.t[:, :], 4, 64, 1)), MUL)
                    yield
                    pA = nxt("ps", psb)
                    yield
                    pQ = nxt("ps", psb)
                    for i in range(4):
                        P.mm(pA[0:64, lc(i)], T_("kT")[:, lc(i)], T_("kbT")[:, lc(i)], flag=(i == 3))
                    for i in range(4):
                        P.mm(pQ[0:64, lc(i)], T_("kT")[:, lc(i)], T_("qT")[:, lc(i)], flag=(i == 3))
                    P.tt(T_("Xa")[:, :], pA[0:64, 0:256], T_("EmS")[:, :], MUL)
                    P.tt(T_("QKD")[:, :], pQ[0:64, 0:256], T_("EmI")[:, :], MUL)
                    yield
                    pX = nxt("ps", psb)
                    for i in range(4):
                        P.mm(pX[0:64, lc(i)], T_("Xa")[:, lc(i)], I64, flag=(i == 3))
                    evac(T_("XTa")[:, :], pX[0:64, 0:256])
                    P.tt(V(T_("Pa"), v3(T_("Pa"), 0, 4)), V(ident, bc3(ident.t[0:64, 0:64], 4, 64, 1)),
                         V(T_("Xa"), v3(T_("Xa"), 0, 4)), SUB)
                    Xc, XTc, Pc = "Xa", "XTa", "Pa"
                    for lvl in range(1, 6):
                        Xn = "Xb" if Xc == "Xa" else "Xa"
                        XTn = "XTb" if XTc == "XTa" else "XTa"
                        Pn = "Pb" if Pc == "Pa" else "Pa"
                        if lvl < 5:
                            yield
                            p1 = nxt("ps", psb)
                            for i in range(4):
                                P.mm(p1[0:64, lc(i)], T_(XTc)[:, lc(i)], T_(Xc)[:, lc(i)], flag=(i == 3))
                        yield
                        p2 = nxt("ps", psb)
                        for i in range(4):
                            P.mm(p2[0:64, lc(i)], T_(Xc)[:, lc(i)], T_(XTc)[:, lc(i)], flag=(i == 3))
                        if lvl < 5:
                            evac(T_(Xn)[:, :], p1[0:64, 0:256])
                        evac(T_(XTn)[:, :], p2[0:64, 0:256])
                        yield
                        p3 = nxt("ps", psb)
                        for i in range(4):
                            P.mm(p3[0:64, lc(i)], T_(XTn)[:, lc(i)], T_(Pc)[:, lc(i)], flag=(i == 3))
                        P.tt(T_(Pn)[:, :], T_(Pc)[:, :], p3[0:64, 0:256], ADD)
                        Xc, XTc, Pc = Xn, XTn, Pn
                    TTn = Pc
                    yield
                    pSv = nxt("ps", psb)
                    for i in range(4):
                        P.mm(pSv[0:64, lc(i)], T_(TTn)[:, lc(i)], Vr[:, hc(i)], flag=(i == 3))
                    evac(T_("solv")[:, :], pSv[0:64, 0:256])
                    yield
                    pSk = nxt("ps", psb)
                    for i in range(4):
                        P.mm(pSk[0:64, lc(i)], RHSk[:, hc(i)], T_(TTn)[:, lc(i)], flag=(i == 3))
                    P.ts(T_("nsolkT")[:, :], pSk[0:64, 0:256], -1.0, None, MUL)
                drive([pre_g(0), pre_g(1)])
                def rec_g(g):
                    T_ = lambda nm: G[(nm, g)]
                    hc = lambda i: slice((g * 4 + i) * 64, (g * 4 + i + 1) * 64)
                    lc = lambda i: slice(i * 64, (i + 1) * 64)
                    gs = slice(g * 256, (g + 1) * 256)
                    yield
                    pU = nxt("ps", psb)
                    for i in range(4):
                        P.mm(pU[0:64, lc(i)], T_("nsolkT")[:, lc(i)], S_sb[:, hc(i)], flag=(i == 3))
                    P.tt(T_("U")[:, :], T_("solv")[:, :], pU[0:64, 0:256], ADD)
                    yield
                    pO = nxt("ps", psb)
                    for i in range(4):
                        P.mm(pO[0:64, lc(i)], T_("qgT")[:, lc(i)], S_sb[:, hc(i)], start=True, stop=False, flag=False)
                        P.mm(pO[0:64, lc(i)], T_("QKD")[:, lc(i)], T_("U")[:, lc(i)], start=False, stop=True, flag=(i == 3))
                    evac(O[:, gs], pO[0:64, 0:256])
                    P.tt(V(T_("Stmp"), v3(T_("Stmp"), 0, 4)), V(S_sb, v3(S_sb, g * 256, 4)),
                         V(sm["eGG"], bc3(sm["eGG"].t[:, 8 + g * 4:8 + g * 4 + 4], 4, 64, 2)), MUL)
                    yield
                    pS = nxt("ps", psb)
                    for i in range(4):
                        P.mm(pS[0:64, lc(i)], Kdec[:, hc(i)], T_("U")[:, lc(i)], flag=(i == 3))
                    P.tt(S_sb[:, gs], T_("Stmp")[:, :], pS[0:64, 0:256], ADD)
                drive([rec_g(0), rec_g(1)])
                P.tt(sqt[:, :], O[:, :], O[:, :], MUL)
                P.reduce(sm["ss8"][:, 0:8], V(sqt, v3(sqt, 0, 8)))
                P.act(sm["ss8"][:, 0:8], sm["ss8"][:, 0:8], AF.Sqrt, bias=eps64, scale=1.0 / 64)
                P.recip(sm["ss8"][:, 0:8], sm["ss8"][:, 0:8])
                P.tt(V(O, v3(O, 0, 8)), V(O, v3(O, 0, 8)), V(sm["ss8"], bc3(sm["ss8"].t[:, 0:8], 8, 64, 2)), MUL)
                P.tt(V(O, v3(O, 0, 8)), V(O, v3(O, 0, 8)), V(dng, bc3(dng.t[:, :], 8, 64, 1)), MUL)
                P.tt(O[:, :], O[:, :], zs[:, :], MUL)
                pT = nxt("ps", psb)
                for c in range(4):
                    P.mm(pT[:, c * 64:(c + 1) * 64], O[:, c * 128:(c + 1) * 128], I64, flag=(c == 3))
                evac(V(om, om.t[:, 0:4, cc:cc + 64]), V(pT, pT.t[:, 0:256].rearrange("p (a b) -> p a b", a=4)))

            for si, (c0, n, s) in enumerate(segs):
                if kind == "s":
                    P.dma("sp", V(S_sb, v3(S_sb, 0, 8)), V(s_delta, s_delta.t[s - 1].rearrange("h k v -> k h v")))
                for k in range(n // 64):
                    chunk(c0 + 64 * k, offs3[si] + 64 * k)
                if kind == "s":
                    P.dma("sp", V(o_s_delta, o_s_delta.t[s - 1].rearrange("h k v -> k h v")), V(S_sb, v3(S_sb, 0, 8)))
                    P.dma("sp", V(o_s_qkv, o_s_qkv.t[:, :].rearrange("p (a q k) -> p a q k", a=12, q=NS)[:, :, s - 1, :]),
                          V(qkv_pre, qkv_pre.t[:, :, offs3[si] + n:offs3[si] + n + 3]))
            if kind == "p":
                P.copy(V(hal3, hal3.t[:, :, :]), V(qkv_pre, qkv_pre.t[:, :, T:T + 3]))
                if last:
                    P.dma("sp", V(o_p_qkv, o_p_qkv.t[:, :].rearrange("p (a k) -> p a k", a=12)), V(hal3, hal3.t[:, :, :]))
                    P.dma("sp", V(o_p_delta, o_p_delta.t[:, :, :].rearrange("h k v -> k h v")), V(S_sb, v3(S_sb, 0, 8)))
            proj_fm(ab_w_out, ab_w_out.t, 0, D, FC, om, T, resid_add(0, 1, segs))

        def sb_mixer(kind, ti, T, segs):
            Wq = sb_w_qkv.t
            ar.reset()
            P.barrier()
            DB = cfg.sbdbg
            qT = ar.alloc("qT", [128, FC, TT], BF16)
            kbf = ar.alloc("kbf", [128, FC, TT], BF16)
            vbf = ar.alloc("vbf", [128, 4, D], BF16)
            mark = ar.off
            kf = ar.alloc("kf", [128, FC, TT])
            vf = ar.alloc("vf", [128, 4, D])
            vblk = 128 if kind == "p" else 64
            nvb = T // vblk

            def c_q(j, pt):
                P.act(qT[:, j, 0:T], pt[:, 0:T], AF.Copy, scale=0.125)

            def c_k(j, pt):
                P.act(kf[:, j, 0:T], pt[:, 0:T], AF.Copy)
                P.copy(kbf[:, j, 0:T], pt[:, 0:T])
            if DB & 1:
                proj_fm(sb_w_qkv, Wq, 0, D, FC, h, T, c_q)
            if DB & 2:
                proj_fm(sb_w_qkv, Wq, D, D, FC, h, T, c_k)
            for half in range(2 if DB & 4 else 0):
                wb = nxt("wp", wp)
                wv = wb.t[:, 0:FC * 512].rearrange("p (kc n) -> p kc n", kc=FC)
                load_w(wb, wv[:, :, :], sb_w_qkv, Wq[:, 2 * D + half * 512:2 * D + (half + 1) * 512].rearrange("(kc p) n -> p kc n", p=128))
                for b in range(nvb):
                    pt = nxt("ps", psb)
                    for kc in range(FC):
                        P.mm(pt[0:vblk, :], h[:, kc, b * vblk:(b + 1) * vblk], V(wb, wv[:, kc, :]),
                             start=(kc == 0), stop=(kc == FC - 1), flag=(kc == FC - 1))
                    P.act(vf[0:vblk, b, half * 512:(half + 1) * 512], pt[0:vblk, :], AF.Copy)
                    P.copy(vbf[0:vblk, b, half * 512:(half + 1) * 512], pt[0:vblk, :])
            t0 = ti * TT
            if not (DB & 8):
                pass
            elif kind == "p":
                P.dma("sp", V(o_p_k, o_p_k.t[:, t0:t0 + T].rearrange("(fc p) t -> p fc t", p=128)), kf[:, :, 0:T])
                P.dma("sp", V(o_p_v, o_p_v.t[t0:t0 + T, :].rearrange("(b p) c -> p b c", p=128)), vf[:, 0:nvb, :])
                if DB & 16:
                    P.dma("sp", V(kT_scr, kT_scr.t[:, t0:t0 + T].rearrange("(fc p) t -> p fc t", p=128)), kbf[:, :, 0:T])
                    P.dma("sp", V(v_scr, v_scr.t[t0:t0 + T, :].rearrange("(b p) c -> p b c", p=128)), vbf[:, 0:nvb, :])
            else:
                P.dma("sp", V(o_s_k, o_s_k.t[:, 0:T].rearrange("(fc p) t -> p fc t", p=128)), kf[:, :, 0:T])
                P.dma("sp", V(o_s_v, o_s_v.t[0:T, :].rearrange("(b p) c -> p b c", p=64)), vf[0:64, 0:nvb, :])

            if not (DB & 96):
                return
            P.barrier()
            ar.reset(mark)
            NPB = 3
            E_ = [ar.alloc("E%d" % i, [128, 512]) for i in range(NPB)]
            SP_ = [ar.alloc("SP%d" % i, [128, 512], BF16) for i in range(NPB)]
            ARG_ = [ar.alloc("ARG%d" % i, [128, 512]) for i in range(NPB)]
            W_ = [ar.alloc("W%d" % i, [128, 512], BF16) for i in range(NPB)]
            R_ = [ar.alloc("R%d" % i, [128, 512]) for i in range(2)]
            kTb = [ar.alloc("kTb%d" % i, [128, 4096], BF16) for i in range(2)]
            vb = [ar.alloc("vb%d" % i, [128, 4, 512], BF16) for i in range(2)]
            if kind == "s":
                qS = ar.alloc("qS", [64, 16, NS * DS], BF16)
                kS = ar.alloc("kS", [64, 16, NS * DS], BF16)
                for dstb, srcb in ((qS, qT), (kS, kbf)):
                    dv = dstb.t[:, :, :].rearrange("p (c two) t -> p c two t", two=2)
                    P.dma("sp", V(dstb, dv[:, :, 0, :]), V(srcb, srcb.t[0:64, :, 0:T]))
                    P.dma("sp", V(dstb, dv[:, :, 1, :]), V(srcb, srcb.t[64:128, :, 0:T]))
            psr = psb[0:6]
            one_col = lambda kk: V(onesf, onesf.t[0:kk, 0:1])

            def run_stream(blocks):
                nb = len(blocks)
                st_ = {}

                def stA(b):
                    B = blocks[b]
                    kk, GW = B["kk"], B["GW"]
                    pz = nxt("psr", psr)
                    st_[b] = pz
                    hl = B["heads"]
                    for i, (c0, N, kTv, qv, vv, pov, ost) in enumerate(hl):
                        P.mm(pz[0:kk, c0:c0 + N], kTv, qv, start=(i == 0), stop=False, flag=(i == len(hl) - 1), skip=True)
                    E, SP = E_[b % NPB], SP_[b % NPB]
                    P.act(E[0:kk, 0:GW], pz[0:kk, 0:GW], AF.Exp)
                    P.act(SP[0:kk, 0:GW], E[0:kk, 0:GW], AF.Ln, bias=one_col(kk), scale=1.0)
                    if B["mask"] is not None:
                        P.tt(SP[0:kk, 0:GW], SP[0:kk, 0:GW], B["mask"], ALU.mult)

                def stB(b):
                    B = blocks[b]
                    kk, GW = B["kk"], B["GW"]
                    pz = st_[b]
                    SP, ARG, W, R = SP_[b % NPB], ARG_[b % NPB], W_[b % NPB], B["R"]
                    P.mm(pz[0:kk, 0:GW], negincl[0:kk, 0:kk], SP[0:kk, 0:GW], start=False, stop=True, skip=True)
                    pt = None
                    if not B["last"]:
                        pt = nxt("psr", psr)
                        P.mm(pt[:, 0:GW], ones1b[0:kk, :], SP[0:kk, 0:GW])
                    if B["first"]:
                        P.act(W[0:kk, 0:GW], pz[0:kk, 0:GW], AF.Exp)
                    else:
                        P.tt(ARG[0:kk, 0:GW], pz[0:kk, 0:GW], R[0:kk, 0:GW], ALU.subtract)
                        P.act(W[0:kk, 0:GW], ARG[0:kk, 0:GW], AF.Exp)
                    if B["mask"] is not None:
                        P.tt(W[0:kk, 0:GW], W[0:kk, 0:GW], B["mask"], ALU.mult)
                    if pt is not None:
                        if B["first"]:
                            P.copy(R[:, 0:GW], pt[:, 0:GW])
                        else:
                            P.tt(R[:, 0:GW], R[:, 0:GW], pt[:, 0:GW], ALU.add)

                def stC(b):
                    B = blocks[b]
                    kk = B["kk"]
                    W = W_[b % NPB]
                    hl = B["heads"]
                    for i, (c0, N, kTv, qv, vv, pov, ost) in enumerate(hl):
                        P.mm(pov, vv, W[0:kk, c0:c0 + N], start=(B["first"] and ost), stop=B["last"], flag=(i == len(hl) - 1), skip=True)

                for step in range(nb + 4):
                    if step < nb:
                        stA(step)
                    if 0 <= step - 2 < nb:
                        stB(step - 2)
                    if 0 <= step - 4 < nb:
                        stC(step - 4)
                        if "post" in blocks[step - 4]:
                            blocks[step - 4]["post"]()

            if kind == "p" and (DB & 32):
                i = ti
                for c in range(8):
                    po = psb[6 + c % 2]
                    blocks = []
                    last_of = []
                    for j, kb in enumerate(range(i, -1, -1)):
                        kt = kTb[j % 2]
                        vt = vb[j % 2]

                        def load(kt=kt, vt=vt, kb=kb):
                            P.dma("sp", V(kt, kt.t[:, 0:512]), V(kT_scr, kT_scr.t[c * 128:(c + 1) * 128, kb * 512:(kb + 1) * 512]))
                            P.dma("sp", V(vt, vt.t[:, :, 0:128]),
                                  V(v_scr, v_scr.t[kb * 512:(kb + 1) * 512, c * 128:(c + 1) * 128].rearrange("(j p) c -> p j c", p=128)))
                        if j < 2:
                            load()
                        else:
                            last_of[j - 2]["post"] = load
                        for d in range(3, -1, -1):
                            for hh in range(2):
                                pb = hh * 64
                                blocks.append(dict(
                                    kk=128, GW=512,
                                    heads=[(0, 512, V(kt, kt.t[pb:pb + 64, d * 128:(d + 1) * 128]),
                                            V(qT, qT.t[pb:pb + 64, c, 0:512]),
                                            V(vt, vt.t[:, d, hh * 64:(hh + 1) * 64]),
                                            V(po, po.t[pb:pb + 64, 0:512]), True)],
                                    mask=(mdiag[d][:, :] if kb == i else None),
                                    first=(kb == i and d == 3), last=(kb == 0 and d == 0), R=R_[hh]))
                        last_of.append(blocks[-1])
                    run_stream(blocks)
                    evac(om[:, c, 0:T], po[:, 0:T])
            if kind == "s" and (DB & 64):
                NKB = PAST // 512
                for si, (c0, n, s) in enumerate(segs):
                    q = s - 1
                    for g in range(2):
                        po = psb[6 + (si * 2 + g) % 2]
                        blocks = []

                        def heads_for(kk, kT_of, v_of):
                            hl = []
                            for gi in range(8):
                                hd = g * 8 + gi
                                pb = (hd % 2) * 64
                                hl.append((gi * 64, 64, kT_of(hd, pb), V(qS, qS.t[0:64, hd, c0:c0 + 64]),
                                           v_of(hd), V(po, po.t[pb:pb + 64, (gi // 2) * 64:(gi // 2 + 1) * 64]), gi < 2))
                            return hl
                        blocks.append(dict(
                            kk=64, GW=512,
                            heads=heads_for(64, lambda hd, pb: V(kS, kS.t[0:64, hd, c0:c0 + 64]),
                                            lambda hd: V(vbf, vbf.t[0:64, si, hd * 64:(hd + 1) * 64])),
                            mask=mnew[:, :], first=True, last=(NKB == 0), R=R_[0]))
                        last_of = []
                        for j, kb in enumerate(range(NKB - 1, -1, -1)):
                            kt = kTb[j % 2]
                            vt = vb[j % 2]
                            ktv = kt.t[0:64, :].rearrange("p (hh k) -> p hh k", hh=8)

                            def load(kt=kt, vt=vt, ktv=ktv, kb=kb):
                                P.dma("pool", V(kt, ktv),
                                      V(ckT_d, ckT_d.t[q, g * 512:(g + 1) * 512, kb * 512:(kb + 1) * 512].rearrange("(hh dd) k -> dd hh k", dd=64)))
                                P.dma("pool", V(vt, vt.t[:, :, :]),
                                      V(cv_d, cv_d.t[q, kb * 512:(kb + 1) * 512, g * 512:(g + 1) * 512].rearrange("(j p) c -> p j c", p=128)))
                            if j < 2:
                                load()
                            else:
                                last_of[j - 2]["post"] = load
                            for d in range(3, -1, -1):
                                blocks.append(dict(
                                    kk=128, GW=512,
                                    heads=heads_for(128, lambda hd, pb, kt=kt, ktv=ktv, d=d: V(kt, ktv[:, hd - 8 * g, d * 128:(d + 1) * 128]),
                                                    lambda hd, vt=vt, d=d: V(vt, vt.t[:, d, (hd - 8 * g) * 64:(hd - 8 * g + 1) * 64])),
                                    mask=None, first=False, last=(kb == 0 and d == 0), R=R_[0]))
                            last_of.append(blocks[-1])
                        run_stream(blocks)
                        evac(V(om, om.t[:, 4 * g:4 * g + 4, c0:c0 + 64]), V(po, po.t[:, 0:256].rearrange("p (a b) -> p a b", a=4)))
            proj_fm(sb_w_out, sb_w_out.t, 0, D, FC, om, T, resid_add(1, 1, segs))

        tiles = []
        for i in range(cfg.ntile):
            tiles.append(("p", i, TT, [(0, TT, 0)], i == cfg.ntile - 1))
        tiles.append(("s", 0, NS * DS, [(j * DS, DS, 1 + j) for j in range(NS)], True))

        for (kind, ti, T, segs, last) in tiles:
            src = xp if kind == "p" else xs
            dst = yp if kind == "p" else ys
            t0 = ti * TT
            P.barrier()
            P.dma("sp", x[:, :, 0:T], V(src, src.t[:, t0:t0 + T].rearrange("(fc p) t -> p fc t", p=128)))
            for l in range(2):
                mod_norm(T, segs, l, 0)
                ffn(T, segs, l, 0, 0)
                if cfg.stage <= 1:
                    break
                mod_norm(T, segs, l, 1)
                if l == 0:
                    ab_mixer(kind, ti, T, segs, last)
                    if cfg.stage <= 2:
                        break
                else:
                    sb_mixer(kind, ti, T, segs)
                    if cfg.stage <= 3:
                        break
                mod_norm(T, segs, l, 2)
                ffn(T, segs, l, 2, 1)
            if cfg.stage >= 99:
                ar.reset()
                yo = Buf(ar.t[:, AR4 - FC * TT:AR4].rearrange("p (a b) -> p a b", a=FC), "yo")
                mod_norm(T, segs, None, None, out_f32=yo)
                P.dma("sp", V(dst, dst.t[:, t0:t0 + T].rearrange("(fc p) t -> p fc t", p=128)), yo[:, :, 0:T])
            else:
                P.dma("sp", V(dst, dst.t[:, t0:t0 + T].rearrange("(fc p) t -> p fc t", p=128)), x[:, :, 0:T])

        P.finish()
        P.run()
        print("instructions:", P.nins, {e: len(P.ops[e]) for e in P.ENG})
    return nc


def host_ab(inp, ssl):
    f = np.ascontiguousarray
    cw = inp["ab_conv_qkv"][0]
    scw = inp["sc_conv"][0]
    sq = inp["state_qkv_conv"][0, ssl]
    ss = inp["state_sconv"][0, ssl]
    return {
        "ab_w_in": inp["ab_w_in"][0], "ab_w_out": inp["ab_w_out"][0],
        "convw": f(cw.reshape(4, 12, 128).transpose(2, 1, 0).reshape(128, 48)),
        "scw": f(scw.reshape(3, 4, 128).transpose(2, 1, 0).reshape(128, 12)),
        "alog": f(np.broadcast_to(inp["dn_A_log"][0][None, :], (64, 8))),
        "dtb": f(np.broadcast_to(inp["dn_dt_bias"][0][None, :], (64, 8))),
        "dng": f(np.broadcast_to(inp["dn_norm_g"][0][None, :], (64, 64))),
        "s_delta": f(inp["state_delta"][0, ssl]),
        "s_qkv": f(sq.reshape(NS, 3, 12, 128).transpose(3, 2, 0, 1).reshape(128, 12 * NS * 3)),
        "s_sconv": f(ss.reshape(NS, 2, 4, 128).transpose(3, 2, 0, 1).reshape(128, 4 * NS * 2)),
    }


def host_sb(inp, ssl, past):
    f = np.ascontiguousarray
    ck = inp["cache_k"][0, ssl, :, :past]
    cv = inp["cache_v"][0, ssl, :, :past]
    return {
        "sb_w_qkv": inp["sb_w_qkv"][0], "sb_w_out": inp["sb_w_out"][0],
        "ckT": f(ck.transpose(0, 1, 3, 2).reshape(ck.shape[0], 1024, past)),
        "cv": f(cv.transpose(0, 2, 1, 3).reshape(cv.shape[0], past, 1024)),
    }


def host_common(inp, b, ssl):
    f = np.ascontiguousarray
    c_all = np.concatenate([inp["c_prompt"][b][None], inp["c_sample"][ssl]], 0)
    cT = c_all.T.reshape(8, 128, NSEQ).transpose(1, 0, 2).reshape(128, 8 * NSEQ)
    normg = inp["norm_g"].reshape(6, 8, 128).transpose(2, 0, 1).reshape(128, 48)
    finalg = inp["final_g"].reshape(8, 128).T
    return {
        "xp": f(inp["x_prompt"][b].T), "xs": f(inp["x_sample"][ssl].reshape(NS * DS, D).T),
        "cT": f(cT), "normg": f(normg), "finalg": f(finalg),
        "ada_w": inp["ada_w"], "ada_b": inp["ada_b"], "ff_w_in": inp["ff_w_in"], "ff_w_out": inp["ff_w_out"],
    }


_NC_CACHE = {}


def kernel(**inputs):
    inp = {k: np.asarray(v, dtype=np.float32) for k, v in inputs.items()}
    SEQ = inp["x_prompt"].shape[1]
    PAST = inp["cache_k"].shape[3]
    key = (SEQ, PAST)
    if key not in _NC_CACHE:
        _NC_CACHE[key] = build(Cfg(seq=SEQ, past=PAST, stage=99))
    nc = _NC_CACHE[key]
    n_cores = 8
    in_maps = []
    for c in range(n_cores):
        b = c // 2
        ssl = slice(NS * c, NS * (c + 1))
        m = host_common(inp, b, ssl)
        m.update(host_ab(inp, ssl))
        m.update(host_sb(inp, ssl, PAST))
        in_maps.append(m)
    res = run_bass_kernel_spmd(nc, in_maps, core_ids=list(range(n_cores)))
    R = res.results
    NB = inp["x_prompt"].shape[0]
    f32 = np.float32
    y_prompt = np.stack([R[2 * b]["yp"].T for b in range(NB)]).astype(f32)
    y_sample = np.concatenate([R[c]["ys"].T.reshape(NS, DS, D) for c in range(n_cores)]).astype(f32)
    p_delta = np.stack([R[2 * b]["o_p_delta"] for b in range(NB)])[None].astype(f32)
    p_qkv = np.stack([R[2 * b]["o_p_qkv"].reshape(128, 12, 3).transpose(2, 1, 0).reshape(3, 1536) for b in range(NB)])[None].astype(f32)
    p_sc = np.stack([R[2 * b]["o_p_sconv"].reshape(128, 4, 2).transpose(2, 1, 0).reshape(2, 512) for b in range(NB)])[None].astype(f32)
    p_k = np.stack([R[2 * b]["o_p_k"].reshape(16, 64, SEQ).transpose(0, 2, 1) for b in range(NB)])[None].astype(f32)
    p_v = np.stack([R[2 * b]["o_p_v"].reshape(SEQ, 16, 64).transpose(1, 0, 2) for b in range(NB)])[None].astype(f32)
    s_delta = np.concatenate([R[c]["o_s_delta"] for c in range(n_cores)])[None].astype(f32)
    s_qkv = np.concatenate([R[c]["o_s_qkv"].reshape(128, 12, NS, 3).transpose(2, 3, 1, 0).reshape(NS, 3, 1536) for c in range(n_cores)])[None].astype(f32)
    s_sc = np.concatenate([R[c]["o_s_sconv"].reshape(128, 4, NS, 2).transpose(2, 3, 1, 0).reshape(NS, 2, 512) for c in range(n_cores)])[None].astype(f32)
    s_k = np.concatenate([R[c]["o_s_k"].reshape(16, 64, NS, DS).transpose(2, 0, 3, 1) for c in range(n_cores)])[None].astype(f32)
    s_v = np.concatenate([R[c]["o_s_v"].reshape(NS, DS, 16, 64).transpose(0, 2, 1, 3) for c in range(n_cores)])[None].astype(f32)
    asc = np.ascontiguousarray
    return tuple(asc(a) for a in (y_prompt, y_sample, p_delta, p_qkv, p_sc, p_k, p_v, s_delta, s_qkv, s_sc, s_k, s_v))
```

```python
import numpy as np
from contextlib import ExitStack
import concourse.bass as bass
import concourse.mybir as mybir
from concourse.bass_utils import run_bass_kernel_spmd

F32 = mybir.dt.float32
BF16 = mybir.dt.bfloat16
AF = mybir.ActivationFunctionType
ALU = mybir.AluOpType

D = 1024
FC = 8
DFF = 2816
HC = 22
NS = 4
DS = 64
NSEQ = 1 + NS
TT = 512
EPS = 1e-6
SAME_SYNC = True
ND = 24


class V:
    __slots__ = ("buf", "ap")

    def __init__(self, buf, ap):
        self.buf = buf
        self.ap = ap


class Buf:
    def __init__(self, t, name="", psum=False):
        self.t = t
        self.name = name
        self.w = None
        self.r = {}
        self.psum = psum
        self.arena = False

    def __getitem__(self, idx):
        return V(self, self.t[idx])

    def v(self, ap):
        return V(self, ap)


class Prog:
    ENG = ["pe", "act", "dve", "pool", "sp"]

    def __init__(self, nc, stack):
        self.nc = nc
        self.stack = stack
        self.ops = {e: [] for e in self.ENG}
        self.sems = []
        for e in self.ENG:
            self.sems.append(stack.enter_context(nc.semaphore("s_" + e)))
        for k in range(ND):
            self.sems.append(stack.enter_context(nc.semaphore("d%d" % k)))
        self.eidx = {e: i for i, e in enumerate(self.ENG)}
        self.cnt = {e: 0 for e in self.ENG}
        self.waited = {e: {} for e in self.ENG}
        self.dma_cum = [0] * ND
        self.dma_rr = 0
        self.dma_rr2 = 0
        self.bar = None
        self.nins = 0

    def _wait(self, e, s, v):
        if v <= 0 or self.waited[e].get(s, 0) >= v:
            return
        self.waited[e][s] = v
        sem = self.sems[s]
        self.ops[e].append(lambda eng, sem=sem, v=v: eng.wait_ge(sem, v))
        self.nins += 1

    def _deps(self, e, reads, writes, own_always=False):
        deps = {}

        def add(s, v):
            if v > deps.get(s, 0):
                deps[s] = v
        for b in reads:
            if b.w:
                add(*b.w)
            if b.psum:
                for s, v in b.r.items():
                    add(s, v)
        for b in writes:
            if b.w:
                add(*b.w)
            for s, v in b.r.items():
                add(s, v)
        own = self.eidx[e]
        for s, v in deps.items():
            if s == own and not own_always and (e == "pe" or not SAME_SYNC):
                continue
            self._wait(e, s, v)

    def emit(self, e, fn, reads=(), writes=(), flag=True):
        reads = [b for b in reads if b is not None]
        writes = [b for b in writes if b is not None]
        self._deps(e, reads, writes)
        own = self.eidx[e]
        if flag:
            self.cnt[e] += 1
            tok = (own, self.cnt[e])
            sem = self.sems[own]
            self.ops[e].append(lambda eng, fn=fn, sem=sem: fn(eng).then_inc(sem, 1))
        else:
            tok = (own, self.cnt[e] + 1)
            self.ops[e].append(lambda eng, fn=fn: fn(eng))
        self.nins += 1
        for b in writes:
            b.w = tok
            b.r = {}
        for b in reads:
            if tok[1] > b.r.get(own, 0):
                b.r[own] = tok[1]

    def dma(self, q, out, in_, **kw):
        reads = [in_.buf]
        writes = [out.buf]
        self._deps(q, reads, writes, own_always=True)
        if q == "pool" and out.buf.arena and self.bar is not None:
            bc, bd = self.bar
            for f in self.ENG:
                if f != "pool":
                    self._wait(q, self.eidx[f], bc[f])
            for kk in range(ND // 2):
                self._wait(q, len(self.ENG) + kk, bd[kk])
        half = ND // 2
        if q == "pool":
            k = half + self.dma_rr2
            self.dma_rr2 = (self.dma_rr2 + 1) % half
        else:
            k = self.dma_rr
            self.dma_rr = (self.dma_rr + 1) % half
        sidx = len(self.ENG) + k
        self._wait(q, sidx, self.dma_cum[k])
        self.dma_cum[k] += 16
        tok = (sidx, self.dma_cum[k])
        sem = self.sems[sidx]
        oa, ia = out.ap, in_.ap
        self.ops[q].append(lambda eng, oa=oa, ia=ia, sem=sem, kw=kw: eng.dma_start(out=oa, in_=ia, **kw).then_inc(sem, 16))
        self.nins += 1
        out.buf.w = tok
        out.buf.r = {}
        if tok[1] > in_.buf.r.get(sidx, 0):
            in_.buf.r[sidx] = tok[1]

    def barrier(self):
        self.bar = (dict(self.cnt), list(self.dma_cum))
        for e in ("pe", "act", "dve", "sp"):
            for f in self.ENG:
                if f != e:
                    self._wait(e, self.eidx[f], self.cnt[f])
            for k in range(ND // 2):
                self._wait(e, len(self.ENG) + k, self.dma_cum[k])

    def finish(self):
        for k in range(ND):
            self._wait("sp", len(self.ENG) + k, self.dma_cum[k])
        for f in self.ENG:
            if f != "sp":
                self._wait("sp", self.eidx[f], self.cnt[f])

    def run(self):
        nc = self.nc
        ops = self.ops
        with nc.Block() as block:
            @block.tensor
            def _(eng):
                for f in ops["pe"]:
                    f(eng)

            @block.scalar
            def _(eng):
                for f in ops["act"]:
                    f(eng)

            @block.vector
            def _(eng):
                for f in ops["dve"]:
                    f(eng)

            @block.gpsimd
            def _(eng):
                for f in ops["pool"]:
                    f(eng)

            @block.sync
            def _(eng):
                for f in ops["sp"]:
                    f(eng)

    def mm(self, out, lhsT, rhs, start=True, stop=True, flag=True, skip=False):
        self.emit("pe", lambda eng, o=out.ap, l=lhsT.ap, r=rhs.ap, st=start, sp=stop, sk=skip:
                  eng.matmul(o, l, r, start=st, stop=sp, skip_group_check=sk),
                  reads=[lhsT.buf, rhs.buf], writes=[out.buf], flag=flag)

    def act(self, out, in_, func, bias=None, scale=None, e="act"):
        kw = {}
        rd = [in_.buf]
        if bias is not None:
            if isinstance(bias, V):
                kw["bias"] = bias.ap
                rd.append(bias.buf)
            else:
                kw["bias"] = bias
        if scale is not None:
            if isinstance(scale, V):
                kw["scale"] = scale.ap
                rd.append(scale.buf)
            else:
                kw["scale"] = scale
        self.emit("act", lambda eng, o=out.ap, i=in_.ap, f=func, kw=kw: eng.activation(o, i, f, **kw),
                  reads=rd, writes=[out.buf])

    def tt(self, out, in0, in1, op, e="dve"):
        self.emit(e, lambda eng, o=out.ap, a=in0.ap, b=in1.ap, op=op: eng.tensor_tensor(o, a, b, op),
                  reads=[in0.buf, in1.buf], writes=[out.buf])

    def stt(self, out, in0, scalar, in1, op0, op1, e="dve"):
        rd = [in0.buf, in1.buf]
        if isinstance(scalar, V):
            rd.append(scalar.buf)
            sc = scalar.ap
        else:
            sc = scalar
        self.emit(e, lambda eng, o=out.ap, a=in0.ap, s=sc, b=in1.ap, op0=op0, op1=op1:
                  eng.scalar_tensor_tensor(o, a, s, b, op0, op1),
                  reads=rd, writes=[out.buf])

    def ts(self, out, in0, s1, s2, op0, op1=None, e="dve"):
        rd = [in0.buf]
        a1 = s1
        a2 = s2
        if isinstance(s1, V):
            rd.append(s1.buf)
            a1 = s1.ap
        if isinstance(s2, V):
            rd.append(s2.buf)
            a2 = s2.ap
        if op1 is None:
            self.emit(e, lambda eng, o=out.ap, a=in0.ap, a1=a1, op0=op0: eng.tensor_scalar(o, a, a1, None, op0),
                      reads=rd, writes=[out.buf])
        else:
            self.emit(e, lambda eng, o=out.ap, a=in0.ap, a1=a1, a2=a2, op0=op0, op1=op1:
                      eng.tensor_scalar(o, a, a1, a2, op0, op1),
                      reads=rd, writes=[out.buf])

    def copy(self, out, in_, e="dve"):
        self.emit(e, lambda eng, o=out.ap, i=in_.ap: eng.tensor_copy(o, i), reads=[in_.buf], writes=[out.buf])

    def memset(self, out, val, e="dve"):
        self.emit(e, lambda eng, o=out.ap, v=val: eng.memset(o, v), reads=[], writes=[out.buf])

    def reduce(self, out, in_, op=None):
        self.emit("dve", lambda eng, o=out.ap, i=in_.ap: eng.tensor_reduce(o, i, mybir.AxisListType.X, ALU.add),
                  reads=[in_.buf], writes=[out.buf])

    def recip(self, out, in_):
        self.emit("dve", lambda eng, o=out.ap, i=in_.ap: eng.reciprocal(o, i), reads=[in_.buf], writes=[out.buf])

    def aselect(self, out, in_, pattern, cmp, fill, base, cm):
        self.emit("pool", lambda eng, o=out.ap, i=in_.ap: eng.affine_select(o, i, pattern, cmp, fill, base=base, channel_multiplier=cm),
                  reads=[in_.buf], writes=[out.buf])


class Cfg:
    def __init__(self, seq=4096, past=4096, stage=99, sbdbg=255):
        self.sbdbg = sbdbg
        self.seq = seq
        self.past = past
        self.stage = stage
        self.ntile = seq // TT


class Arena:
    def __init__(self, t, n4):
        self.t = t
        self.n4 = n4
        self.off = 0

    def reset(self, off=0):
        self.off = off

    def alloc(self, name, shape, dt=F32):
        n = 1
        for d in shape[1:]:
            n *= d
        n4 = n if dt == F32 else (n + 1) // 2
        n4 = (n4 + 1) // 2 * 2
        o = self.off
        self.off += n4
        assert self.off <= self.n4, ("arena overflow", name, self.off, self.n4)
        ap = self.t[0:shape[0], o:o + n4]
        if dt != F32:
            ap = ap.bitcast(dt)
        ap = ap[:, 0:n]
        if len(shape) == 3:
            ap = ap.rearrange("p (a b) -> p a b", a=shape[1])
        bf = Buf(ap, name)
        bf.arena = True
        return bf


def build(cfg):
    nc = bass.Bass("TRN2", target_bir_lowering=False)
    SEQ = cfg.seq
    st = ExitStack()
    with st:
        P = Prog(nc, st)

        def dram_in(name, shape, dt=F32):
            return Buf(nc.dram_tensor(name, list(shape), dt, kind="ExternalInput").ap(), name)

        def dram_out(name, shape, dt=F32):
            return Buf(nc.dram_tensor(name, list(shape), dt, kind="ExternalOutput").ap(), name)

        def sb(name, shape, dt=F32):
            return Buf(st.enter_context(nc.sbuf_tensor(name, list(shape), dt)), name)

        def ps(name, shape, dt=F32):
            return Buf(st.enter_context(nc.psum_tensor(name, list(shape), dt)), name, psum=True)

        xp = dram_in("xp", [D, SEQ])
        xs = dram_in("xs", [D, NS * DS])
        cT = dram_in("cT", [128, FC * NSEQ])
        normg = dram_in("normg", [128, 6 * FC])
        finalg = dram_in("finalg", [128, FC])
        ada_w = dram_in("ada_w", [2, D, 9 * D])
        ada_b = dram_in("ada_b", [2, 9 * D])
        ff_w_in = dram_in("ff_w_in", [2, 2, D, 2 * DFF])
        ff_w_out = dram_in("ff_w_out", [2, 2, DFF, D])
        ab_w_in = dram_in("ab_w_in", [D, 3600])
        ab_w_out = dram_in("ab_w_out", [D, D])
        convw_d = dram_in("convw", [128, 12 * 4])
        scw_d = dram_in("scw", [128, 4 * 3])
        alog_d = dram_in("alog", [64, 8])
        dtb_d = dram_in("dtb", [64, 8])
        dng_d = dram_in("dng", [64, 64])
        s_delta = dram_in("s_delta", [NS, 8, 64, 64])
        s_qkv = dram_in("s_qkv", [128, 12 * NS * 3])
        s_sconv = dram_in("s_sconv", [128, 4 * NS * 2])
        sb_w_qkv = dram_in("sb_w_qkv", [D, 3 * D])
        sb_w_out = dram_in("sb_w_out", [D, D])
        PAST = cfg.past
        ckT_d = dram_in("ckT", [NS, D, PAST])
        cv_d = dram_in("cv", [NS, PAST, D])
        kT_scr = Buf(nc.dram_tensor("kT_scr", [D, SEQ], BF16, kind="ExternalOutput").ap(), "kT_scr")
        v_scr = Buf(nc.dram_tensor("v_scr", [SEQ, D], BF16, kind="ExternalOutput").ap(), "v_scr")
        o_p_k = dram_out("o_p_k", [D, SEQ])
        o_p_v = dram_out("o_p_v", [SEQ, D])
        o_s_k = dram_out("o_s_k", [D, NS * DS])
        o_s_v = dram_out("o_s_v", [NS * DS, D])
        yp = dram_out("yp", [D, SEQ])
        ys = dram_out("ys", [D, NS * DS])
        o_p_delta = dram_out("o_p_delta", [8, 64, 64])
        o_s_delta = dram_out("o_s_delta", [NS, 8, 64, 64])
        o_p_qkv = dram_out("o_p_qkv", [128, 12 * 3])
        o_s_qkv = dram_out("o_s_qkv", [128, 12 * NS * 3])
        o_p_sconv = dram_out("o_p_sconv", [128, 4 * 2])
        o_s_sconv = dram_out("o_s_sconv", [128, 4 * NS * 2])

        ident = sb("ident", [128, 128])
        ones_bf = sb("ones_bf", [128, 128], BF16)
        onesf = sb("onesf", [128, 128])
        maskU = sb("maskU", [64, 64])
        maskL = sb("maskL", [64, 64])
        maskS = sb("maskS", [64, 64])
        condT = sb("condT", [128, FC * NSEQ])
        epsb = sb("epsb", [128, 1])
        g_sb = sb("g_sb", [128, 6 * FC])
        fg_sb = sb("fg_sb", [128, FC])
        modT = sb("modT", [128, 2 * 72 * NSEQ])
        gsT = sb("gsT", [128, 6 * FC * NSEQ])
        convw = sb("convw_s", [128, 12, 4])
        scw = sb("scw_s", [128, 4, 3])
        negA = sb("negA", [64, 8])
        dtb = sb("dtb_s", [64, 8])
        dng = sb("dng_s", [64, 64])
        ones512 = sb("ones512", [128, 512], BF16)
        negincl = sb("negincl", [128, 128], BF16)
        ones1b = sb("ones1b", [128, 128], BF16)
        mdiag = [sb("mdiag%d" % d, [128, 512], BF16) for d in range(4)]
        mnew = sb("mnew", [64, 512], BF16)
        hal3 = sb("hal3", [128, 12, 3])
        hal2 = sb("hal2", [128, 4, 2])
        S_sb = sb("S_sb", [64, 512])
        x = sb("x", [128, FC, TT])
        h = sb("h", [128, FC, TT], BF16)
        om = sb("om", [128, FC, TT], BF16)
        WPN = 2
        wp = [sb("wp%d" % i, [128, HC * 512], BF16) for i in range(WPN)]
        modrow = [sb("modrow%d" % i, [8, 512]) for i in range(2)]
        biasrow = [sb("biasrow%d" % i, [8, 512]) for i in range(2)]
        AR4 = 24576
        ar = Arena(st.enter_context(nc.sbuf_tensor("arena", [128, AR4], F32)), AR4)
        psb = [ps("ps%d" % i, [128, 512]) for i in range(8)]
        rr = {"ps": 0, "wp": 0, "mr": 0, "alt": 0, "psr": 0, "kTb": 0}

        def nxt(key, lst):
            i = rr[key]
            rr[key] = (i + 1) % len(lst)
            return lst[i]

        def drive(gens):
            gens = list(gens)
            while gens:
                for gg in list(gens):
                    try:
                        next(gg)
                    except StopIteration:
                        gens.remove(gg)

        def evac(out, in_):
            rr["alt"] ^= 1
            if rr["alt"]:
                P.act(out, in_, AF.Copy)
            else:
                P.copy(out, in_)

        P.memset(onesf[:, :], 1.0, e="pool")
        P.memset(epsb[:, :], EPS, e="pool")
        P.memset(ones_bf[:, :], 1.0 / D, e="pool")
        P.aselect(ident[:, :], onesf[:, :], [[-1, 128]], ALU.is_equal, 0.0, 0, 1)
        P.aselect(maskU[:, :], onesf[0:64, 0:64], [[1, 64]], ALU.is_ge, 0.0, 0, -1)
        P.aselect(maskL[:, :], onesf[0:64, 0:64], [[-1, 64]], ALU.is_gt, 0.0, 0, 1)
        P.aselect(maskS[:, :], onesf[0:64, 0:64], [[1, 64]], ALU.is_gt, 0.0, 0, -1)
        P.memset(ones512[:, :], 1.0, e="pool")
        P.memset(ones1b[:, :], 1.0, e="pool")
        P.memset(negincl[:, :], -1.0, e="pool")
        P.aselect(negincl[:, :], negincl[:, :], [[-1, 128]], ALU.is_ge, 0.0, 0, 1)
        for d in range(4):
            P.aselect(mdiag[d][:, :], ones512[:, :], [[1, 512]], ALU.is_gt, 0.0, -128 * d, -1)
        P.aselect(V(mnew, mnew.t[:, :].rearrange("p (a b) -> p a b", a=8)),
                  V(ones512, ones512.t[0:64, :].rearrange("p (a b) -> p a b", a=8)), [[0, 8], [1, 64]], ALU.is_gt, 0.0, 0, -1)
        P.dma("sp", condT[:, :], cT[:, :])
        P.dma("sp", g_sb[:, :], normg[:, :])
        P.dma("sp", fg_sb[:, :], finalg[:, :])
        P.dma("sp", V(convw, convw.t[:, :, :]), V(convw_d, convw_d.t[:, :].rearrange("p (a b) -> p a b", a=12)))
        P.dma("sp", V(scw, scw.t[:, :, :]), V(scw_d, scw_d.t[:, :].rearrange("p (a b) -> p a b", a=4)))
        P.dma("sp", negA[:, :], alog_d[:, :])
        P.dma("sp", dtb[:, :], dtb_d[:, :])
        P.dma("sp", dng[:, :], dng_d[:, :])
        P.act(condT[:, :], condT[:, :], AF.Silu)
        P.act(negA[:, :], negA[:, :], AF.Exp)
        P.ts(negA[:, :], negA[:, :], -1.0, None, ALU.mult)
        P.memset(V(hal3, hal3.t[:, :, :]), 0.0)
        P.memset(V(hal2, hal2.t[:, :, :]), 0.0)
        P.memset(S_sb[:, :], 0.0)

        def mod_idx(l, chunk):
            return (l * 72 + chunk) * NSEQ

        for l in range(2):
            for cb in range(18):
                wb = nxt("wp", wp)
                wv = wb.t[:, 0:8192].bitcast(F32)
                src = ada_w.t[l, :, cb * 512:(cb + 1) * 512].rearrange("(kc p) n -> p kc n", p=128)
                P.dma("sp", V(wb, wv.rearrange("p (kc n) -> p kc n", kc=FC)), V(ada_w, src))
                br = nxt("mr", biasrow)
                mr = modrow[biasrow.index(br)]
                P.dma("sp", br[0:NSEQ, :], V(ada_b, ada_b.t[l, cb * 512:(cb + 1) * 512].partition_broadcast(NSEQ)))
                pt = nxt("ps", psb)
                for kc in range(FC):
                    P.mm(pt[0:NSEQ, :], V(condT, condT.t[:, kc * NSEQ:(kc + 1) * NSEQ]),
                         V(wb, wv[:, kc * 512:(kc + 1) * 512]), start=(kc == 0), stop=(kc == FC - 1),
                         flag=(kc == FC - 1))
                P.tt(mr[0:NSEQ, :], pt[0:NSEQ, :], br[0:NSEQ, :], ALU.add)
                pt2 = nxt("ps", psb)
                for j in range(4):
                    P.mm(pt2[:, j * NSEQ:(j + 1) * NSEQ], mr[0:NSEQ, j * 128:(j + 1) * 128],
                         ident[0:NSEQ, 0:NSEQ], flag=(j == 3))
                c0 = mod_idx(l, cb * 4)
                P.copy(modT[:, c0:c0 + 4 * NSEQ], pt2[:, 0:4 * NSEQ])

        def mod_ap(l, sub, kind, fc, seq):
            c0 = mod_idx(l, (sub * 3 + kind) * 8 + fc) + seq
            return V(modT, modT.t[:, c0:c0 + 1])

        def gs_ap(l, sub, fc, seq):
            c0 = ((l * 3 + sub) * FC + fc) * NSEQ + seq
            return V(gsT, gsT.t[:, c0:c0 + 1])

        for l in range(2):
            for sub in range(3):
                c0 = mod_idx(l, (sub * 3 + 1) * 8)
                o0 = (l * 3 + sub) * FC * NSEQ
                gv = g_sb.t[:, (l * 3 + sub) * FC:(l * 3 + sub + 1) * FC].unsqueeze(2).broadcast_to([128, FC, NSEQ])
                P.stt(V(gsT, gsT.t[:, o0:o0 + FC * NSEQ].rearrange("p (f s) -> p f s", s=NSEQ)),
                      V(modT, modT.t[:, c0:c0 + FC * NSEQ].rearrange("p (f s) -> p f s", s=NSEQ)),
                      1.0, V(g_sb, gv), ALU.add, ALU.mult)
                if sub != 1:
                    c2 = mod_idx(l, (sub * 3 + 2) * 8)
                    P.ts(modT[:, c2:c2 + FC * NSEQ], modT[:, c2:c2 + FC * NSEQ], 0.5, None, ALU.mult)

        def mod_norm(T, segs, l, sub, out_f32=None):
            ar.reset()
            P.barrier()
            sq = ar.alloc("sq", [128, FC, TT], BF16)
            rstd = ar.alloc("rstd", [128, TT])
            tmpn = [ar.alloc("tmpn%d" % i, [128, TT]) for i in range(2)]
            for fc in range(FC):
                P.act(sq[:, fc, 0:T], x[:, fc, 0:T], AF.Square)
            pt = nxt("ps", psb)
            for fc in range(FC):
                P.mm(pt[:, 0:T], ones_bf[:, :], sq[:, fc, 0:T], start=(fc == 0), stop=(fc == FC - 1),
                     flag=(fc == FC - 1))
            P.act(rstd[:, 0:T], pt[:, 0:T], AF.Sqrt, bias=V(epsb, epsb.t[:, 0:1]), scale=1.0)
            P.recip(rstd[:, 0:T], rstd[:, 0:T])
            for fc in range(FC):
                tn = tmpn[fc % 2]
                P.tt(tn[:, 0:T], x[:, fc, 0:T], rstd[:, 0:T], ALU.mult)
                for (c0, n, s) in segs:
                    if l is None:
                        P.act(out_f32[:, fc, c0:c0 + n], tn[:, c0:c0 + n], AF.Identity, scale=V(fg_sb, fg_sb.t[:, fc:fc + 1]))
                    else:
                        P.act(h[:, fc, c0:c0 + n], tn[:, c0:c0 + n], AF.Identity,
                              bias=mod_ap(l, sub, 0, fc, s), scale=gs_ap(l, sub, fc, s))

        def load_w(dst_buf, dst_ap, src_buf, src_ap):
            P.dma("pool", V(dst_buf, dst_ap), V(src_buf, src_ap))

        def proj_fm(W2d_buf, W2d, col0, ncols, KC, rhs, T, consume):
            c = 0
            while c < ncols:
                w = min(512, ncols - c)
                wb = nxt("wp", wp)
                wv = wb.t[:, 0:KC * 512].rearrange("p (kc n) -> p kc n", kc=KC)
                load_w(wb, wv[:, :, 0:w], W2d_buf, W2d[:, col0 + c:col0 + c + w].rearrange("(kc p) n -> p kc n", p=128))
                for j in range(w // 128):
                    pt = nxt("ps", psb)
                    for kc in range(KC):
                        P.mm(pt[:, 0:T], V(wb, wv[:, kc, j * 128:(j + 1) * 128]), rhs[:, kc, 0:T],
                             start=(kc == 0), stop=(kc == KC - 1), flag=(kc == KC - 1))
                    consume(c // 128 + j, pt)
                c += w

        def ffn(T, segs, l, sub, fi):
            w_in = ff_w_in.t[l, fi]
            w_out = ff_w_out.t[l, fi]
            ar.reset()
            P.barrier()
            hid = ar.alloc("hid", [128, HC, TT], BF16)
            sg = [ar.alloc("sg%d" % i, [128, TT]) for i in range(2)]
            c0 = 0
            k = 0
            while c0 < DFF:
                w = min(512, DFF - c0)
                wb = nxt("wp", wp)
                gview = wb.t[:, 0:FC * 512].rearrange("p (kc n) -> p kc n", kc=FC)
                uview = wb.t[:, FC * 512:2 * FC * 512].rearrange("p (kc n) -> p kc n", kc=FC)
                load_w(wb, gview[:, :, 0:w], ff_w_in, w_in[:, c0:c0 + w].rearrange("(kc p) n -> p kc n", p=128))
                load_w(wb, uview[:, :, 0:w], ff_w_in, w_in[:, DFF + c0:DFF + c0 + w].rearrange("(kc p) n -> p kc n", p=128))
                for j in range(w // 128):
                    pg = nxt("ps", psb)
                    pu = nxt("ps", psb)
                    for kc in range(FC):
                        P.mm(pg[:, 0:T], V(wb, gview[:, kc, j * 128:(j + 1) * 128]), h[:, kc, 0:T],
                             start=(kc == 0), stop=(kc == FC - 1), flag=(kc == FC - 1))
                    for kc in range(FC):
                        P.mm(pu[:, 0:T], V(wb, uview[:, kc, j * 128:(j + 1) * 128]), h[:, kc, 0:T],
                             start=(kc == 0), stop=(kc == FC - 1), flag=(kc == FC - 1))
                    s_ = sg[k % 2]
                    k += 1
                    P.act(s_[:, 0:T], pg[:, 0:T], AF.Silu)
                    P.tt(hid[:, c0 // 128 + j, 0:T], s_[:, 0:T], pu[:, 0:T], ALU.mult)
                c0 += w
            for half in range(2):
                wb = nxt("wp", wp)
                wv = wb.t[:, 0:HC * 512].rearrange("p (kc n) -> p kc n", kc=HC)
                for k0 in range(0, HC, 11):
                    load_w(wb, wv[:, k0:k0 + 11, :], ff_w_out,
                           w_out[k0 * 128:(k0 + 11) * 128, half * 512:(half + 1) * 512].rearrange("(kc p) n -> p kc n", p=128))
                for m in range(4):
                    po = nxt("ps", psb)
                    for kc in range(HC):
                        P.mm(po[:, 0:T], V(wb, wv[:, kc, m * 128:(m + 1) * 128]), hid[:, kc, 0:T],
                             start=(kc == 0), stop=(kc == HC - 1), flag=(kc == HC - 1))
                    fc = half * 4 + m
                    for (c0, n, s) in segs:
                        P.stt(x[:, fc, c0:c0 + n], po[:, c0:c0 + n], mod_ap(l, sub, 2, fc, s), x[:, fc, c0:c0 + n],
                              ALU.mult, ALU.add)

        def resid_add(l, sub, segs):
            def consume(j, pt):
                for (c0, n, s) in segs:
                    P.stt(x[:, j, c0:c0 + n], pt[:, c0:c0 + n], mod_ap(l, sub, 2, j, s), x[:, j, c0:c0 + n],
                          ALU.mult, ALU.add)
            return consume

        def bc3(v_ap, n_mid, n_in, axis):
            if axis == 2:
                return v_ap.unsqueeze(2).broadcast_to([v_ap.shape[0], n_mid, n_in])
            return v_ap.unsqueeze(1).broadcast_to([v_ap.shape[0], n_mid, n_in])

        def v3(buf, lo, nh, w=64):
            return buf.t[:, lo:lo + nh * w].rearrange("p (a b) -> p a b", a=nh)

        def ab_mixer(kind, ti, T, segs, last):
            W = ab_w_in.t
            ar.reset()
            P.barrier()
            offs3, offs2 = [], []
            o3 = o2 = 0
            for (c0, n, s) in segs:
                offs3.append(o3)
                offs2.append(o2)
                o3 += 3 + n
                o2 += 2 + n
            L3, L2 = o3, o2
            qkv_pre = ar.alloc("qkv_pre", [128, 12, L3])
            mark = ar.off
            sBs = ar.alloc("sBs", [128, 4, TT])
            sCs = ar.alloc("sCs", [128, 4, TT])
            scx = ar.alloc("scx", [128, 4, L2])
            acc = [ar.alloc("acc%d" % i, [128, L2]) for i in range(2)]

            if kind == "p":
                P.copy(V(qkv_pre, qkv_pre.t[:, :, 0:3]), V(hal3, hal3.t[:, :, :]))
                P.copy(V(scx, scx.t[:, :, 0:2]), V(hal2, hal2.t[:, :, :]))
            else:
                for si, (c0, n, s) in enumerate(segs):
                    q = s - 1
                    P.dma("sp", V(qkv_pre, qkv_pre.t[:, :, offs3[si]:offs3[si] + 3]),
                          V(s_qkv, s_qkv.t[:, :].rearrange("p (a q k) -> p a q k", a=12, q=NS)[:, :, q, :]))
                    P.dma("sp", V(scx, scx.t[:, :, offs2[si]:offs2[si] + 2]),
                          V(s_sconv, s_sconv.t[:, :].rearrange("p (a q k) -> p a q k", a=4, q=NS)[:, :, q, :]))

            def c_qkv(j, pt):
                for si, (c0, n, s) in enumerate(segs):
                    evac(V(qkv_pre, qkv_pre.t[:, j, offs3[si] + 3:offs3[si] + 3 + n]), pt[:, c0:c0 + n])
            proj_fm(ab_w_in, W, 0, 1536, FC, h, T, c_qkv)

            def c_sB(j, pt):
                evac(sBs[:, j, 0:T], pt[:, 0:T])

            def c_sC(j, pt):
                evac(sCs[:, j, 0:T], pt[:, 0:T])

            def c_sx(j, pt):
                for si, (c0, n, s) in enumerate(segs):
                    P.tt(V(scx, scx.t[:, j, offs2[si] + 2:offs2[si] + 2 + n]), sCs[:, j, c0:c0 + n], pt[:, c0:c0 + n], ALU.mult)
            proj_fm(ab_w_in, W, 2064, 512, FC, h, T, c_sB)
            proj_fm(ab_w_in, W, 2576, 512, FC, h, T, c_sC)
            proj_fm(ab_w_in, W, 3088, 512, FC, h, T, c_sx)
            for j in range(4):
                a_ = acc[j % 2]
                P.ts(a_[:, 0:L2 - 2], V(scx, scx.t[:, j, 0:L2 - 2]), V(scw, scw.t[:, j, 0:1]), None, ALU.mult)
                P.stt(a_[:, 0:L2 - 2], V(scx, scx.t[:, j, 1:L2 - 1]), V(scw, scw.t[:, j, 1:2]), a_[:, 0:L2 - 2], ALU.mult, ALU.add)
                P.stt(a_[:, 0:L2 - 2], V(scx, scx.t[:, j, 2:L2]), V(scw, scw.t[:, j, 2:3]), a_[:, 0:L2 - 2], ALU.mult, ALU.add)
                for si, (c0, n, s) in enumerate(segs):
                    P.tt(om[:, 4 + j, c0:c0 + n], sBs[:, j, c0:c0 + n], a_[:, offs2[si]:offs2[si] + n], ALU.mult)
            if kind == "p":
                P.copy(V(hal2, hal2.t[:, :, :]), V(scx, scx.t[:, :, T:T + 2]))
                if last:
                    P.dma("sp", V(o_p_sconv, o_p_sconv.t[:, :].rearrange("p (a k) -> p a k", a=4)), V(hal2, hal2.t[:, :, :]))
            else:
                for si, (c0, n, s) in enumerate(segs):
                    q = s - 1
                    P.dma("sp", V(o_s_sconv, o_s_sconv.t[:, :].rearrange("p (a q k) -> p a q k", a=4, q=NS)[:, :, q, :]),
                          V(scx, scx.t[:, :, offs2[si] + n:offs2[si] + n + 2]))

            P.barrier()
            ar.reset(mark)
            f = lambda nm, shp: ar.alloc(nm, shp)
            cacc = f("cacc", [128, 12, 64])
            ctmp = f("ctmp", [128, 12, 64])
            qkvc = f("qkvc", [128, 12, 64])
            Qr, Kr, Vr, Kb, Qg, RHSk, Kdec, zs, sqt, O = [f(nm, [64, 512]) for nm in
                                                           ("Qr", "Kr", "Vr", "Kb", "Qg", "RHSk", "Kdec", "zs", "sqt", "O")]
            sm = {nm: f(nm, [64, 16]) for nm in ("ssq", "rn", "ab", "t8", "g8", "b8", "beta", "Gs", "eGG", "dG", "eGe", "ss8")}
            G = {}
            for g in range(2):
                for nm in ("kT", "kbT", "qT", "qgT", "rhsE", "E", "EmS", "EmI", "Xa", "Xb", "XTa", "XTb", "Pa", "Pb",
                           "QKD", "solv", "nsolkT", "U", "Stmp"):
                    G[(nm, g)] = f(nm + str(g), [64, 256])
            zwb = nxt("wp", wp)
            zw = zwb.t[:, 0:FC * 528].rearrange("p (kc n) -> p kc n", kc=FC)
            load_w(zwb, zw[:, :, :], ab_w_in, W[:, 1536:2064].rearrange("(kc p) n -> p kc n", p=128))
            I64 = V(ident, ident.t[0:64, 0:64])
            one64 = V(onesf, onesf.t[0:64, 0:1])
            eps64 = V(epsb, epsb.t[0:64, 0:1])
            MUL, ADD, SUB = ALU.mult, ALU.add, ALU.subtract

            def chunk(cc, pc):
                def pre(k):
                    return V(qkv_pre, qkv_pre.t[:, :, pc + k:pc + k + 64])

                def wv(k):
                    return V(convw, convw.t[:, :, k:k + 1].broadcast_to([128, 12, 64]))
                A3 = lambda b: V(b, b.t[:, :, :])
                P.tt(A3(cacc), pre(0), wv(0), MUL)
                for k in range(1, 4):
                    P.tt(A3(ctmp), pre(k), wv(k), MUL)
                    P.tt(A3(cacc), A3(cacc), A3(ctmp), ADD)
                P.act(A3(qkvc), A3(cacc), AF.Silu)
                for dst, base in ((Qr, 0), (Kr, 4), (Vr, 8)):
                    pt = nxt("ps", psb)
                    for j in range(4):
                        P.mm(pt[0:64, j * 128:(j + 1) * 128], V(qkvc, qkvc.t[:, base + j, :]), ident[:, :], flag=(j == 3))
                    evac(dst[:, :], pt[0:64, :])
                pz = nxt("ps", psb)
                pab = nxt("ps", psb)
                for kc in range(FC):
                    P.mm(pz[0:64, :], h[:, kc, cc:cc + 64], V(zwb, zw[:, kc, 0:512]), start=(kc == 0), stop=(kc == FC - 1),
                         flag=(kc == FC - 1))
                for kc in range(FC):
                    P.mm(pab[0:64, 0:16], h[:, kc, cc:cc + 64], V(zwb, zw[:, kc, 512:528]), start=(kc == 0),
                         stop=(kc == FC - 1), flag=(kc == FC - 1))
                P.act(zs[:, :], pz[0:64, :], AF.Silu)
                P.copy(sm["ab"][:, :], pab[0:64, 0:16])
                P.tt(sqt[:, :], Qr[:, :], Qr[:, :], MUL)
                P.reduce(sm["ssq"][:, 0:8], V(sqt, v3(sqt, 0, 8)))
                P.tt(sqt[:, :], Kr[:, :], Kr[:, :], MUL)
                P.reduce(sm["ssq"][:, 8:16], V(sqt, v3(sqt, 0, 8)))
                P.act(sm["rn"][:, :], sm["ssq"][:, :], AF.Sqrt, bias=eps64, scale=1.0)
                P.recip(sm["rn"][:, :], sm["rn"][:, :])
                P.ts(sm["rn"][:, 0:8], sm["rn"][:, 0:8], 0.125, None, MUL)
                P.tt(V(Qr, v3(Qr, 0, 8)), V(Qr, v3(Qr, 0, 8)), V(sm["rn"], bc3(sm["rn"].t[:, 0:8], 8, 64, 2)), MUL)
                P.tt(V(Kr, v3(Kr, 0, 8)), V(Kr, v3(Kr, 0, 8)), V(sm["rn"], bc3(sm["rn"].t[:, 8:16], 8, 64, 2)), MUL)
                P.tt(sm["t8"][:, 0:8], sm["ab"][:, 0:8], dtb[:, :], ADD)
                P.act(sm["t8"][:, 0:8], sm["t8"][:, 0:8], AF.Exp)
                P.act(sm["t8"][:, 0:8], sm["t8"][:, 0:8], AF.Ln, bias=one64, scale=1.0)
                P.tt(sm["g8"][:, 0:8], sm["t8"][:, 0:8], negA[:, :], MUL)
                P.act(sm["b8"][:, 0:8], sm["ab"][:, 8:16], AF.Exp, scale=-1.0)
                P.ts(sm["b8"][:, 0:8], sm["b8"][:, 0:8], 1.0, None, ADD)
                P.recip(sm["beta"][:, 0:8], sm["b8"][:, 0:8])
                pg = nxt("ps", psb)
                P.mm(pg[0:64, 0:8], maskU[:, :], sm["g8"][:, 0:8], flag=False)
                P.mm(pg[0:64, 8:16], onesf[0:64, 0:64], sm["g8"][:, 0:8])
                P.copy(sm["Gs"][:, :], pg[0:64, 0:16])
                P.act(sm["eGG"][:, :], sm["Gs"][:, :], AF.Exp)
                P.tt(sm["dG"][:, 0:8], sm["Gs"][:, 8:16], sm["Gs"][:, 0:8], SUB)
                P.act(sm["eGe"][:, 0:8], sm["dG"][:, 0:8], AF.Exp)
                beta_b = V(sm["beta"], bc3(sm["beta"].t[:, 0:8], 8, 64, 2))
                eG_b = V(sm["eGG"], bc3(sm["eGG"].t[:, 0:8], 8, 64, 2))
                eGe_b = V(sm["eGe"], bc3(sm["eGe"].t[:, 0:8], 8, 64, 2))
                P.tt(V(Kb, v3(Kb, 0, 8)), V(Kr, v3(Kr, 0, 8)), beta_b, MUL)
                P.tt(V(Qg, v3(Qg, 0, 8)), V(Qr, v3(Qr, 0, 8)), eG_b, MUL)
                P.tt(V(RHSk, v3(RHSk, 0, 8)), V(Kb, v3(Kb, 0, 8)), eG_b, MUL)
                P.tt(V(Vr, v3(Vr, 0, 8)), V(Vr, v3(Vr, 0, 8)), beta_b, MUL)
                P.tt(V(Kdec, v3(Kdec, 0, 8)), V(Kr, v3(Kr, 0, 8)), eGe_b, MUL)
                def pre_g(g):
                    T_ = lambda nm: G[(nm, g)]
                    hc = lambda i: slice((g * 4 + i) * 64, (g * 4 + i + 1) * 64)
                    lc = lambda i: slice(i * 64, (i + 1) * 64)
                    for src, dn in ((Kr, "kT"), (Kb, "kbT"), (Qr, "qT"), (Qg, "qgT")):
                        yield
                        pt = nxt("ps", psb)
                        for i in range(4):
                            P.mm(pt[0:64, lc(i)], src[:, hc(i)], I64, flag=(i == 3))
                        evac(T_(dn)[:, :], pt[0:64, 0:256])
                    P.tt(V(T_("rhsE"), v3(T_("rhsE"), 0, 4)), V(maskU, bc3(maskU.t[:, :], 4, 64, 1)),
                         V(sm["g8"], bc3(sm["g8"].t[:, g * 4:g * 4 + 4], 4, 64, 2)), MUL)
                    yield
                    pe_ = nxt("ps", psb)
                    P.mm(pe_[0:64, 0:256], maskL[:, :], T_("rhsE")[:, :])
                    P.act(T_("E")[:, :], pe_[0:64, 0:256], AF.Exp)
                    P.tt(V(T_("EmS"), v3(T_("EmS"), 0, 4)), V(T_("E"), v3(T_("E"), 0, 4)), V(maskS, bc3(maskS.t[:, :], 4, 64, 1)), MUL)
                    P.tt(V(T_("EmI"), v3(T_("EmI"), 0, 4)), V(T_("E"), v3(T_("E"), 0, 4)), V(maskU, bc3(maskU.t[:, :], 4, 64, 1)), MUL)
                    yield
                    pA = nxt("ps", psb)
                    yield
                    pQ = nxt("ps", psb)
                    for i in range(4):
                        P.mm(pA[0:64, lc(i)], T_("kT")[:, lc(i)], T_("kbT")[:, lc(i)], flag=(i == 3))
                    for i in range(4):
                        P.mm(pQ[0:64, lc(i)], T_("kT")[:, lc(i)], T_("qT")[:, lc(i)], flag=(i == 3))
                    P.tt(T_("Xa")[:, :], pA[0:64, 0:256], T_("EmS")[:, :], MUL)
                    P.tt(T_("QKD")[:, :], pQ[0:64, 0:256], T_("EmI")[:, :], MUL)
                    yield
                    pX = nxt("ps", psb)
                    for i in range(4):
                        P.mm(pX[0:64, lc(i)], T_("Xa")[:, lc(i)], I64, flag=(i == 3))
                    evac(T_("XTa")[:, :], pX[0:64, 0:256])
                    P.tt(V(T_("Pa"), v3(T_("Pa"), 0, 4)), V(ident, bc3(ident.t[0:64, 0:64], 4, 64, 1)),
                         V(T_("Xa"), v3(T_("Xa"), 0, 4)), SUB)
                    Xc, XTc, Pc = "Xa", "XTa", "Pa"
                    for lvl in range(1, 6):
                        Xn = "Xb" if Xc == "Xa" else "Xa"
                        XTn = "XTb" if XTc == "XTa" else "XTa"
                        Pn = "Pb" if Pc == "Pa" else "Pa"
                        if lvl < 5:
                            yield
                            p1 = nxt("ps", psb)
                            for i in range(4):
                                P.mm(p1[0:64, lc(i)], T_(XTc)[:, lc(i)], T_(Xc)[:, lc(i)], flag=(i == 3))
                        yield
                        p2 = nxt("ps", psb)
                        for i in range(4):
                            P.mm(p2[0:64, lc(i)], T_(Xc)[:, lc(i)], T_(XTc)[:, lc(i)], flag=(i == 3))
                        if lvl < 5:
                            evac(T_(Xn)[:, :], p1[0:64, 0:256])
                        evac(T_(XTn)[:, :], p2[0:64, 0:256])
                        yield
                        p3 = nxt("ps", psb)
                        for i in range(4):
                            P.mm(p3[0:64, lc(i)], T_(XTn)[:, lc(i)], T_(Pc)[:, lc(i)], flag=(i == 3))
                        P.tt(T_(Pn)[:, :], T_(Pc)[:, :], p3[0:64, 0:256], ADD)
                        Xc, XTc, Pc = Xn, XTn, Pn
                    TTn = Pc
                    yield
                    pSv = nxt("ps", psb)
                    for i in range(4):
                        P.mm(pSv[0:64, lc(i)], T_(TTn)[:, lc(i)], Vr[:, hc(i)], flag=(i == 3))
                    evac(T_("solv")[:, :], pSv[0:64, 0:256])
                    yield
                    pSk = nxt("ps", psb)
                    for i in range(4):
                        P.mm(pSk[0:64, lc(i)], RHSk[:, hc(i)], T_(TTn)[:, lc(i)], flag=(i == 3))
                    P.ts(T_("nsolkT")[:, :], pSk[0:64, 0:256], -1.0, None, MUL)
                drive([pre_g(0), pre_g(1)])
                def rec_g(g):
                    T_ = lambda nm: G[(nm, g)]
                    hc = lambda i: slice((g * 4 + i) * 64, (g * 4 + i + 1) * 64)
                    lc = lambda i: slice(i * 64, (i + 1) * 64)
                    gs = slice(g * 256, (g + 1) * 256)
                    yield
                    pU = nxt("ps", psb)
                    for i in range(4):
                        P.mm(pU[0:64, lc(i)], T_("nsolkT")[:, lc(i)], S_sb[:, hc(i)], flag=(i == 3))
                    P.tt(T_("U")[:, :], T_("solv")[:, :], pU[0:64, 0:256], ADD)
                    yield
                    pO = nxt("ps", psb)
                    for i in range(4):
                        P.mm(pO[0:64, lc(i)], T_("qgT")[:, lc(i)], S_sb[:, hc(i)], start=True, stop=False, flag=False)
                        P.mm(pO[0:64, lc(i)], T_("QKD")[:, lc(i)], T_("U")[:, lc(i)], start=False, stop=True, flag=(i == 3))
                    evac(O[:, gs], pO[0:64, 0:256])
                    P.tt(V(T_("Stmp"), v3(T_("Stmp"), 0, 4)), V(S_sb, v3(S_sb, g * 256, 4)),
                         V(sm["eGG"], bc3(sm["eGG"].t[:, 8 + g * 4:8 + g * 4 + 4], 4, 64, 2)), MUL)
                    yield
                    pS = nxt("ps", psb)
                    for i in range(4):
                        P.mm(pS[0:64, lc(i)], Kdec[:, hc(i)], T_("U")[:, lc(i)], flag=(i == 3))
                    P.tt(S_sb[:, gs], T_("Stmp")[:, :], pS[0:64, 0:256], ADD)
                drive([rec_g(0), rec_g(1)])
                P.tt(sqt[:, :], O[:, :], O[:, :], MUL)
                P.reduce(sm["ss8"][:, 0:8], V(sqt, v3(sqt, 0, 8)))
                P.act(sm["ss8"][:, 0:8], sm["ss8"][:, 0:8], AF.Sqrt, bias=eps64, scale=1.0 / 64)
                P.recip(sm["ss8"][:, 0:8], sm["ss8"][:, 0:8])
                P.tt(V(O, v3(O, 0, 8)), V(O, v3(O, 0, 8)), V(sm["ss8"], bc3(sm["ss8"].t[:, 0:8], 8, 64, 2)), MUL)
                P.tt(V(O, v3(O, 0, 8)), V(O, v3(O, 0, 8)), V(dng, bc3(dng.t[:, :], 8, 64, 1)), MUL)
                P.tt(O[:, :], O[:, :], zs[:, :], MUL)
                pT = nxt("ps", psb)
                for c in range(4):
                    P.mm(pT[:, c * 64:(c + 1) * 64], O[:, c * 128:(c + 1) * 128], I64, flag=(c == 3))
                evac(V(om, om.t[:, 0:4, cc:cc + 64]), V(pT, pT.t[:, 0:256].rearrange("p (a b) -> p a b", a=4)))

            for si, (c0, n, s) in enumerate(segs):
                if kind == "s":
                    P.dma("sp", V(S_sb, v3(S_sb, 0, 8)), V(s_delta, s_delta.t[s - 1].rearrange("h k v -> k h v")))
                for k in range(n // 64):
                    chunk(c0 + 64 * k, offs3[si] + 64 * k)
                if kind == "s":
                    P.dma("sp", V(o_s_delta, o_s_delta.t[s - 1].rearrange("h k v -> k h v")), V(S_sb, v3(S_sb, 0, 8)))
                    P.dma("sp", V(o_s_qkv, o_s_qkv.t[:, :].rearrange("p (a q k) -> p a q k", a=12, q=NS)[:, :, s - 1, :]),
                          V(qkv_pre, qkv_pre.t[:, :, offs3[si] + n:offs3[si] + n + 3]))
            if kind == "p":
                P.copy(V(hal3, hal3.t[:, :, :]), V(qkv_pre, qkv_pre.t[:, :, T:T + 3]))
                if last:
                    P.dma("sp", V(o_p_qkv, o_p_qkv.t[:, :].rearrange("p (a k) -> p a k", a=12)), V(hal3, hal3.t[:, :, :]))
                    P.dma("sp", V(o_p_delta, o_p_delta.t[:, :, :].rearrange("h k v -> k h v")), V(S_sb, v3(S_sb, 0, 8)))
            proj_fm(ab_w_out, ab_w_out.t, 0, D, FC, om, T, resid_add(0, 1, segs))

        def sb_mixer(kind, ti, T, segs):
            Wq = sb_w_qkv.t
            ar.reset()
            P.barrier()
            DB = cfg.sbdbg
            qT = ar.alloc("qT", [128, FC, TT], BF16)
            kbf = ar.alloc("kbf", [128, FC, TT], BF16)
            vbf = ar.alloc("vbf", [128, 4, D], BF16)
            mark = ar.off
            kf = ar.alloc("kf", [128, FC, TT])
            vf = ar.alloc("vf", [128, 4, D])
            vblk = 128 if kind == "p" else 64
            nvb = T // vblk

            def c_q(j, pt):
                P.act(qT[:, j, 0:T], pt[:, 0:T], AF.Copy, scale=0.125)

            def c_k(j, pt):
                P.act(kf[:, j, 0:T], pt[:, 0:T], AF.Copy)
                P.copy(kbf[:, j, 0:T], pt[:, 0:T])
            if DB & 1:
                proj_fm(sb_w_qkv, Wq, 0, D, FC, h, T, c_q)
            if DB & 2:
                proj_fm(sb_w_qkv, Wq, D, D, FC, h, T, c_k)
            for half in range(2 if DB & 4 else 0):
                wb = nxt("wp", wp)
                wv = wb.t[:, 0:FC * 512].rearrange("p (kc n) -> p kc n", kc=FC)
                load_w(wb, wv[:, :, :], sb_w_qkv, Wq[:, 2 * D + half * 512:2 * D + (half + 1) * 512].rearrange("(kc p) n -> p kc n", p=128))
                for b in range(nvb):
                    pt = nxt("ps", psb)
                    for kc in range(FC):
                        P.mm(pt[0:vblk, :], h[:, kc, b * vblk:(b + 1) * vblk], V(wb, wv[:, kc, :]),
                             start=(kc == 0), stop=(kc == FC - 1), flag=(kc == FC - 1))
                    P.act(vf[0:vblk, b, half * 512:(half + 1) * 512], pt[0:vblk, :], AF.Copy)
                    P.copy(vbf[0:vblk, b, half * 512:(half + 1) * 512], pt[0:vblk, :])
            t0 = ti * TT
            if not (DB & 8):
                pass
            elif kind == "p":
                P.dma("sp", V(o_p_k, o_p_k.t[:, t0:t0 + T].rearrange("(fc p) t -> p fc t", p=128)), kf[:, :, 0:T])
                P.dma("sp", V(o_p_v, o_p_v.t[t0:t0 + T, :].rearrange("(b p) c -> p b c", p=128)), vf[:, 0:nvb, :])
                if DB & 16:
                    P.dma("sp", V(kT_scr, kT_scr.t[:, t0:t0 + T].rearrange("(fc p) t -> p fc t", p=128)), kbf[:, :, 0:T])
                    P.dma("sp", V(v_scr, v_scr.t[t0:t0 + T, :].rearrange("(b p) c -> p b c", p=128)), vbf[:, 0:nvb, :])
            else:
                P.dma("sp", V(o_s_k, o_s_k.t[:, 0:T].rearrange("(fc p) t -> p fc t", p=128)), kf[:, :, 0:T])
                P.dma("sp", V(o_s_v, o_s_v.t[0:T, :].rearrange("(b p) c -> p b c", p=64)), vf[0:64, 0:nvb, :])

            if not (DB & 96):
                return
            P.barrier()
            ar.reset(mark)
            NPB = 3
            E_ = [ar.alloc("E%d" % i, [128, 512]) for i in range(NPB)]
            SP_ = [ar.alloc("SP%d" % i, [128, 512], BF16) for i in range(NPB)]
            ARG_ = [ar.alloc("ARG%d" % i, [128, 512]) for i in range(NPB)]
            W_ = [ar.alloc("W%d" % i, [128, 512], BF16) for i in range(NPB)]
            R_ = [ar.alloc("R%d" % i, [128, 512]) for i in range(2)]
            kTb = [ar.alloc("kTb%d" % i, [128, 4096], BF16) for i in range(2)]
            vb = [ar.alloc("vb%d" % i, [128, 4, 512], BF16) for i in range(2)]
            if kind == "p":
                qz = [ar.alloc("qz%d" % i, [128, FC, TT], BF16) for i in range(2)]
                for i in range(2):
                    P.memset(V(qz[i], qz[i].t[:, :, :]), 0.0)
                    P.copy(V(qz[i], qz[i].t[i * 64:(i + 1) * 64, :, 0:T]), V(qT, qT.t[i * 64:(i + 1) * 64, :, 0:T]))
            if kind == "s":
                qS = ar.alloc("qS", [64, 16, NS * DS], BF16)
                kS = ar.alloc("kS", [64, 16, NS * DS], BF16)
                for dstb, srcb in ((qS, qT), (kS, kbf)):
                    dv = dstb.t[:, :, :].rearrange("p (c two) t -> p c two t", two=2)
                    P.dma("sp", V(dstb, dv[:, :, 0, :]), V(srcb, srcb.t[0:64, :, 0:T]))
                    P.dma("sp", V(dstb, dv[:, :, 1, :]), V(srcb, srcb.t[64:128, :, 0:T]))
            psr = psb[0:6]
            one_col = lambda kk: V(onesf, onesf.t[0:kk, 0:1])

            def run_stream(blocks):
                nb = len(blocks)
                st_ = {}

                def stA(b):
                    B = blocks[b]
                    kk, GW = B["kk"], B["GW"]
                    pz = nxt("psr", psr)
                    st_[b] = pz
                    hl = B["heads"]
                    for i, (c0, N, kTv, qv, vv, pov, ost) in enumerate(hl):
                        P.mm(pz[0:kk, c0:c0 + N], kTv, qv, start=(i == 0), stop=False, flag=(i == len(hl) - 1), skip=True)
                    E, SP = E_[b % NPB], SP_[b % NPB]
                    P.act(E[0:kk, 0:GW], pz[0:kk, 0:GW], AF.Exp)
                    P.act(SP[0:kk, 0:GW], E[0:kk, 0:GW], AF.Ln, bias=one_col(kk), scale=1.0)
                    if B["mask"] is not None:
                        P.tt(SP[0:kk, 0:GW], SP[0:kk, 0:GW], B["mask"], ALU.mult)

                def stB(b):
                    B = blocks[b]
                    kk, GW = B["kk"], B["GW"]
                    pz = st_[b]
                    SP, ARG, W, R = SP_[b % NPB], ARG_[b % NPB], W_[b % NPB], B["R"]
                    P.mm(pz[0:kk, 0:GW], negincl[0:kk, 0:kk], SP[0:kk, 0:GW], start=False, stop=True, skip=True)
                    pt = None
                    if not B["last"]:
                        pt = nxt("psr", psr)
                        P.mm(pt[:, 0:GW], ones1b[0:kk, :], SP[0:kk, 0:GW])
                    if B["first"]:
                        P.act(W[0:kk, 0:GW], pz[0:kk, 0:GW], AF.Exp)
                    else:
                        P.tt(ARG[0:kk, 0:GW], pz[0:kk, 0:GW], R[0:kk, 0:GW], ALU.subtract)
                        P.act(W[0:kk, 0:GW], ARG[0:kk, 0:GW], AF.Exp)
                    if B["mask"] is not None:
                        P.tt(W[0:kk, 0:GW], W[0:kk, 0:GW], B["mask"], ALU.mult)
                    if pt is not None:
                        if B["first"]:
                            P.copy(R[:, 0:GW], pt[:, 0:GW])
                        else:
                            P.tt(R[:, 0:GW], R[:, 0:GW], pt[:, 0:GW], ALU.add)

                def stC(b):
                    B = blocks[b]
                    kk = B["kk"]
                    W = W_[b % NPB]
                    hl = B["heads"]
                    for i, (c0, N, kTv, qv, vv, pov, ost) in enumerate(hl):
                        P.mm(pov, vv, W[0:kk, c0:c0 + N], start=(B["first"] and ost), stop=B["last"], flag=(i == len(hl) - 1), skip=True)

                for step in range(nb + 4):
                    if step < nb:
                        stA(step)
                    if 0 <= step - 2 < nb:
                        stB(step - 2)
                    if 0 <= step - 4 < nb:
                        stC(step - 4)
                        if "post" in blocks[step - 4]:
                            blocks[step - 4]["post"]()

            if kind == "p" and (DB & 32):
                i = ti
                for c in range(8):
                    po = psb[6 + c % 2]
                    blocks = []
                    last_of = []
                    for j, kb in enumerate(range(i, -1, -1)):
                        kt = kTb[j % 2]
                        vt = vb[j % 2]

                        def load(kt=kt, vt=vt, kb=kb):
                            P.dma("sp", V(kt, kt.t[:, 0:512]), V(kT_scr, kT_scr.t[c * 128:(c + 1) * 128, kb * 512:(kb + 1) * 512]))
                            P.dma("sp", V(vt, vt.t[:, :, 0:128]),
                                  V(v_scr, v_scr.t[kb * 512:(kb + 1) * 512, c * 128:(c + 1) * 128].rearrange("(j p) c -> p j c", p=128)))
                        if j < 2:
                            load()
                        else:
                            last_of[j - 2]["post"] = load
                        for d in range(3, -1, -1):
                            for hh in range(2):
                                pb = hh * 64
                                blocks.append(dict(
                                    kk=128, GW=512,
                                    heads=[(0, 512, V(kt, kt.t[:, d * 128:(d + 1) * 128]),
                                            V(qz[hh], qz[hh].t[:, c, 0:512]),
                                            V(vt, vt.t[:, d, 0:128]),
                                            V(psb[6 + hh], psb[6 + hh].t[:, 0:512]), True)],
                                    mask=(mdiag[d][:, :] if kb == i else None),
                                    first=(kb == i and d == 3), last=(kb == 0 and d == 0), R=R_[hh]))
                        last_of.append(blocks[-1])
                    run_stream(blocks)
                    P.act(V(om, om.t[0:64, c, 0:T]), V(psb[6], psb[6].t[0:64, 0:T]), AF.Copy)
                    P.copy(V(om, om.t[64:128, c, 0:T]), V(psb[7], psb[7].t[64:128, 0:T]))
            if kind == "s" and (DB & 64):
                NKB = PAST // 512
                for si, (c0, n, s) in enumerate(segs):
                    q = s - 1
                    for g in range(2):
                        po = psb[6 + (si * 2 + g) % 2]
                        blocks = []

                        def heads_for(kk, kT_of, v_of):
                            hl = []
                            for gi in range(8):
                                hd = g * 8 + gi
                                pb = (hd % 2) * 64
                                hl.append((gi * 64, 64, kT_of(hd, pb), V(qS, qS.t[0:64, hd, c0:c0 + 64]),
                                           v_of(hd), V(po, po.t[pb:pb + 64, (gi // 2) * 64:(gi // 2 + 1) * 64]), gi < 2))
                            return hl
                        blocks.append(dict(
                            kk=64, GW=512,
                            heads=heads_for(64, lambda hd, pb: V(kS, kS.t[0:64, hd, c0:c0 + 64]),
                                            lambda hd: V(vbf, vbf.t[0:64, si, hd * 64:(hd + 1) * 64])),
                            mask=mnew[:, :], first=True, last=(NKB == 0), R=R_[0]))
                        last_of = []
                        for j, kb in enumerate(range(NKB - 1, -1, -1)):
                            kt = kTb[j % 2]
                            vt = vb[j % 2]
                            ktv = kt.t[0:64, :].rearrange("p (hh k) -> p hh k", hh=8)

                            def load(kt=kt, vt=vt, ktv=ktv, kb=kb):
                                P.dma("pool", V(kt, ktv),
                                      V(ckT_d, ckT_d.t[q, g * 512:(g + 1) * 512, kb * 512:(kb + 1) * 512].rearrange("(hh dd) k -> dd hh k", dd=64)))
                                P.dma("pool", V(vt, vt.t[:, :, :]),
                                      V(cv_d, cv_d.t[q, kb * 512:(kb + 1) * 512, g * 512:(g + 1) * 512].rearrange("(j p) c -> p j c", p=128)))
                            if j < 2:
                                load()
                            else:
                                last_of[j - 2]["post"] = load
                            for d in range(3, -1, -1):
                                blocks.append(dict(
                                    kk=128, GW=512,
                                    heads=heads_for(128, lambda hd, pb, kt=kt, ktv=ktv, d=d: V(kt, ktv[:, hd - 8 * g, d * 128:(d + 1) * 128]),
                                                    lambda hd, vt=vt, d=d: V(vt, vt.t[:, d, (hd - 8 * g) * 64:(hd - 8 * g + 1) * 64])),
                                    mask=None, first=False, last=(kb == 0 and d == 0), R=R_[0]))
                            last_of.append(blocks[-1])
                        run_stream(blocks)
                        evac(V(om, om.t[:, 4 * g:4 * g + 4, c0:c0 + 64]), V(po, po.t[:, 0:256].rearrange("p (a b) -> p a b", a=4)))
            proj_fm(sb_w_out, sb_w_out.t, 0, D, FC, om, T, resid_add(1, 1, segs))

        tiles = []
        for i in range(cfg.ntile):
            tiles.append(("p", i, TT, [(0, TT, 0)], i == cfg.ntile - 1))
        tiles.append(("s", 0, NS * DS, [(j * DS, DS, 1 + j) for j in range(NS)], True))

        for (kind, ti, T, segs, last) in tiles:
            src = xp if kind == "p" else xs
            dst = yp if kind == "p" else ys
            t0 = ti * TT
            P.barrier()
            P.dma("sp", x[:, :, 0:T], V(src, src.t[:, t0:t0 + T].rearrange("(fc p) t -> p fc t", p=128)))
            for l in range(2):
                mod_norm(T, segs, l, 0)
                ffn(T, segs, l, 0, 0)
                if cfg.stage <= 1:
                    break
                mod_norm(T, segs, l, 1)
                if l == 0:
                    ab_mixer(kind, ti, T, segs, last)
                    if cfg.stage <= 2:
                        break
                else:
                    sb_mixer(kind, ti, T, segs)
                    if cfg.stage <= 3:
                        break
                mod_norm(T, segs, l, 2)
                ffn(T, segs, l, 2, 1)
            if cfg.stage >= 99:
                ar.reset()
                yo = Buf(ar.t[:, AR4 - FC * TT:AR4].rearrange("p (a b) -> p a b", a=FC), "yo")
                mod_norm(T, segs, None, None, out_f32=yo)
                P.dma("sp", V(dst, dst.t[:, t0:t0 + T].rearrange("(fc p) t -> p fc t", p=128)), yo[:, :, 0:T])
            else:
                P.dma("sp", V(dst, dst.t[:, t0:t0 + T].rearrange("(fc p) t -> p fc t", p=128)), x[:, :, 0:T])

        P.finish()
        P.run()
        print("instructions:", P.nins, {e: len(P.ops[e]) for e in P.ENG})
    return nc


def host_ab(inp, ssl):
    f = np.ascontiguousarray
    cw = inp["ab_conv_qkv"][0]
    scw = inp["sc_conv"][0]
    sq = inp["state_qkv_conv"][0, ssl]
    ss = inp["state_sconv"][0, ssl]
    return {
        "ab_w_in": inp["ab_w_in"][0], "ab_w_out": inp["ab_w_out"][0],
        "convw": f(cw.reshape(4, 12, 128).transpose(2, 1, 0).reshape(128, 48)),
        "scw": f(scw.reshape(3, 4, 128).transpose(2, 1, 0).reshape(128, 12)),
        "alog": f(np.broadcast_to(inp["dn_A_log"][0][None, :], (64, 8))),
        "dtb": f(np.broadcast_to(inp["dn_dt_bias"][0][None, :], (64, 8))),
        "dng": f(np.broadcast_to(inp["dn_norm_g"][0][None, :], (64, 64))),
        "s_delta": f(inp["state_delta"][0, ssl]),
        "s_qkv": f(sq.reshape(NS, 3, 12, 128).transpose(3, 2, 0, 1).reshape(128, 12 * NS * 3)),
        "s_sconv": f(ss.reshape(NS, 2, 4, 128).transpose(3, 2, 0, 1).reshape(128, 4 * NS * 2)),
    }


def host_sb(inp, ssl, past):
    f = np.ascontiguousarray
    ck = inp["cache_k"][0, ssl, :, :past]
    cv = inp["cache_v"][0, ssl, :, :past]
    return {
        "sb_w_qkv": inp["sb_w_qkv"][0], "sb_w_out": inp["sb_w_out"][0],
        "ckT": f(ck.transpose(0, 1, 3, 2).reshape(ck.shape[0], 1024, past)),
        "cv": f(cv.transpose(0, 2, 1, 3).reshape(cv.shape[0], past, 1024)),
    }


def host_common(inp, b, ssl):
    f = np.ascontiguousarray
    c_all = np.concatenate([inp["c_prompt"][b][None], inp["c_sample"][ssl]], 0)
    cT = c_all.T.reshape(8, 128, NSEQ).transpose(1, 0, 2).reshape(128, 8 * NSEQ)
    normg = inp["norm_g"].reshape(6, 8, 128).transpose(2, 0, 1).reshape(128, 48)
    finalg = inp["final_g"].reshape(8, 128).T
    return {
        "xp": f(inp["x_prompt"][b].T), "xs": f(inp["x_sample"][ssl].reshape(NS * DS, D).T),
        "cT": f(cT), "normg": f(normg), "finalg": f(finalg),
        "ada_w": inp["ada_w"], "ada_b": inp["ada_b"], "ff_w_in": inp["ff_w_in"], "ff_w_out": inp["ff_w_out"],
    }


_NC_CACHE = {}


def kernel(**inputs):
    inp = {k: np.asarray(v, dtype=np.float32) for k, v in inputs.items()}
    SEQ = inp["x_prompt"].shape[1]
    PAST = inp["cache_k"].shape[3]
    key = (SEQ, PAST)
    if key not in _NC_CACHE:
        _NC_CACHE[key] = build(Cfg(seq=SEQ, past=PAST, stage=99))
    nc = _NC_CACHE[key]
    n_cores = 8
    in_maps = []
    for c in range(n_cores):
        b = c // 2
        ssl = slice(NS * c, NS * (c + 1))
        m = host_common(inp, b, ssl)
        m.update(host_ab(inp, ssl))
        m.update(host_sb(inp, ssl, PAST))
        in_maps.append(m)
    res = run_bass_kernel_spmd(nc, in_maps, core_ids=list(range(n_cores)))
    R = res.results
    NB = inp["x_prompt"].shape[0]
    f32 = np.float32
    y_prompt = np.stack([R[2 * b]["yp"].T for b in range(NB)]).astype(f32)
    y_sample = np.concatenate([R[c]["ys"].T.reshape(NS, DS, D) for c in range(n_cores)]).astype(f32)
    p_delta = np.stack([R[2 * b]["o_p_delta"] for b in range(NB)])[None].astype(f32)
    p_qkv = np.stack([R[2 * b]["o_p_qkv"].reshape(128, 12, 3).transpose(2, 1, 0).reshape(3, 1536) for b in range(NB)])[None].astype(f32)
    p_sc = np.stack([R[2 * b]["o_p_sconv"].reshape(128, 4, 2).transpose(2, 1, 0).reshape(2, 512) for b in range(NB)])[None].astype(f32)
    p_k = np.stack([R[2 * b]["o_p_k"].reshape(16, 64, SEQ).transpose(0, 2, 1) for b in range(NB)])[None].astype(f32)
    p_v = np.stack([R[2 * b]["o_p_v"].reshape(SEQ, 16, 64).transpose(1, 0, 2) for b in range(NB)])[None].astype(f32)
    s_delta = np.concatenate([R[c]["o_s_delta"] for c in range(n_cores)])[None].astype(f32)
    s_qkv = np.concatenate([R[c]["o_s_qkv"].reshape(128, 12, NS, 3).transpose(2, 3, 1, 0).reshape(NS, 3, 1536) for c in range(n_cores)])[None].astype(f32)
    s_sc = np.concatenate([R[c]["o_s_sconv"].reshape(128, 4, NS, 2).transpose(2, 3, 1, 0).reshape(NS, 2, 512) for c in range(n_cores)])[None].astype(f32)
    s_k = np.concatenate([R[c]["o_s_k"].reshape(16, 64, NS, DS).transpose(2, 0, 3, 1) for c in range(n_cores)])[None].astype(f32)
    s_v = np.concatenate([R[c]["o_s_v"].reshape(NS, DS, 16, 64).transpose(0, 2, 1, 3) for c in range(n_cores)])[None].astype(f32)
    asc = np.ascontiguousarray
    return tuple(asc(a) for a in (y_prompt, y_sample, p_delta, p_qkv, p_sc, p_k, p_v, s_delta, s_qkv, s_sc, s_k, s_v))
```

```python
import numpy as np
from contextlib import ExitStack
import concourse.bass as bass
import concourse.mybir as mybir
from concourse.bass_utils import run_bass_kernel_spmd

F32 = mybir.dt.float32
BF16 = mybir.dt.bfloat16
AF = mybir.ActivationFunctionType
ALU = mybir.AluOpType

D = 1024
FC = 8
DFF = 2816
HC = 22
NS = 4
DS = 64
NSEQ = 1 + NS
TT = 512
EPS = 1e-6
SAME_SYNC = True
ND = 24


class V:
    __slots__ = ("buf", "ap")

    def __init__(self, buf, ap):
        self.buf = buf
        self.ap = ap


class Buf:
    def __init__(self, t, name="", psum=False):
        self.t = t
        self.name = name
        self.w = None
        self.r = {}
        self.psum = psum
        self.arena = False

    def __getitem__(self, idx):
        return V(self, self.t[idx])

    def v(self, ap):
        return V(self, ap)


class Prog:
    ENG = ["pe", "act", "dve", "pool", "sp"]

    def __init__(self, nc, stack):
        self.nc = nc
        self.stack = stack
        self.ops = {e: [] for e in self.ENG}
        self.sems = []
        for e in self.ENG:
            self.sems.append(stack.enter_context(nc.semaphore("s_" + e)))
        for k in range(ND):
            self.sems.append(stack.enter_context(nc.semaphore("d%d" % k)))
        self.eidx = {e: i for i, e in enumerate(self.ENG)}
        self.cnt = {e: 0 for e in self.ENG}
        self.waited = {e: {} for e in self.ENG}
        self.dma_cum = [0] * ND
        self.dma_rr = 0
        self.dma_rr2 = 0
        self.bar = None
        self.nins = 0

    def _wait(self, e, s, v):
        if v <= 0 or self.waited[e].get(s, 0) >= v:
            return
        self.waited[e][s] = v
        sem = self.sems[s]
        self.ops[e].append(lambda eng, sem=sem, v=v: eng.wait_ge(sem, v))
        self.nins += 1

    def _deps(self, e, reads, writes, own_always=False):
        deps = {}

        def add(s, v):
            if v > deps.get(s, 0):
                deps[s] = v
        for b in reads:
            if b.w:
                add(*b.w)
            if b.psum:
                for s, v in b.r.items():
                    add(s, v)
        for b in writes:
            if b.w:
                add(*b.w)
            for s, v in b.r.items():
                add(s, v)
        own = self.eidx[e]
        for s, v in deps.items():
            if s == own and not own_always and (e == "pe" or not SAME_SYNC):
                continue
            self._wait(e, s, v)

    def emit(self, e, fn, reads=(), writes=(), flag=True):
        reads = [b for b in reads if b is not None]
        writes = [b for b in writes if b is not None]
        self._deps(e, reads, writes)
        own = self.eidx[e]
        if flag:
            self.cnt[e] += 1
            tok = (own, self.cnt[e])
            sem = self.sems[own]
            self.ops[e].append(lambda eng, fn=fn, sem=sem: fn(eng).then_inc(sem, 1))
        else:
            tok = (own, self.cnt[e] + 1)
            self.ops[e].append(lambda eng, fn=fn: fn(eng))
        self.nins += 1
        for b in writes:
            b.w = tok
            b.r = {}
        for b in reads:
            if tok[1] > b.r.get(own, 0):
                b.r[own] = tok[1]

    def dma(self, q, out, in_, **kw):
        reads = [in_.buf]
        writes = [out.buf]
        self._deps(q, reads, writes, own_always=True)
        if q == "pool" and out.buf.arena and self.bar is not None:
            bc, bd = self.bar
            for f in self.ENG:
                if f != "pool":
                    self._wait(q, self.eidx[f], bc[f])
            for kk in range(ND // 2):
                self._wait(q, len(self.ENG) + kk, bd[kk])
        half = ND // 2
        if q == "pool":
            k = half + self.dma_rr2
            self.dma_rr2 = (self.dma_rr2 + 1) % half
        else:
            k = self.dma_rr
            self.dma_rr = (self.dma_rr + 1) % half
        sidx = len(self.ENG) + k
        self._wait(q, sidx, self.dma_cum[k])
        self.dma_cum[k] += 16
        tok = (sidx, self.dma_cum[k])
        sem = self.sems[sidx]
        oa, ia = out.ap, in_.ap
        self.ops[q].append(lambda eng, oa=oa, ia=ia, sem=sem, kw=kw: eng.dma_start(out=oa, in_=ia, **kw).then_inc(sem, 16))
        self.nins += 1
        out.buf.w = tok
        out.buf.r = {}
        if tok[1] > in_.buf.r.get(sidx, 0):
            in_.buf.r[sidx] = tok[1]

    def barrier(self):
        self.bar = (dict(self.cnt), list(self.dma_cum))
        for e in ("pe", "act", "dve", "sp"):
            for f in self.ENG:
                if f != e:
                    self._wait(e, self.eidx[f], self.cnt[f])
            for k in range(ND // 2):
                self._wait(e, len(self.ENG) + k, self.dma_cum[k])

    def finish(self):
        for k in range(ND):
            self._wait("sp", len(self.ENG) + k, self.dma_cum[k])
        for f in self.ENG:
            if f != "sp":
                self._wait("sp", self.eidx[f], self.cnt[f])

    def run(self):
        nc = self.nc
        ops = self.ops
        with nc.Block() as block:
            @block.tensor
            def _(eng):
                for f in ops["pe"]:
                    f(eng)

            @block.scalar
            def _(eng):
                for f in ops["act"]:
                    f(eng)

            @block.vector
            def _(eng):
                for f in ops["dve"]:
                    f(eng)

            @block.gpsimd
            def _(eng):
                for f in ops["pool"]:
                    f(eng)

            @block.sync
            def _(eng):
                for f in ops["sp"]:
                    f(eng)

    def mm(self, out, lhsT, rhs, start=True, stop=True, flag=True, skip=False):
        self.emit("pe", lambda eng, o=out.ap, l=lhsT.ap, r=rhs.ap, st=start, sp=stop, sk=skip:
                  eng.matmul(o, l, r, start=st, stop=sp, skip_group_check=sk),
                  reads=[lhsT.buf, rhs.buf], writes=[out.buf], flag=flag)

    def act(self, out, in_, func, bias=None, scale=None, e="act"):
        kw = {}
        rd = [in_.buf]
        if bias is not None:
            if isinstance(bias, V):
                kw["bias"] = bias.ap
                rd.append(bias.buf)
            else:
                kw["bias"] = bias
        if scale is not None:
            if isinstance(scale, V):
                kw["scale"] = scale.ap
                rd.append(scale.buf)
            else:
                kw["scale"] = scale
        self.emit("act", lambda eng, o=out.ap, i=in_.ap, f=func, kw=kw: eng.activation(o, i, f, **kw),
                  reads=rd, writes=[out.buf])

    def tt(self, out, in0, in1, op, e="dve"):
        self.emit(e, lambda eng, o=out.ap, a=in0.ap, b=in1.ap, op=op: eng.tensor_tensor(o, a, b, op),
                  reads=[in0.buf, in1.buf], writes=[out.buf])

    def stt(self, out, in0, scalar, in1, op0, op1, e="dve"):
        rd = [in0.buf, in1.buf]
        if isinstance(scalar, V):
            rd.append(scalar.buf)
            sc = scalar.ap
        else:
            sc = scalar
        self.emit(e, lambda eng, o=out.ap, a=in0.ap, s=sc, b=in1.ap, op0=op0, op1=op1:
                  eng.scalar_tensor_tensor(o, a, s, b, op0, op1),
                  reads=rd, writes=[out.buf])

    def ts(self, out, in0, s1, s2, op0, op1=None, e="dve"):
        rd = [in0.buf]
        a1 = s1
        a2 = s2
        if isinstance(s1, V):
            rd.append(s1.buf)
            a1 = s1.ap
        if isinstance(s2, V):
            rd.append(s2.buf)
            a2 = s2.ap
        if op1 is None:
            self.emit(e, lambda eng, o=out.ap, a=in0.ap, a1=a1, op0=op0: eng.tensor_scalar(o, a, a1, None, op0),
                      reads=rd, writes=[out.buf])
        else:
            self.emit(e, lambda eng, o=out.ap, a=in0.ap, a1=a1, a2=a2, op0=op0, op1=op1:
                      eng.tensor_scalar(o, a, a1, a2, op0, op1),
                      reads=rd, writes=[out.buf])

    def copy(self, out, in_, e="dve"):
        self.emit(e, lambda eng, o=out.ap, i=in_.ap: eng.tensor_copy(o, i), reads=[in_.buf], writes=[out.buf])

    def memset(self, out, val, e="dve"):
        self.emit(e, lambda eng, o=out.ap, v=val: eng.memset(o, v), reads=[], writes=[out.buf])

    def reduce(self, out, in_, op=None):
        self.emit("dve", lambda eng, o=out.ap, i=in_.ap: eng.tensor_reduce(o, i, mybir.AxisListType.X, ALU.add),
                  reads=[in_.buf], writes=[out.buf])

    def recip(self, out, in_):
        self.emit("dve", lambda eng, o=out.ap, i=in_.ap: eng.reciprocal(o, i), reads=[in_.buf], writes=[out.buf])

    def aselect(self, out, in_, pattern, cmp, fill, base, cm):
        self.emit("pool", lambda eng, o=out.ap, i=in_.ap: eng.affine_select(o, i, pattern, cmp, fill, base=base, channel_multiplier=cm),
                  reads=[in_.buf], writes=[out.buf])


class Cfg:
    def __init__(self, seq=4096, past=4096, stage=99, sbdbg=255):
        self.sbdbg = sbdbg
        self.seq = seq
        self.past = past
        self.stage = stage
        self.ntile = seq // TT


class Arena:
    def __init__(self, t, n4):
        self.t = t
        self.n4 = n4
        self.off = 0

    def reset(self, off=0):
        self.off = off

    def alloc(self, name, shape, dt=F32):
        n = 1
        for d in shape[1:]:
            n *= d
        n4 = n if dt == F32 else (n + 1) // 2
        n4 = (n4 + 1) // 2 * 2
        o = self.off
        self.off += n4
        assert self.off <= self.n4, ("arena overflow", name, self.off, self.n4)
        ap = self.t[0:shape[0], o:o + n4]
        if dt != F32:
            ap = ap.bitcast(dt)
        ap = ap[:, 0:n]
        if len(shape) == 3:
            ap = ap.rearrange("p (a b) -> p a b", a=shape[1])
        bf = Buf(ap, name)
        bf.arena = True
        return bf


def build(cfg):
    nc = bass.Bass("TRN2", target_bir_lowering=False)
    SEQ = cfg.seq
    st = ExitStack()
    with st:
        P = Prog(nc, st)

        def dram_in(name, shape, dt=F32):
            return Buf(nc.dram_tensor(name, list(shape), dt, kind="ExternalInput").ap(), name)

        def dram_out(name, shape, dt=F32):
            return Buf(nc.dram_tensor(name, list(shape), dt, kind="ExternalOutput").ap(), name)

        def sb(name, shape, dt=F32):
            return Buf(st.enter_context(nc.sbuf_tensor(name, list(shape), dt)), name)

        def ps(name, shape, dt=F32):
            return Buf(st.enter_context(nc.psum_tensor(name, list(shape), dt)), name, psum=True)

        xp = dram_in("xp", [D, SEQ])
        xs = dram_in("xs", [D, NS * DS])
        cT = dram_in("cT", [128, FC * NSEQ])
        normg = dram_in("normg", [128, 6 * FC])
        finalg = dram_in("finalg", [128, FC])
        ada_w = dram_in("ada_w", [2, D, 9 * D])
        ada_b = dram_in("ada_b", [2, 9 * D])
        ff_w_in = dram_in("ff_w_in", [2, 2, D, 2 * DFF])
        ff_w_out = dram_in("ff_w_out", [2, 2, DFF, D])
        ab_w_in = dram_in("ab_w_in", [D, 3600])
        ab_w_out = dram_in("ab_w_out", [D, D])
        convw_d = dram_in("convw", [128, 12 * 4])
        scw_d = dram_in("scw", [128, 4 * 3])
        alog_d = dram_in("alog", [64, 8])
        dtb_d = dram_in("dtb", [64, 8])
        dng_d = dram_in("dng", [64, 64])
        s_delta = dram_in("s_delta", [NS, 8, 64, 64])
        s_qkv = dram_in("s_qkv", [128, 12 * NS * 3])
        s_sconv = dram_in("s_sconv", [128, 4 * NS * 2])
        sb_w_qkv = dram_in("sb_w_qkv", [D, 3 * D])
        sb_w_out = dram_in("sb_w_out", [D, D])
        PAST = cfg.past
        ckT_d = dram_in("ckT", [NS, D, PAST])
        cv_d = dram_in("cv", [NS, PAST, D])
        kT_scr = Buf(nc.dram_tensor("kT_scr", [D, SEQ], BF16, kind="ExternalOutput").ap(), "kT_scr")
        v_scr = Buf(nc.dram_tensor("v_scr", [SEQ, D], BF16, kind="ExternalOutput").ap(), "v_scr")
        o_p_k = dram_out("o_p_k", [D, SEQ])
        o_p_v = dram_out("o_p_v", [SEQ, D])
        o_s_k = dram_out("o_s_k", [D, NS * DS])
        o_s_v = dram_out("o_s_v", [NS * DS, D])
        yp = dram_out("yp", [D, SEQ])
        ys = dram_out("ys", [D, NS * DS])
        o_p_delta = dram_out("o_p_delta", [8, 64, 64])
        o_s_delta = dram_out("o_s_delta", [NS, 8, 64, 64])
        o_p_qkv = dram_out("o_p_qkv", [128, 12 * 3])
        o_s_qkv = dram_out("o_s_qkv", [128, 12 * NS * 3])
        o_p_sconv = dram_out("o_p_sconv", [128, 4 * 2])
        o_s_sconv = dram_out("o_s_sconv", [128, 4 * NS * 2])

        ident = sb("ident", [128, 128])
        ones_bf = sb("ones_bf", [128, 128], BF16)
        onesf = sb("onesf", [128, 128])
        identb = sb("identb", [64, 64], BF16)
        maskU = sb("maskU", [64, 64])
        maskL = sb("maskL", [64, 64])
        maskS = sb("maskS", [64, 64])
        condT = sb("condT", [128, FC * NSEQ])
        epsb = sb("epsb", [128, 1])
        g_sb = sb("g_sb", [128, 6 * FC])
        fg_sb = sb("fg_sb", [128, FC])
        modT = sb("modT", [128, 2 * 72 * NSEQ])
        gsT = sb("gsT", [128, 6 * FC * NSEQ])
        convw = sb("convw_s", [128, 12, 4])
        scw = sb("scw_s", [128, 4, 3])
        negA = sb("negA", [64, 8])
        dtb = sb("dtb_s", [64, 8])
        dng = sb("dng_s", [64, 64])
        ones512 = sb("ones512", [128, 512], BF16)
        negincl = sb("negincl", [128, 128], BF16)
        ones1b = sb("ones1b", [128, 128], BF16)
        mdiag = [sb("mdiag%d" % d, [128, 512], BF16) for d in range(4)]
        mnew = sb("mnew", [64, 512], BF16)
        hal3 = sb("hal3", [128, 12, 3])
        hal2 = sb("hal2", [128, 4, 2])
        S_sb = sb("S_sb", [64, 512])
        x = sb("x", [128, FC, TT])
        h = sb("h", [128, FC, TT], BF16)
        om = sb("om", [128, FC, TT], BF16)
        WPN = 2
        wp = [sb("wp%d" % i, [128, HC * 512], BF16) for i in range(WPN)]
        modrow = [sb("modrow%d" % i, [8, 512]) for i in range(2)]
        biasrow = [sb("biasrow%d" % i, [8, 512]) for i in range(2)]
        AR4 = 24576
        ar = Arena(st.enter_context(nc.sbuf_tensor("arena", [128, AR4], F32)), AR4)
        psb = [ps("ps%d" % i, [128, 512]) for i in range(8)]
        rr = {"ps": 0, "wp": 0, "mr": 0, "alt": 0, "psr": 0, "kTb": 0}

        def nxt(key, lst):
            i = rr[key]
            rr[key] = (i + 1) % len(lst)
            return lst[i]

        def drive(gens):
            gens = list(gens)
            while gens:
                for gg in list(gens):
                    try:
                        next(gg)
                    except StopIteration:
                        gens.remove(gg)

        def evac(out, in_):
            rr["alt"] ^= 1
            if rr["alt"]:
                P.act(out, in_, AF.Copy)
            else:
                P.copy(out, in_)

        P.memset(onesf[:, :], 1.0, e="pool")
        P.memset(epsb[:, :], EPS, e="pool")
        P.memset(ones_bf[:, :], 1.0 / D, e="pool")
        P.aselect(ident[:, :], onesf[:, :], [[-1, 128]], ALU.is_equal, 0.0, 0, 1)
        P.copy(identb[:, :], ident[0:64, 0:64])
        P.aselect(maskU[:, :], onesf[0:64, 0:64], [[1, 64]], ALU.is_ge, 0.0, 0, -1)
        P.aselect(maskL[:, :], onesf[0:64, 0:64], [[-1, 64]], ALU.is_gt, 0.0, 0, 1)
        P.aselect(maskS[:, :], onesf[0:64, 0:64], [[1, 64]], ALU.is_gt, 0.0, 0, -1)
        P.memset(ones512[:, :], 1.0, e="pool")
        P.memset(ones1b[:, :], 1.0, e="pool")
        P.memset(negincl[:, :], -1.0, e="pool")
        P.aselect(negincl[:, :], negincl[:, :], [[-1, 128]], ALU.is_ge, 0.0, 0, 1)
        for d in range(4):
            P.aselect(mdiag[d][:, :], ones512[:, :], [[1, 512]], ALU.is_gt, 0.0, -128 * d, -1)
        P.aselect(V(mnew, mnew.t[:, :].rearrange("p (a b) -> p a b", a=8)),
                  V(ones512, ones512.t[0:64, :].rearrange("p (a b) -> p a b", a=8)), [[0, 8], [1, 64]], ALU.is_gt, 0.0, 0, -1)
        P.dma("sp", condT[:, :], cT[:, :])
        P.dma("sp", g_sb[:, :], normg[:, :])
        P.dma("sp", fg_sb[:, :], finalg[:, :])
        P.dma("sp", V(convw, convw.t[:, :, :]), V(convw_d, convw_d.t[:, :].rearrange("p (a b) -> p a b", a=12)))
        P.dma("sp", V(scw, scw.t[:, :, :]), V(scw_d, scw_d.t[:, :].rearrange("p (a b) -> p a b", a=4)))
        P.dma("sp", negA[:, :], alog_d[:, :])
        P.dma("sp", dtb[:, :], dtb_d[:, :])
        P.dma("sp", dng[:, :], dng_d[:, :])
        P.act(condT[:, :], condT[:, :], AF.Silu)
        P.act(negA[:, :], negA[:, :], AF.Exp)
        P.ts(negA[:, :], negA[:, :], -1.0, None, ALU.mult)
        P.memset(V(hal3, hal3.t[:, :, :]), 0.0)
        P.memset(V(hal2, hal2.t[:, :, :]), 0.0)
        P.memset(S_sb[:, :], 0.0)

        def mod_idx(l, chunk):
            return (l * 72 + chunk) * NSEQ

        for l in range(2):
            for cb in range(18):
                wb = nxt("wp", wp)
                wv = wb.t[:, 0:8192].bitcast(F32)
                src = ada_w.t[l, :, cb * 512:(cb + 1) * 512].rearrange("(kc p) n -> p kc n", p=128)
                P.dma("sp", V(wb, wv.rearrange("p (kc n) -> p kc n", kc=FC)), V(ada_w, src))
                br = nxt("mr", biasrow)
                mr = modrow[biasrow.index(br)]
                P.dma("sp", br[0:NSEQ, :], V(ada_b, ada_b.t[l, cb * 512:(cb + 1) * 512].partition_broadcast(NSEQ)))
                pt = nxt("ps", psb)
                for kc in range(FC):
                    P.mm(pt[0:NSEQ, :], V(condT, condT.t[:, kc * NSEQ:(kc + 1) * NSEQ]),
                         V(wb, wv[:, kc * 512:(kc + 1) * 512]), start=(kc == 0), stop=(kc == FC - 1),
                         flag=(kc == FC - 1))
                P.tt(mr[0:NSEQ, :], pt[0:NSEQ, :], br[0:NSEQ, :], ALU.add)
                pt2 = nxt("ps", psb)
                for j in range(4):
                    P.mm(pt2[:, j * NSEQ:(j + 1) * NSEQ], mr[0:NSEQ, j * 128:(j + 1) * 128],
                         ident[0:NSEQ, 0:NSEQ], flag=(j == 3))
                c0 = mod_idx(l, cb * 4)
                P.copy(modT[:, c0:c0 + 4 * NSEQ], pt2[:, 0:4 * NSEQ])

        def mod_ap(l, sub, kind, fc, seq):
            c0 = mod_idx(l, (sub * 3 + kind) * 8 + fc) + seq
            return V(modT, modT.t[:, c0:c0 + 1])

        def gs_ap(l, sub, fc, seq):
            c0 = ((l * 3 + sub) * FC + fc) * NSEQ + seq
            return V(gsT, gsT.t[:, c0:c0 + 1])

        for l in range(2):
            for sub in range(3):
                c0 = mod_idx(l, (sub * 3 + 1) * 8)
                o0 = (l * 3 + sub) * FC * NSEQ
                gv = g_sb.t[:, (l * 3 + sub) * FC:(l * 3 + sub + 1) * FC].unsqueeze(2).broadcast_to([128, FC, NSEQ])
                P.stt(V(gsT, gsT.t[:, o0:o0 + FC * NSEQ].rearrange("p (f s) -> p f s", s=NSEQ)),
                      V(modT, modT.t[:, c0:c0 + FC * NSEQ].rearrange("p (f s) -> p f s", s=NSEQ)),
                      1.0, V(g_sb, gv), ALU.add, ALU.mult)
                if sub != 1:
                    c2 = mod_idx(l, (sub * 3 + 2) * 8)
                    P.ts(modT[:, c2:c2 + FC * NSEQ], modT[:, c2:c2 + FC * NSEQ], 0.5, None, ALU.mult)

        def mod_norm(T, segs, l, sub, out_f32=None):
            ar.reset()
            P.barrier()
            sq = ar.alloc("sq", [128, FC, TT], BF16)
            rstd = ar.alloc("rstd", [128, TT])
            tmpn = [ar.alloc("tmpn%d" % i, [128, TT]) for i in range(2)]
            for fc in range(FC):
                P.act(sq[:, fc, 0:T], x[:, fc, 0:T], AF.Square)
            pt = nxt("ps", psb)
            for fc in range(FC):
                P.mm(pt[:, 0:T], ones_bf[:, :], sq[:, fc, 0:T], start=(fc == 0), stop=(fc == FC - 1),
                     flag=(fc == FC - 1))
            P.act(rstd[:, 0:T], pt[:, 0:T], AF.Sqrt, bias=V(epsb, epsb.t[:, 0:1]), scale=1.0)
            P.recip(rstd[:, 0:T], rstd[:, 0:T])
            for fc in range(FC):
                tn = tmpn[fc % 2]
                P.tt(tn[:, 0:T], x[:, fc, 0:T], rstd[:, 0:T], ALU.mult)
                for (c0, n, s) in segs:
                    if l is None:
                        P.act(out_f32[:, fc, c0:c0 + n], tn[:, c0:c0 + n], AF.Identity, scale=V(fg_sb, fg_sb.t[:, fc:fc + 1]))
                    else:
                        P.act(h[:, fc, c0:c0 + n], tn[:, c0:c0 + n], AF.Identity,
                              bias=mod_ap(l, sub, 0, fc, s), scale=gs_ap(l, sub, fc, s))

        def load_w(dst_buf, dst_ap, src_buf, src_ap):
            P.dma("pool", V(dst_buf, dst_ap), V(src_buf, src_ap))

        def proj_fm(W2d_buf, W2d, col0, ncols, KC, rhs, T, consume):
            c = 0
            while c < ncols:
                w = min(512, ncols - c)
                wb = nxt("wp", wp)
                wv = wb.t[:, 0:KC * 512].rearrange("p (kc n) -> p kc n", kc=KC)
                load_w(wb, wv[:, :, 0:w], W2d_buf, W2d[:, col0 + c:col0 + c + w].rearrange("(kc p) n -> p kc n", p=128))
                for j in range(w // 128):
                    pt = nxt("ps", psb)
                    for kc in range(KC):
                        P.mm(pt[:, 0:T], V(wb, wv[:, kc, j * 128:(j + 1) * 128]), rhs[:, kc, 0:T],
                             start=(kc == 0), stop=(kc == KC - 1), flag=(kc == KC - 1))
                    consume(c // 128 + j, pt)
                c += w

        def ffn(T, segs, l, sub, fi):
            w_in = ff_w_in.t[l, fi]
            w_out = ff_w_out.t[l, fi]
            ar.reset()
            P.barrier()
            hid = ar.alloc("hid", [128, HC, TT], BF16)
            sg = [ar.alloc("sg%d" % i, [128, TT]) for i in range(2)]
            c0 = 0
            k = 0
            while c0 < DFF:
                w = min(512, DFF - c0)
                wb = nxt("wp", wp)
                gview = wb.t[:, 0:FC * 512].rearrange("p (kc n) -> p kc n", kc=FC)
                uview = wb.t[:, FC * 512:2 * FC * 512].rearrange("p (kc n) -> p kc n", kc=FC)
                load_w(wb, gview[:, :, 0:w], ff_w_in, w_in[:, c0:c0 + w].rearrange("(kc p) n -> p kc n", p=128))
                load_w(wb, uview[:, :, 0:w], ff_w_in, w_in[:, DFF + c0:DFF + c0 + w].rearrange("(kc p) n -> p kc n", p=128))
                for j in range(w // 128):
                    pg = nxt("ps", psb)
                    pu = nxt("ps", psb)
                    for kc in range(FC):
                        P.mm(pg[:, 0:T], V(wb, gview[:, kc, j * 128:(j + 1) * 128]), h[:, kc, 0:T],
                             start=(kc == 0), stop=(kc == FC - 1), flag=(kc == FC - 1))
                    for kc in range(FC):
                        P.mm(pu[:, 0:T], V(wb, uview[:, kc, j * 128:(j + 1) * 128]), h[:, kc, 0:T],
                             start=(kc == 0), stop=(kc == FC - 1), flag=(kc == FC - 1))
                    s_ = sg[k % 2]
                    k += 1
                    P.act(s_[:, 0:T], pg[:, 0:T], AF.Silu)
                    P.tt(hid[:, c0 // 128 + j, 0:T], s_[:, 0:T], pu[:, 0:T], ALU.mult)
                c0 += w
            for half in range(2):
                wb = nxt("wp", wp)
                wv = wb.t[:, 0:HC * 512].rearrange("p (kc n) -> p kc n", kc=HC)
                for k0 in range(0, HC, 11):
                    load_w(wb, wv[:, k0:k0 + 11, :], ff_w_out,
                           w_out[k0 * 128:(k0 + 11) * 128, half * 512:(half + 1) * 512].rearrange("(kc p) n -> p kc n", p=128))
                for m in range(4):
                    po = nxt("ps", psb)
                    for kc in range(HC):
                        P.mm(po[:, 0:T], V(wb, wv[:, kc, m * 128:(m + 1) * 128]), hid[:, kc, 0:T],
                             start=(kc == 0), stop=(kc == HC - 1), flag=(kc == HC - 1))
                    fc = half * 4 + m
                    for (c0, n, s) in segs:
                        P.stt(x[:, fc, c0:c0 + n], po[:, c0:c0 + n], mod_ap(l, sub, 2, fc, s), x[:, fc, c0:c0 + n],
                              ALU.mult, ALU.add)

        def resid_add(l, sub, segs):
            def consume(j, pt):
                for (c0, n, s) in segs:
                    P.stt(x[:, j, c0:c0 + n], pt[:, c0:c0 + n], mod_ap(l, sub, 2, j, s), x[:, j, c0:c0 + n],
                          ALU.mult, ALU.add)
            return consume

        def bc3(v_ap, n_mid, n_in, axis):
            if axis == 2:
                return v_ap.unsqueeze(2).broadcast_to([v_ap.shape[0], n_mid, n_in])
            return v_ap.unsqueeze(1).broadcast_to([v_ap.shape[0], n_mid, n_in])

        def v3(buf, lo, nh, w=64):
            return buf.t[:, lo:lo + nh * w].rearrange("p (a b) -> p a b", a=nh)

        def ab_mixer(kind, ti, T, segs, last):
            W = ab_w_in.t
            ar.reset()
            P.barrier()
            offs3, offs2 = [], []
            o3 = o2 = 0
            for (c0, n, s) in segs:
                offs3.append(o3)
                offs2.append(o2)
                o3 += 3 + n
                o2 += 2 + n
            L3, L2 = o3, o2
            qkv_pre = ar.alloc("qkv_pre", [128, 12, L3])
            mark = ar.off
            sBs = ar.alloc("sBs", [128, 4, TT])
            sCs = ar.alloc("sCs", [128, 4, TT])
            scx = ar.alloc("scx", [128, 4, L2])
            acc = [ar.alloc("acc%d" % i, [128, L2]) for i in range(2)]

            if kind == "p":
                P.copy(V(qkv_pre, qkv_pre.t[:, :, 0:3]), V(hal3, hal3.t[:, :, :]))
                P.copy(V(scx, scx.t[:, :, 0:2]), V(hal2, hal2.t[:, :, :]))
            else:
                for si, (c0, n, s) in enumerate(segs):
                    q = s - 1
                    P.dma("sp", V(qkv_pre, qkv_pre.t[:, :, offs3[si]:offs3[si] + 3]),
                          V(s_qkv, s_qkv.t[:, :].rearrange("p (a q k) -> p a q k", a=12, q=NS)[:, :, q, :]))
                    P.dma("sp", V(scx, scx.t[:, :, offs2[si]:offs2[si] + 2]),
                          V(s_sconv, s_sconv.t[:, :].rearrange("p (a q k) -> p a q k", a=4, q=NS)[:, :, q, :]))

            def c_qkv(j, pt):
                for si, (c0, n, s) in enumerate(segs):
                    evac(V(qkv_pre, qkv_pre.t[:, j, offs3[si] + 3:offs3[si] + 3 + n]), pt[:, c0:c0 + n])
            proj_fm(ab_w_in, W, 0, 1536, FC, h, T, c_qkv)

            def c_sB(j, pt):
                evac(sBs[:, j, 0:T], pt[:, 0:T])

            def c_sC(j, pt):
                evac(sCs[:, j, 0:T], pt[:, 0:T])

            def c_sx(j, pt):
                for si, (c0, n, s) in enumerate(segs):
                    P.tt(V(scx, scx.t[:, j, offs2[si] + 2:offs2[si] + 2 + n]), sCs[:, j, c0:c0 + n], pt[:, c0:c0 + n], ALU.mult)
            proj_fm(ab_w_in, W, 2064, 512, FC, h, T, c_sB)
            proj_fm(ab_w_in, W, 2576, 512, FC, h, T, c_sC)
            proj_fm(ab_w_in, W, 3088, 512, FC, h, T, c_sx)
            for j in range(4):
                a_ = acc[j % 2]
                P.ts(a_[:, 0:L2 - 2], V(scx, scx.t[:, j, 0:L2 - 2]), V(scw, scw.t[:, j, 0:1]), None, ALU.mult)
                P.stt(a_[:, 0:L2 - 2], V(scx, scx.t[:, j, 1:L2 - 1]), V(scw, scw.t[:, j, 1:2]), a_[:, 0:L2 - 2], ALU.mult, ALU.add)
                P.stt(a_[:, 0:L2 - 2], V(scx, scx.t[:, j, 2:L2]), V(scw, scw.t[:, j, 2:3]), a_[:, 0:L2 - 2], ALU.mult, ALU.add)
                for si, (c0, n, s) in enumerate(segs):
                    P.tt(om[:, 4 + j, c0:c0 + n], sBs[:, j, c0:c0 + n], a_[:, offs2[si]:offs2[si] + n], ALU.mult)
            if kind == "p":
                P.copy(V(hal2, hal2.t[:, :, :]), V(scx, scx.t[:, :, T:T + 2]))
                if last:
                    P.dma("sp", V(o_p_sconv, o_p_sconv.t[:, :].rearrange("p (a k) -> p a k", a=4)), V(hal2, hal2.t[:, :, :]))
            else:
                for si, (c0, n, s) in enumerate(segs):
                    q = s - 1
                    P.dma("sp", V(o_s_sconv, o_s_sconv.t[:, :].rearrange("p (a q k) -> p a q k", a=4, q=NS)[:, :, q, :]),
                          V(scx, scx.t[:, :, offs2[si] + n:offs2[si] + n + 2]))

            P.barrier()
            ar.reset(mark)
            f = lambda nm, shp: ar.alloc(nm, shp)
            cacc = f("cacc", [128, 12, 64])
            ctmp = f("ctmp", [128, 12, 64])
            qkvc = f("qkvc", [128, 12, 64])
            Qr, Kr, Vr, Kb, Qg, Kdec, zs, sqt, O = [f(nm, [64, 512]) for nm in
                                                     ("Qr", "Kr", "Vr", "Kb", "Qg", "Kdec", "zs", "sqt", "O")]
            RHSk = ar.alloc("RHSk", [64, 512], BF16)
            RHSvb = ar.alloc("RHSvb", [64, 512], BF16)
            sm = {nm: f(nm, [64, 16]) for nm in ("ssq", "rn", "ab", "t8", "g8", "b8", "beta", "Gs", "eGG", "dG", "eGe", "ss8")}
            G = {}
            for g in range(2):
                for nm in ("kT", "kbT", "qT", "qgT", "rhsE", "E", "EmS", "EmI", "Xa", "Xb", "XTa", "XTb", "Pa", "Pb",
                           "QKD", "solv", "nsolkT", "U", "Stmp"):
                    G[(nm, g)] = ar.alloc(nm + str(g), [64, 256], BF16 if nm[0] in "XP" else F32)
            zwb = nxt("wp", wp)
            zw = zwb.t[:, 0:FC * 528].rearrange("p (kc n) -> p kc n", kc=FC)
            load_w(zwb, zw[:, :, :], ab_w_in, W[:, 1536:2064].rearrange("(kc p) n -> p kc n", p=128))
            I64 = V(ident, ident.t[0:64, 0:64])
            one64 = V(onesf, onesf.t[0:64, 0:1])
            eps64 = V(epsb, epsb.t[0:64, 0:1])
            MUL, ADD, SUB = ALU.mult, ALU.add, ALU.subtract

            def chunk(cc, pc):
                def pre(k):
                    return V(qkv_pre, qkv_pre.t[:, :, pc + k:pc + k + 64])

                def wv(k):
                    return V(convw, convw.t[:, :, k:k + 1].broadcast_to([128, 12, 64]))
                A3 = lambda b: V(b, b.t[:, :, :])
                P.tt(A3(cacc), pre(0), wv(0), MUL)
                for k in range(1, 4):
                    P.tt(A3(ctmp), pre(k), wv(k), MUL)
                    P.tt(A3(cacc), A3(cacc), A3(ctmp), ADD)
                P.act(A3(qkvc), A3(cacc), AF.Silu)
                for dst, base in ((Qr, 0), (Kr, 4), (Vr, 8)):
                    pt = nxt("ps", psb)
                    for j in range(4):
                        P.mm(pt[0:64, j * 128:(j + 1) * 128], V(qkvc, qkvc.t[:, base + j, :]), ident[:, :], flag=(j == 3))
                    evac(dst[:, :], pt[0:64, :])
                pz = nxt("ps", psb)
                pab = nxt("ps", psb)
                for kc in range(FC):
                    P.mm(pz[0:64, :], h[:, kc, cc:cc + 64], V(zwb, zw[:, kc, 0:512]), start=(kc == 0), stop=(kc == FC - 1),
                         flag=(kc == FC - 1))
                for kc in range(FC):
                    P.mm(pab[0:64, 0:16], h[:, kc, cc:cc + 64], V(zwb, zw[:, kc, 512:528]), start=(kc == 0),
                         stop=(kc == FC - 1), flag=(kc == FC - 1))
                P.act(zs[:, :], pz[0:64, :], AF.Silu)
                P.copy(sm["ab"][:, :], pab[0:64, 0:16])
                P.tt(sqt[:, :], Qr[:, :], Qr[:, :], MUL)
                P.reduce(sm["ssq"][:, 0:8], V(sqt, v3(sqt, 0, 8)))
                P.tt(sqt[:, :], Kr[:, :], Kr[:, :], MUL)
                P.reduce(sm["ssq"][:, 8:16], V(sqt, v3(sqt, 0, 8)))
                P.act(sm["rn"][:, :], sm["ssq"][:, :], AF.Sqrt, bias=eps64, scale=1.0)
                P.recip(sm["rn"][:, :], sm["rn"][:, :])
                P.ts(sm["rn"][:, 0:8], sm["rn"][:, 0:8], 0.125, None, MUL)
                P.tt(V(Qr, v3(Qr, 0, 8)), V(Qr, v3(Qr, 0, 8)), V(sm["rn"], bc3(sm["rn"].t[:, 0:8], 8, 64, 2)), MUL)
                P.tt(V(Kr, v3(Kr, 0, 8)), V(Kr, v3(Kr, 0, 8)), V(sm["rn"], bc3(sm["rn"].t[:, 8:16], 8, 64, 2)), MUL)
                P.tt(sm["t8"][:, 0:8], sm["ab"][:, 0:8], dtb[:, :], ADD)
                P.act(sm["t8"][:, 0:8], sm["t8"][:, 0:8], AF.Exp)
                P.act(sm["t8"][:, 0:8], sm["t8"][:, 0:8], AF.Ln, bias=one64, scale=1.0)
                P.tt(sm["g8"][:, 0:8], sm["t8"][:, 0:8], negA[:, :], MUL)
                P.act(sm["b8"][:, 0:8], sm["ab"][:, 8:16], AF.Exp, scale=-1.0)
                P.ts(sm["b8"][:, 0:8], sm["b8"][:, 0:8], 1.0, None, ADD)
                P.recip(sm["beta"][:, 0:8], sm["b8"][:, 0:8])
                pg = nxt("ps", psb)
                P.mm(pg[0:64, 0:8], maskU[:, :], sm["g8"][:, 0:8], flag=False)
                P.mm(pg[0:64, 8:16], onesf[0:64, 0:64], sm["g8"][:, 0:8])
                P.copy(sm["Gs"][:, :], pg[0:64, 0:16])
                P.act(sm["eGG"][:, :], sm["Gs"][:, :], AF.Exp)
                P.tt(sm["dG"][:, 0:8], sm["Gs"][:, 8:16], sm["Gs"][:, 0:8], SUB)
                P.act(sm["eGe"][:, 0:8], sm["dG"][:, 0:8], AF.Exp)
                beta_b = V(sm["beta"], bc3(sm["beta"].t[:, 0:8], 8, 64, 2))
                eG_b = V(sm["eGG"], bc3(sm["eGG"].t[:, 0:8], 8, 64, 2))
                eGe_b = V(sm["eGe"], bc3(sm["eGe"].t[:, 0:8], 8, 64, 2))
                P.tt(V(Kb, v3(Kb, 0, 8)), V(Kr, v3(Kr, 0, 8)), beta_b, MUL)
                P.tt(V(Qg, v3(Qg, 0, 8)), V(Qr, v3(Qr, 0, 8)), eG_b, MUL)
                P.tt(V(RHSk, v3(RHSk, 0, 8)), V(Kb, v3(Kb, 0, 8)), eG_b, MUL)
                P.tt(V(RHSvb, v3(RHSvb, 0, 8)), V(Vr, v3(Vr, 0, 8)), beta_b, MUL)
                P.tt(V(Kdec, v3(Kdec, 0, 8)), V(Kr, v3(Kr, 0, 8)), eGe_b, MUL)
                def pre_g(g):
                    T_ = lambda nm: G[(nm, g)]
                    hc = lambda i: slice((g * 4 + i) * 64, (g * 4 + i + 1) * 64)
                    lc = lambda i: slice(i * 64, (i + 1) * 64)
                    for src, dn in ((Kr, "kT"), (Kb, "kbT"), (Qr, "qT"), (Qg, "qgT")):
                        yield
                        pt = nxt("ps", psb)
                        for i in range(4):
                            P.mm(pt[0:64, lc(i)], src[:, hc(i)], I64, flag=(i == 3))
                        evac(T_(dn)[:, :], pt[0:64, 0:256])
                    P.tt(V(T_("rhsE"), v3(T_("rhsE"), 0, 4)), V(maskU, bc3(maskU.t[:, :], 4, 64, 1)),
                         V(sm["g8"], bc3(sm["g8"].t[:, g * 4:g * 4 + 4], 4, 64, 2)), MUL)
                    yield
                    pe_ = nxt("ps", psb)
                    P.mm(pe_[0:64, 0:256], maskL[:, :], T_("rhsE")[:, :])
                    P.act(T_("E")[:, :], pe_[0:64, 0:256], AF.Exp)
                    P.tt(V(T_("EmS"), v3(T_("EmS"), 0, 4)), V(T_("E"), v3(T_("E"), 0, 4)), V(maskS, bc3(maskS.t[:, :], 4, 64, 1)), MUL)
                    P.tt(V(T_("EmI"), v3(T_("EmI"), 0, 4)), V(T_("E"), v3(T_("E"), 0, 4)), V(maskU, bc3(maskU.t[:, :], 4, 64, 1)), MUL)
                    yield
                    pA = nxt("ps", psb)
                    yield
                    pQ = nxt("ps", psb)
                    for i in range(4):
                        P.mm(pA[0:64, lc(i)], T_("kT")[:, lc(i)], T_("kbT")[:, lc(i)], flag=(i == 3))
                    for i in range(4):
                        P.mm(pQ[0:64, lc(i)], T_("kT")[:, lc(i)], T_("qT")[:, lc(i)], flag=(i == 3))
                    P.tt(T_("Xa")[:, :], pA[0:64, 0:256], T_("EmS")[:, :], MUL)
                    P.tt(T_("QKD")[:, :], pQ[0:64, 0:256], T_("EmI")[:, :], MUL)
                    yield
                    pX = nxt("ps", psb)
                    for i in range(4):
                        P.mm(pX[0:64, lc(i)], T_("Xa")[:, lc(i)], identb[:, :], flag=(i == 3))
                    evac(T_("XTa")[:, :], pX[0:64, 0:256])
                    P.tt(V(T_("Pa"), v3(T_("Pa"), 0, 4)), V(ident, bc3(ident.t[0:64, 0:64], 4, 64, 1)),
                         V(T_("Xa"), v3(T_("Xa"), 0, 4)), SUB)
                    Xc, XTc, Pc = "Xa", "XTa", "Pa"
                    for lvl in range(1, 6):
                        Xn = "Xb" if Xc == "Xa" else "Xa"
                        XTn = "XTb" if XTc == "XTa" else "XTa"
                        Pn = "Pb" if Pc == "Pa" else "Pa"
                        if lvl < 5:
                            yield
                            p1 = nxt("ps", psb)
                            for i in range(4):
                                P.mm(p1[0:64, lc(i)], T_(XTc)[:, lc(i)], T_(Xc)[:, lc(i)], flag=(i == 3))
                        yield
                        p2 = nxt("ps", psb)
                        for i in range(4):
                            P.mm(p2[0:64, lc(i)], T_(Xc)[:, lc(i)], T_(XTc)[:, lc(i)], flag=(i == 3))
                        if lvl < 5:
                            evac(T_(Xn)[:, :], p1[0:64, 0:256])
                        evac(T_(XTn)[:, :], p2[0:64, 0:256])
                        yield
                        p3 = nxt("ps", psb)
                        for i in range(4):
                            P.mm(p3[0:64, lc(i)], T_(XTn)[:, lc(i)], T_(Pc)[:, lc(i)], flag=(i == 3))
                        P.tt(T_(Pn)[:, :], T_(Pc)[:, :], p3[0:64, 0:256], ADD)
                        Xc, XTc, Pc = Xn, XTn, Pn
                    TTn = Pc
                    yield
                    pSv = nxt("ps", psb)
                    for i in range(4):
                        P.mm(pSv[0:64, lc(i)], T_(TTn)[:, lc(i)], RHSvb[:, hc(i)], flag=(i == 3))
                    evac(T_("solv")[:, :], pSv[0:64, 0:256])
                    yield
                    pSk = nxt("ps", psb)
                    for i in range(4):
                        P.mm(pSk[0:64, lc(i)], RHSk[:, hc(i)], T_(TTn)[:, lc(i)], flag=(i == 3))
                    P.ts(T_("nsolkT")[:, :], pSk[0:64, 0:256], -1.0, None, MUL)
                drive([pre_g(0), pre_g(1)])
                def rec_g(g):
                    T_ = lambda nm: G[(nm, g)]
                    hc = lambda i: slice((g * 4 + i) * 64, (g * 4 + i + 1) * 64)
                    lc = lambda i: slice(i * 64, (i + 1) * 64)
                    gs = slice(g * 256, (g + 1) * 256)
                    yield
                    pU = nxt("ps", psb)
                    for i in range(4):
                        P.mm(pU[0:64, lc(i)], T_("nsolkT")[:, lc(i)], S_sb[:, hc(i)], flag=(i == 3))
                    P.tt(T_("U")[:, :], T_("solv")[:, :], pU[0:64, 0:256], ADD)
                    yield
                    pO = nxt("ps", psb)
                    for i in range(4):
                        P.mm(pO[0:64, lc(i)], T_("qgT")[:, lc(i)], S_sb[:, hc(i)], start=True, stop=False, flag=False)
                        P.mm(pO[0:64, lc(i)], T_("QKD")[:, lc(i)], T_("U")[:, lc(i)], start=False, stop=True, flag=(i == 3))
                    evac(O[:, gs], pO[0:64, 0:256])
                    P.tt(V(T_("Stmp"), v3(T_("Stmp"), 0, 4)), V(S_sb, v3(S_sb, g * 256, 4)),
                         V(sm["eGG"], bc3(sm["eGG"].t[:, 8 + g * 4:8 + g * 4 + 4], 4, 64, 2)), MUL)
                    yield
                    pS = nxt("ps", psb)
                    for i in range(4):
                        P.mm(pS[0:64, lc(i)], Kdec[:, hc(i)], T_("U")[:, lc(i)], flag=(i == 3))
                    P.tt(S_sb[:, gs], T_("Stmp")[:, :], pS[0:64, 0:256], ADD)
                drive([rec_g(0), rec_g(1)])
                P.tt(sqt[:, :], O[:, :], O[:, :], MUL)
                P.reduce(sm["ss8"][:, 0:8], V(sqt, v3(sqt, 0, 8)))
                P.act(sm["ss8"][:, 0:8], sm["ss8"][:, 0:8], AF.Sqrt, bias=eps64, scale=1.0 / 64)
                P.recip(sm["ss8"][:, 0:8], sm["ss8"][:, 0:8])
                P.tt(V(O, v3(O, 0, 8)), V(O, v3(O, 0, 8)), V(sm["ss8"], bc3(sm["ss8"].t[:, 0:8], 8, 64, 2)), MUL)
                P.tt(V(O, v3(O, 0, 8)), V(O, v3(O, 0, 8)), V(dng, bc3(dng.t[:, :], 8, 64, 1)), MUL)
                P.tt(O[:, :], O[:, :], zs[:, :], MUL)
                pT = nxt("ps", psb)
                for c in range(4):
                    P.mm(pT[:, c * 64:(c + 1) * 64], O[:, c * 128:(c + 1) * 128], I64, flag=(c == 3))
                evac(V(om, om.t[:, 0:4, cc:cc + 64]), V(pT, pT.t[:, 0:256].rearrange("p (a b) -> p a b", a=4)))

            for si, (c0, n, s) in enumerate(segs):
                if kind == "s":
                    P.dma("sp", V(S_sb, v3(S_sb, 0, 8)), V(s_delta, s_delta.t[s - 1].rearrange("h k v -> k h v")))
                for k in range(n // 64):
                    chunk(c0 + 64 * k, offs3[si] + 64 * k)
                if kind == "s":
                    P.dma("sp", V(o_s_delta, o_s_delta.t[s - 1].rearrange("h k v -> k h v")), V(S_sb, v3(S_sb, 0, 8)))
                    P.dma("sp", V(o_s_qkv, o_s_qkv.t[:, :].rearrange("p (a q k) -> p a q k", a=12, q=NS)[:, :, s - 1, :]),
                          V(qkv_pre, qkv_pre.t[:, :, offs3[si] + n:offs3[si] + n + 3]))
            if kind == "p":
                P.copy(V(hal3, hal3.t[:, :, :]), V(qkv_pre, qkv_pre.t[:, :, T:T + 3]))
                if last:
                    P.dma("sp", V(o_p_qkv, o_p_qkv.t[:, :].rearrange("p (a k) -> p a k", a=12)), V(hal3, hal3.t[:, :, :]))
                    P.dma("sp", V(o_p_delta, o_p_delta.t[:, :, :].rearrange("h k v -> k h v")), V(S_sb, v3(S_sb, 0, 8)))
            proj_fm(ab_w_out, ab_w_out.t, 0, D, FC, om, T, resid_add(0, 1, segs))

        def sb_mixer(kind, ti, T, segs):
            Wq = sb_w_qkv.t
            ar.reset()
            P.barrier()
            DB = cfg.sbdbg
            qT = ar.alloc("qT", [128, FC, TT], BF16)
            kbf = ar.alloc("kbf", [128, FC, TT], BF16)
            vbf = ar.alloc("vbf", [128, 4, D], BF16)
            mark = ar.off
            kf = ar.alloc("kf", [128, FC, TT])
            vf = ar.alloc("vf", [128, 4, D])
            vblk = 128 if kind == "p" else 64
            nvb = T // vblk

            def c_q(j, pt):
                P.act(qT[:, j, 0:T], pt[:, 0:T], AF.Copy, scale=0.125)

            def c_k(j, pt):
                P.act(kf[:, j, 0:T], pt[:, 0:T], AF.Copy)
                P.copy(kbf[:, j, 0:T], pt[:, 0:T])
            if DB & 1:
                proj_fm(sb_w_qkv, Wq, 0, D, FC, h, T, c_q)
            if DB & 2:
                proj_fm(sb_w_qkv, Wq, D, D, FC, h, T, c_k)
            for half in range(2 if DB & 4 else 0):
                wb = nxt("wp", wp)
                wv = wb.t[:, 0:FC * 512].rearrange("p (kc n) -> p kc n", kc=FC)
                load_w(wb, wv[:, :, :], sb_w_qkv, Wq[:, 2 * D + half * 512:2 * D + (half + 1) * 512].rearrange("(kc p) n -> p kc n", p=128))
                for b in range(nvb):
                    pt = nxt("ps", psb)
                    for kc in range(FC):
                        P.mm(pt[0:vblk, :], h[:, kc, b * vblk:(b + 1) * vblk], V(wb, wv[:, kc, :]),
                             start=(kc == 0), stop=(kc == FC - 1), flag=(kc == FC - 1))
                    P.act(vf[0:vblk, b, half * 512:(half + 1) * 512], pt[0:vblk, :], AF.Copy)
                    P.copy(vbf[0:vblk, b, half * 512:(half + 1) * 512], pt[0:vblk, :])
            t0 = ti * TT
            if not (DB & 8):
                pass
            elif kind == "p":
                P.dma("sp", V(o_p_k, o_p_k.t[:, t0:t0 + T].rearrange("(fc p) t -> p fc t", p=128)), kf[:, :, 0:T])
                P.dma("sp", V(o_p_v, o_p_v.t[t0:t0 + T, :].rearrange("(b p) c -> p b c", p=128)), vf[:, 0:nvb, :])
                if DB & 16:
                    P.dma("sp", V(kT_scr, kT_scr.t[:, t0:t0 + T].rearrange("(fc p) t -> p fc t", p=128)), kbf[:, :, 0:T])
                    P.dma("sp", V(v_scr, v_scr.t[t0:t0 + T, :].rearrange("(b p) c -> p b c", p=128)), vbf[:, 0:nvb, :])
            else:
                P.dma("sp", V(o_s_k, o_s_k.t[:, 0:T].rearrange("(fc p) t -> p fc t", p=128)), kf[:, :, 0:T])
                P.dma("sp", V(o_s_v, o_s_v.t[0:T, :].rearrange("(b p) c -> p b c", p=64)), vf[0:64, 0:nvb, :])

            if not (DB & 96):
                return
            P.barrier()
            ar.reset(mark)
            NPB = 3
            E_ = [ar.alloc("E%d" % i, [128, 512]) for i in range(NPB)]
            SP_ = [ar.alloc("SP%d" % i, [128, 512], BF16) for i in range(NPB)]
            ARG_ = [ar.alloc("ARG%d" % i, [128, 512]) for i in range(NPB)]
            W_ = [ar.alloc("W%d" % i, [128, 512], BF16) for i in range(NPB)]
            R_ = [ar.alloc("R%d" % i, [128, 512]) for i in range(2)]
            kTb = [ar.alloc("kTb%d" % i, [128, 4096], BF16) for i in range(2)]
            vb = [ar.alloc("vb%d" % i, [128, 4, 512], BF16) for i in range(2)]
            if kind == "p":
                qz = [ar.alloc("qz%d" % i, [128, FC, TT], BF16) for i in range(2)]
                for i in range(2):
                    P.memset(V(qz[i], qz[i].t[:, :, :]), 0.0)
                    P.copy(V(qz[i], qz[i].t[i * 64:(i + 1) * 64, :, 0:T]), V(qT, qT.t[i * 64:(i + 1) * 64, :, 0:T]))
            if kind == "s":
                qS = ar.alloc("qS", [64, 16, NS * DS], BF16)
                kS = ar.alloc("kS", [64, 16, NS * DS], BF16)
                for dstb, srcb in ((qS, qT), (kS, kbf)):
                    dv = dstb.t[:, :, :].rearrange("p (c two) t -> p c two t", two=2)
                    P.dma("sp", V(dstb, dv[:, :, 0, :]), V(srcb, srcb.t[0:64, :, 0:T]))
                    P.dma("sp", V(dstb, dv[:, :, 1, :]), V(srcb, srcb.t[64:128, :, 0:T]))
            psr = psb[0:6]
            one_col = lambda kk: V(onesf, onesf.t[0:kk, 0:1])

            def run_stream(blocks):
                nb = len(blocks)
                st_ = {}

                def stA(b):
                    B = blocks[b]
                    kk, GW = B["kk"], B["GW"]
                    pz = nxt("psr", psr)
                    st_[b] = pz
                    hl = B["heads"]
                    for i, (c0, N, kTv, qv, vv, pov, ost) in enumerate(hl):
                        P.mm(pz[0:kk, c0:c0 + N], kTv, qv, start=(i == 0), stop=False, flag=(i == len(hl) - 1), skip=True)
                    E, SP = E_[b % NPB], SP_[b % NPB]
                    P.act(E[0:kk, 0:GW], pz[0:kk, 0:GW], AF.Exp)
                    P.act(SP[0:kk, 0:GW], E[0:kk, 0:GW], AF.Ln, bias=one_col(kk), scale=1.0)
                    if B["mask"] is not None:
                        P.tt(SP[0:kk, 0:GW], SP[0:kk, 0:GW], B["mask"], ALU.mult)

                def stB(b):
                    B = blocks[b]
                    kk, GW = B["kk"], B["GW"]
                    pz = st_[b]
                    SP, ARG, W, R = SP_[b % NPB], ARG_[b % NPB], W_[b % NPB], B["R"]
                    P.mm(pz[0:kk, 0:GW], negincl[0:kk, 0:kk], SP[0:kk, 0:GW], start=False, stop=True, skip=True)
                    pt = None
                    if not B["last"]:
                        pt = nxt("psr", psr)
                        P.mm(pt[:, 0:GW], ones1b[0:kk, :], SP[0:kk, 0:GW])
                    if B["first"]:
                        P.act(W[0:kk, 0:GW], pz[0:kk, 0:GW], AF.Exp)
                    else:
                        P.tt(ARG[0:kk, 0:GW], pz[0:kk, 0:GW], R[0:kk, 0:GW], ALU.subtract)
                        P.act(W[0:kk, 0:GW], ARG[0:kk, 0:GW], AF.Exp)
                    if B["mask"] is not None:
                        P.tt(W[0:kk, 0:GW], W[0:kk, 0:GW], B["mask"], ALU.mult)
                    if pt is not None:
                        if B["first"]:
                            P.copy(R[:, 0:GW], pt[:, 0:GW])
                        else:
                            P.tt(R[:, 0:GW], R[:, 0:GW], pt[:, 0:GW], ALU.add)

                def stC(b):
                    B = blocks[b]
                    kk = B["kk"]
                    W = W_[b % NPB]
                    hl = B["heads"]
                    for i, (c0, N, kTv, qv, vv, pov, ost) in enumerate(hl):
                        P.mm(pov, vv, W[0:kk, c0:c0 + N], start=(B["first"] and ost), stop=B["last"], flag=(i == len(hl) - 1), skip=True)

                for step in range(nb + 4):
                    if step < nb:
                        stA(step)
                    if 0 <= step - 2 < nb:
                        stB(step - 2)
                    if 0 <= step - 4 < nb:
                        stC(step - 4)
                        if "post" in blocks[step - 4]:
                            blocks[step - 4]["post"]()

            if kind == "p" and (DB & 32):
                i = ti
                for c in range(8):
                    po = psb[6 + c % 2]
                    blocks = []
                    last_of = []
                    for j, kb in enumerate(range(i, -1, -1)):
                        kt = kTb[j % 2]
                        vt = vb[j % 2]

                        def load(kt=kt, vt=vt, kb=kb):
                            P.dma("sp", V(kt, kt.t[:, 0:512]), V(kT_scr, kT_scr.t[c * 128:(c + 1) * 128, kb * 512:(kb + 1) * 512]))
                            P.dma("sp", V(vt, vt.t[:, :, 0:128]),
                                  V(v_scr, v_scr.t[kb * 512:(kb + 1) * 512, c * 128:(c + 1) * 128].rearrange("(j p) c -> p j c", p=128)))
                        if j < 2:
                            load()
                        else:
                            last_of[j - 2]["post"] = load
                        for d in range(3, -1, -1):
                            for hh in range(2):
                                pb = hh * 64
                                blocks.append(dict(
                                    kk=128, GW=512,
                                    heads=[(0, 512, V(kt, kt.t[:, d * 128:(d + 1) * 128]),
                                            V(qz[hh], qz[hh].t[:, c, 0:512]),
                                            V(vt, vt.t[:, d, 0:128]),
                                            V(psb[6 + hh], psb[6 + hh].t[:, 0:512]), True)],
                                    mask=(mdiag[d][:, :] if kb == i else None),
                                    first=(kb == i and d == 3), last=(kb == 0 and d == 0), R=R_[hh]))
                        last_of.append(blocks[-1])
                    run_stream(blocks)
                    P.act(V(om, om.t[0:64, c, 0:T]), V(psb[6], psb[6].t[0:64, 0:T]), AF.Copy)
                    P.copy(V(om, om.t[64:128, c, 0:T]), V(psb[7], psb[7].t[64:128, 0:T]))
            if kind == "s" and (DB & 64):
                NKB = PAST // 512
                for si, (c0, n, s) in enumerate(segs):
                    q = s - 1
                    for g in range(2):
                        po = psb[6 + (si * 2 + g) % 2]
                        blocks = []

                        def heads_for(kk, kT_of, v_of):
                            hl = []
                            for gi in range(8):
                                hd = g * 8 + gi
                                pb = (hd % 2) * 64
                                hl.append((gi * 64, 64, kT_of(hd, pb), V(qS, qS.t[0:64, hd, c0:c0 + 64]),
                                           v_of(hd), V(po, po.t[pb:pb + 64, (gi // 2) * 64:(gi // 2 + 1) * 64]), gi < 2))
                            return hl
                        blocks.append(dict(
                            kk=64, GW=512,
                            heads=heads_for(64, lambda hd, pb: V(kS, kS.t[0:64, hd, c0:c0 + 64]),
                                            lambda hd: V(vbf, vbf.t[0:64, si, hd * 64:(hd + 1) * 64])),
                            mask=mnew[:, :], first=True, last=(NKB == 0), R=R_[0]))
                        last_of = []
                        for j, kb in enumerate(range(NKB - 1, -1, -1)):
                            kt = kTb[j % 2]
                            vt = vb[j % 2]
                            ktv = kt.t[0:64, :].rearrange("p (hh k) -> p hh k", hh=8)

                            def load(kt=kt, vt=vt, ktv=ktv, kb=kb):
                                P.dma("pool", V(kt, ktv),
                                      V(ckT_d, ckT_d.t[q, g * 512:(g + 1) * 512, kb * 512:(kb + 1) * 512].rearrange("(hh dd) k -> dd hh k", dd=64)))
                                P.dma("pool", V(vt, vt.t[:, :, :]),
                                      V(cv_d, cv_d.t[q, kb * 512:(kb + 1) * 512, g * 512:(g + 1) * 512].rearrange("(j p) c -> p j c", p=128)))
                            if j < 2:
                                load()
                            else:
                                last_of[j - 2]["post"] = load
                            for d in range(3, -1, -1):
                                blocks.append(dict(
                                    kk=128, GW=512,
                                    heads=heads_for(128, lambda hd, pb, kt=kt, ktv=ktv, d=d: V(kt, ktv[:, hd - 8 * g, d * 128:(d + 1) * 128]),
                                                    lambda hd, vt=vt, d=d: V(vt, vt.t[:, d, (hd - 8 * g) * 64:(hd - 8 * g + 1) * 64])),
                                    mask=None, first=False, last=(kb == 0 and d == 0), R=R_[0]))
                            last_of.append(blocks[-1])
                        run_stream(blocks)
                        evac(V(om, om.t[:, 4 * g:4 * g + 4, c0:c0 + 64]), V(po, po.t[:, 0:256].rearrange("p (a b) -> p a b", a=4)))
            proj_fm(sb_w_out, sb_w_out.t, 0, D, FC, om, T, resid_add(1, 1, segs))

        tiles = []
        for i in range(cfg.ntile):
            tiles.append(("p", i, TT, [(0, TT, 0)], i == cfg.ntile - 1))
        tiles.append(("s", 0, NS * DS, [(j * DS, DS, 1 + j) for j in range(NS)], True))

        for (kind, ti, T, segs, last) in tiles:
            src = xp if kind == "p" else xs
            dst = yp if kind == "p" else ys
            t0 = ti * TT
            P.barrier()
            P.dma("sp", x[:, :, 0:T], V(src, src.t[:, t0:t0 + T].rearrange("(fc p) t -> p fc t", p=128)))
            for l in range(2):
                mod_norm(T, segs, l, 0)
                ffn(T, segs, l, 0, 0)
                if cfg.stage <= 1:
                    break
                mod_norm(T, segs, l, 1)
                if l == 0:
                    ab_mixer(kind, ti, T, segs, last)
                    if cfg.stage <= 2:
                        break
                else:
                    sb_mixer(kind, ti, T, segs)
                    if cfg.stage <= 3:
                        break
                mod_norm(T, segs, l, 2)
                ffn(T, segs, l, 2, 1)
            if cfg.stage >= 99:
                ar.reset()
                yo = Buf(ar.t[:, AR4 - FC * TT:AR4].rearrange("p (a b) -> p a b", a=FC), "yo")
                mod_norm(T, segs, None, None, out_f32=yo)
                P.dma("sp", V(dst, dst.t[:, t0:t0 + T].rearrange("(fc p) t -> p fc t", p=128)), yo[:, :, 0:T])
            else:
                P.dma("sp", V(dst, dst.t[:, t0:t0 + T].rearrange("(fc p) t -> p fc t", p=128)), x[:, :, 0:T])

        P.finish()
        P.run()
        print("instructions:", P.nins, {e: len(P.ops[e]) for e in P.ENG})
    return nc


def host_ab(inp, ssl):
    f = np.ascontiguousarray
    cw = inp["ab_conv_qkv"][0]
    scw = inp["sc_conv"][0]
    sq = inp["state_qkv_conv"][0, ssl]
    ss = inp["state_sconv"][0, ssl]
    return {
        "ab_w_in": inp["ab_w_in"][0], "ab_w_out": inp["ab_w_out"][0],
        "convw": f(cw.reshape(4, 12, 128).transpose(2, 1, 0).reshape(128, 48)),
        "scw": f(scw.reshape(3, 4, 128).transpose(2, 1, 0).reshape(128, 12)),
        "alog": f(np.broadcast_to(inp["dn_A_log"][0][None, :], (64, 8))),
        "dtb": f(np.broadcast_to(inp["dn_dt_bias"][0][None, :], (64, 8))),
        "dng": f(np.broadcast_to(inp["dn_norm_g"][0][None, :], (64, 64))),
        "s_delta": f(inp["state_delta"][0, ssl]),
        "s_qkv": f(sq.reshape(NS, 3, 12, 128).transpose(3, 2, 0, 1).reshape(128, 12 * NS * 3)),
        "s_sconv": f(ss.reshape(NS, 2, 4, 128).transpose(3, 2, 0, 1).reshape(128, 4 * NS * 2)),
    }


def host_sb(inp, ssl, past):
    f = np.ascontiguousarray
    ck = inp["cache_k"][0, ssl, :, :past]
    cv = inp["cache_v"][0, ssl, :, :past]
    return {
        "sb_w_qkv": inp["sb_w_qkv"][0], "sb_w_out": inp["sb_w_out"][0],
        "ckT": f(ck.transpose(0, 1, 3, 2).reshape(ck.shape[0], 1024, past)),
        "cv": f(cv.transpose(0, 2, 1, 3).reshape(cv.shape[0], past, 1024)),
    }


def host_common(inp, b, ssl):
    f = np.ascontiguousarray
    c_all = np.concatenate([inp["c_prompt"][b][None], inp["c_sample"][ssl]], 0)
    cT = c_all.T.reshape(8, 128, NSEQ).transpose(1, 0, 2).reshape(128, 8 * NSEQ)
    normg = inp["norm_g"].reshape(6, 8, 128).transpose(2, 0, 1).reshape(128, 48)
    finalg = inp["final_g"].reshape(8, 128).T
    return {
        "xp": f(inp["x_prompt"][b].T), "xs": f(inp["x_sample"][ssl].reshape(NS * DS, D).T),
        "cT": f(cT), "normg": f(normg), "finalg": f(finalg),
        "ada_w": inp["ada_w"], "ada_b": inp["ada_b"], "ff_w_in": inp["ff_w_in"], "ff_w_out": inp["ff_w_out"],
    }


_NC_CACHE = {}


def kernel(**inputs):
    inp = {k: np.asarray(v, dtype=np.float32) for k, v in inputs.items()}
    SEQ = inp["x_prompt"].shape[1]
    PAST = inp["cache_k"].shape[3]
    key = (SEQ, PAST)
    if key not in _NC_CACHE:
        _NC_CACHE[key] = build(Cfg(seq=SEQ, past=PAST, stage=99))
    nc = _NC_CACHE[key]
    n_cores = 8
    in_maps = []
    for c in range(n_cores):
        b = c // 2
        ssl = slice(NS * c, NS * (c + 1))
        m = host_common(inp, b, ssl)
        m.update(host_ab(inp, ssl))
        m.update(host_sb(inp, ssl, PAST))
        in_maps.append(m)
    res = run_bass_kernel_spmd(nc, in_maps, core_ids=list(range(n_cores)))
    R = res.results
    NB = inp["x_prompt"].shape[0]
    f32 = np.float32
    y_prompt = np.stack([R[2 * b]["yp"].T for b in range(NB)]).astype(f32)
    y_sample = np.concatenate([R[c]["ys"].T.reshape(NS, DS, D) for c in range(n_cores)]).astype(f32)
    p_delta = np.stack([R[2 * b]["o_p_delta"] for b in range(NB)])[None].astype(f32)
    p_qkv = np.stack([R[2 * b]["o_p_qkv"].reshape(128, 12, 3).transpose(2, 1, 0).reshape(3, 1536) for b in range(NB)])[None].astype(f32)
    p_sc = np.stack([R[2 * b]["o_p_sconv"].reshape(128, 4, 2).transpose(2, 1, 0).reshape(2, 512) for b in range(NB)])[None].astype(f32)
    p_k = np.stack([R[2 * b]["o_p_k"].reshape(16, 64, SEQ).transpose(0, 2, 1) for b in range(NB)])[None].astype(f32)
    p_v = np.stack([R[2 * b]["o_p_v"].reshape(SEQ, 16, 64).transpose(1, 0, 2) for b in range(NB)])[None].astype(f32)
    s_delta = np.concatenate([R[c]["o_s_delta"] for c in range(n_cores)])[None].astype(f32)
    s_qkv = np.concatenate([R[c]["o_s_qkv"].reshape(128, 12, NS, 3).transpose(2, 3, 1, 0).reshape(NS, 3, 1536) for c in range(n_cores)])[None].astype(f32)
    s_sc = np.concatenate([R[c]["o_s_sconv"].reshape(128, 4, NS, 2).transpose(2, 3, 1, 0).reshape(NS, 2, 512) for c in range(n_cores)])[None].astype(f32)
    s_k = np.concatenate([R[c]["o_s_k"].reshape(16, 64, NS, DS).transpose(2, 0, 3, 1) for c in range(n_cores)])[None].astype(f32)
    s_v = np.concatenate([R[c]["o_s_v"].reshape(NS, DS, 16, 64).transpose(0, 2, 1, 3) for c in range(n_cores)])[None].astype(f32)
    asc = np.ascontiguousarray
    return tuple(asc(a) for a in (y_prompt, y_sample, p_delta, p_qkv, p_sc, p_k, p_v, s_delta, s_qkv, s_sc, s_k, s_v))
```
